# Optimizing a Trainium2 kernel written in Bass

```python
import math
import jax, jax.numpy as jnp
from jax import lax
import numpy as np

D_MODEL = 1024
BATCH = 8
SEQ = 2048
DEPTH = 4

N_META = 16
CHUNK = 128
NORM_EPS = 1e-6

SSD_HEADS = 4
SSD_HEAD_DIM = 64
SSD_WIDTH = SSD_HEADS * SSD_HEAD_DIM
SSD_GROUPS = 2
SSD_STATE = 128
SSD_CONV = 4
SSD_CONV_CH = SSD_WIDTH + 2 * SSD_GROUPS * SSD_STATE
SSD_IN = SSD_WIDTH + SSD_CONV_CH + SSD_HEADS

RWKV_HEADS = 4
RWKV_HEAD_DIM = 64
RWKV_WIDTH = RWKV_HEADS * RWKV_HEAD_DIM
RWKV_DECAY_RANK = 64
RWKV_A_RANK = 64
RWKV_GATE_RANK = 128
RWKV_IN = 3 * RWKV_WIDTH + RWKV_DECAY_RANK + RWKV_A_RANK + RWKV_GATE_RANK
RWKV_GN_EPS = 64e-5

LRU_BLOCKS = 4
LRU_BLOCK_DIM = 64
LRU_WIDTH = LRU_BLOCKS * LRU_BLOCK_DIM
LRU_CONV = 4
LRU_C = 8.0
LRU_IN = 2 * LRU_WIDTH

RET_HEADS = 4
RET_QK_DIM = 32
RET_V_DIM = 64
RET_WIDTH = RET_HEADS * RET_V_DIM
RET_IN = 2 * RET_HEADS * RET_QK_DIM + 2 * RET_WIDTH
RET_GN_EPS = 1e-5
ROPE_BASE = 10000.0

MIX_IN = SSD_IN + RWKV_IN + LRU_IN + RET_IN
MIX_WIDTH = SSD_WIDTH + RWKV_WIDTH + LRU_WIDTH + RET_WIDTH
D_FF = -(-8 * D_MODEL // (3 * 256)) * 256

kernel_name = "hybrid_ssd_rwkv7_rglru_retention_trunk"


def split_last(x, sizes):
    idx = [int(s) for s in np.cumsum(sizes)[:-1]]
    return jnp.split(x, idx, axis=-1)


def rms_norm(x, w):
    x32 = x.astype(jnp.float32)
    y = x32 * lax.rsqrt(jnp.mean(x32 * x32, axis=-1, keepdims=True) + NORM_EPS)
    return (y * w.astype(jnp.float32)).astype(x.dtype)


def causal_depthwise_conv(x, w, b):
    K, C = w.shape
    y = lax.conv_general_dilated(x, w[:, None, :].astype(x.dtype), window_strides=(1,),
                                 padding=[(K - 1, 0)], dimension_numbers=('NWC', 'WIO', 'NWC'),
                                 feature_group_count=C)
    return y + b.astype(x.dtype)


def pad_front(x, n):
    return jnp.pad(x, [(0, 0), (n, 0)] + [(0, 0)] * (x.ndim - 2))


def segsum(a):
    L = a.shape[-1]
    cs = jnp.cumsum(a, axis=-1)
    diff = cs[..., :, None] - cs[..., None, :]
    mask = jnp.tril(jnp.ones((L, L), dtype=bool))
    return jnp.where(mask, diff, -jnp.inf)


def head_group_norm(y, eps):
    mu = jnp.mean(y, axis=-1, keepdims=True)
    var = jnp.mean(jnp.square(y - mu), axis=-1, keepdims=True)
    return (y - mu) * lax.rsqrt(var + eps)


def rope(x, pos):
    half = x.shape[-1] // 2
    freqs = ROPE_BASE ** (-jnp.arange(half, dtype=jnp.float32) / half)
    ang = pos.astype(jnp.float32)[:, None] * freqs[None, :]
    cos = jnp.cos(ang)[None, :, None, :]
    sin = jnp.sin(ang)[None, :, None, :]
    x1, x2 = x[..., :half], x[..., half:]
    return jnp.concatenate([x1 * cos - x2 * sin, x1 * sin + x2 * cos], axis=-1)


def ssd_chunked(x, a, b, c):
    Bsz, T, H, P = x.shape
    G = b.shape[2]
    nc = T // CHUNK
    b = jnp.repeat(b, H // G, axis=2).reshape(Bsz, nc, CHUNK, H, -1)
    c = jnp.repeat(c, H // G, axis=2).reshape(Bsz, nc, CHUNK, H, -1)
    x = x.reshape(Bsz, nc, CHUNK, H, P)
    a = a.reshape(Bsz, nc, CHUNK, H).transpose(0, 3, 1, 2)
    a_cs = jnp.cumsum(a, axis=-1)
    Lmat = jnp.exp(segsum(a))
    y_diag = jnp.einsum('bclhn,bcshn,bhcls,bcshp->bclhp', c, b, Lmat, x)
    decay_states = jnp.exp(a_cs[..., -1:] - a_cs)
    states = jnp.einsum('bclhn,bhcl,bclhp->bchpn', b, decay_states, x)
    states = jnp.pad(states, [(0, 0), (1, 0), (0, 0), (0, 0), (0, 0)])
    chunk_a = jnp.pad(a_cs[..., -1], [(0, 0), (0, 0), (1, 0)])
    decay_chunk = jnp.exp(segsum(chunk_a))
    entering = jnp.einsum('bhzc,bchpn->bzhpn', decay_chunk, states)[:, :-1]
    y_off = jnp.einsum('bclhn,bchpn,bhcl->bclhp', c, entering, jnp.exp(a_cs))
    return (y_diag + y_off).reshape(Bsz, T, H, P)


def ssd_mix(p, conv_w, conv_b, dt_bias, a_log, d_skip, norm_w):
    f32 = jnp.float32
    Bsz, T, _ = p.shape
    z, xbc, dt_raw = split_last(p, [SSD_WIDTH, SSD_CONV_CH, SSD_HEADS])
    xbc = jax.nn.silu(causal_depthwise_conv(xbc, conv_w, conv_b)).astype(f32)
    xs, bs, cs = split_last(xbc, [SSD_WIDTH, SSD_GROUPS * SSD_STATE, SSD_GROUPS * SSD_STATE])
    xs = xs.reshape(Bsz, T, SSD_HEADS, SSD_HEAD_DIM)
    bs = bs.reshape(Bsz, T, SSD_GROUPS, SSD_STATE)
    cs = cs.reshape(Bsz, T, SSD_GROUPS, SSD_STATE)
    dt = jax.nn.softplus(dt_raw.astype(f32) + dt_bias.astype(f32))
    A = -jnp.exp(a_log.astype(f32))
    pad = CHUNK - N_META
    y = ssd_chunked(pad_front(xs * dt[..., None], pad), pad_front(dt * A, pad),
                    pad_front(bs, pad), pad_front(cs, pad))[:, pad:]
    y = y + d_skip.astype(f32)[:, None] * xs
    y = y.reshape(Bsz, T, SSD_WIDTH) * jax.nn.silu(z.astype(f32))
    yg = y.reshape(Bsz, T, SSD_GROUPS, -1)
    yg = yg * lax.rsqrt(jnp.mean(yg * yg, axis=-1, keepdims=True) + NORM_EPS)
    return yg.reshape(Bsz, T, SSD_WIDTH) * norm_w.astype(f32)


def rwkv7_scan(r, w, k, v, kk, a):
    Bsz, T, H, N = r.shape

    def step(S, inp):
        r_t, w_t, k_t, v_t, kk_t, a_t = inp
        sa = jnp.einsum('bhvk,bhk->bhv', S, -kk_t)
        S = (S * w_t[:, :, None, :] + sa[..., None] * (kk_t * a_t)[:, :, None, :]
             + v_t[..., None] * k_t[:, :, None, :])
        return S, jnp.einsum('bhvk,bhk->bhv', S, r_t)

    xs = tuple(jnp.moveaxis(t, 1, 0) for t in (r, w, k, v, kk, a))
    _, y = lax.scan(step, jnp.zeros((Bsz, H, N, N), jnp.float32), xs)
    return jnp.moveaxis(y, 0, 1)


def rwkv7_mix(p, mu, w0, w2, a0, a2, g2, k_k, k_a, r_k, ln_w, ln_b):
    f32 = jnp.float32
    p = p.astype(f32)
    Bsz, T, _ = p.shape
    p_prev = jnp.pad(p, [(0, 0), (1, 0), (0, 0)])[:, :-1]
    p = p + (p_prev - p) * mu.astype(f32)
    r, k, v, w_lat, a_lat, g_lat = split_last(
        p, [RWKV_WIDTH, RWKV_WIDTH, RWKV_WIDTH, RWKV_DECAY_RANK, RWKV_A_RANK, RWKV_GATE_RANK])
    w = -jax.nn.softplus(-(w0.astype(f32) + jnp.tanh(w_lat) @ w2.astype(f32))) - 0.5
    decay = jnp.exp(-jnp.exp(w))
    a = jax.nn.sigmoid(a0.astype(f32) + a_lat @ a2.astype(f32))
    g = jax.nn.sigmoid(g_lat) @ g2.astype(f32)
    heads = lambda t: t.reshape(Bsz, T, RWKV_HEADS, RWKV_HEAD_DIM)
    kk = heads(k * k_k.astype(f32))
    kk = kk / jnp.maximum(jnp.sqrt(jnp.sum(kk * kk, axis=-1, keepdims=True)), 1e-12)
    k = k * (1.0 + (a - 1.0) * k_a.astype(f32))
    r_h, k_h, v_h, a_h = heads(r), heads(k), heads(v), heads(a)
    y = rwkv7_scan(r_h, heads(decay), k_h, v_h, kk, a_h)
    y = head_group_norm(y, RWKV_GN_EPS).reshape(Bsz, T, RWKV_WIDTH) * ln_w.astype(f32) + ln_b.astype(f32)
    bonus = jnp.sum(r_h * k_h * r_k.astype(f32), axis=-1, keepdims=True) * v_h
    y = y + bonus.reshape(Bsz, T, RWKV_WIDTH)
    return y * g


def rglru_mix(p, conv_w, conv_b, wa, ba, wx, bx, lam):
    f32 = jnp.float32
    Bsz, T, _ = p.shape
    xb, gb = split_last(p, [LRU_WIDTH, LRU_WIDTH])
    xc = causal_depthwise_conv(xb, conv_w, conv_b).astype(f32)
    xh = xc.reshape(Bsz, T, LRU_BLOCKS, LRU_BLOCK_DIM)
    r = jax.nn.sigmoid(jnp.einsum('btgi,gij->btgj', xh, wa.astype(f32)).reshape(Bsz, T, LRU_WIDTH) + ba.astype(f32))
    i = jax.nn.sigmoid(jnp.einsum('btgi,gij->btgj', xh, wx.astype(f32)).reshape(Bsz, T, LRU_WIDTH) + bx.astype(f32))
    log_a = -LRU_C * r * jax.nn.softplus(-lam.astype(f32))
    a = jnp.exp(log_a)
    u = jnp.sqrt(-jnp.expm1(2.0 * log_a)) * (i * xc)

    def combine(e1, e2):
        a1, b1 = e1
        a2, b2 = e2
        return a1 * a2, a2 * b1 + b2

    _, h = lax.associative_scan(combine, (a, u), axis=1)
    return h * jax.nn.gelu(gb.astype(f32), approximate=True)


def retention_mix(p, gn_w):
    f32 = jnp.float32
    p = p.astype(f32)
    Bsz, T, _ = p.shape
    qk = RET_HEADS * RET_QK_DIM
    q, k, v, g = split_last(p, [qk, qk, RET_WIDTH, RET_WIDTH])
    pos = jnp.arange(T)
    q = rope(q.reshape(Bsz, T, RET_HEADS, RET_QK_DIM), pos)
    k = rope(k.reshape(Bsz, T, RET_HEADS, RET_QK_DIM), pos) * (RET_QK_DIM ** -0.5)
    v = v.reshape(Bsz, T, RET_HEADS, RET_V_DIM)
    pad = CHUNK - N_META
    Tp = T + pad
    nc = Tp // CHUNK
    qc = pad_front(q, pad).reshape(Bsz, nc, CHUNK, RET_HEADS, RET_QK_DIM)
    kc = pad_front(k, pad).reshape(Bsz, nc, CHUNK, RET_HEADS, RET_QK_DIM)
    vc = pad_front(v, pad).reshape(Bsz, nc, CHUNK, RET_HEADS, RET_V_DIM)
    log_g = jnp.log1p(-jnp.exp2(-5.0 - jnp.arange(RET_HEADS, dtype=f32)))
    idx = jnp.arange(CHUNK)
    rel = idx[:, None] - idx[None, :]
    inner_decay = jnp.where(rel >= 0, jnp.exp(jnp.maximum(rel, 0)[None] * log_g[:, None, None]), 0.0)
    scores = jnp.einsum('bclhd,bcshd->bchls', qc, kc) * inner_decay
    y_inner = jnp.einsum('bchls,bcshe->bclhe', scores, vc)
    k_decay = jnp.exp((CHUNK - 1 - idx)[None, :] * log_g[:, None])
    kv = jnp.einsum('bcshd,hs,bcshe->bchde', kc, k_decay, vc)
    cidx = jnp.arange(nc)
    crel = cidx[:, None] - cidx[None, :] - 1
    cross_decay = jnp.where(crel >= 0, jnp.exp(jnp.maximum(crel, 0)[None] * (CHUNK * log_g)[:, None, None]), 0.0)
    R = jnp.einsum('hzc,bchde->bzhde', cross_decay, kv)
    q_decay = jnp.exp((idx + 1)[None, :] * log_g[:, None])
    y_cross = jnp.einsum('bclhd,bchde,hl->bclhe', qc, R, q_decay)
    y = (y_inner + y_cross).reshape(Bsz, Tp, RET_HEADS, RET_V_DIM)[:, pad:]
    y = head_group_norm(y, RET_GN_EPS).reshape(Bsz, T, RET_WIDTH) * gn_w.astype(f32)
    return y * jax.nn.silu(g)


def setup_inputs(seed: int = 0) -> dict:
    key = jax.random.key(seed)
    ks = iter(jax.random.split(key, 48))
    f32 = jnp.float32
    nrm = lambda shape, scale: jax.random.normal(next(ks), shape, f32) * scale
    unif = lambda shape, lo, hi: jax.random.uniform(next(ks), shape, f32, lo, hi)
    L = DEPTH
    x = nrm((BATCH, SEQ, D_MODEL), 1.0)
    meta_tokens = nrm((N_META, D_MODEL), 1.0)
    pre_mix_norm = 1.0 + nrm((L, D_MODEL), 0.02)
    post_mix_norm = 1.0 + nrm((L, D_MODEL), 0.02)
    pre_ffn_norm = 1.0 + nrm((L, D_MODEL), 0.02)
    post_ffn_norm = 1.0 + nrm((L, D_MODEL), 0.02)
    w_in = nrm((L, D_MODEL, MIX_IN), D_MODEL ** -0.5)
    w_out = nrm((L, MIX_WIDTH, D_MODEL), MIX_WIDTH ** -0.5)
    ssd_conv_w = nrm((L, SSD_CONV, SSD_CONV_CH), SSD_CONV ** -0.5)
    ssd_conv_b = nrm((L, SSD_CONV_CH), 0.02)
    dt = jnp.exp(unif((L, SSD_HEADS), math.log(1e-3), math.log(1e-1)))
    ssd_dt_bias = dt + jnp.log(-jnp.expm1(-dt))
    ssd_a_log = jnp.log(unif((L, SSD_HEADS), 1.0, 16.0))
    ssd_d = 1.0 + nrm((L, SSD_HEADS), 0.1)
    ssd_norm_w = 1.0 + nrm((L, SSD_WIDTH), 0.02)
    rwkv_mu = unif((L, RWKV_IN), 0.0, 1.0)
    ratio = jnp.linspace(0.0, 1.0, RWKV_WIDTH, dtype=f32)
    rwkv_w0 = (-6.5 + 5.0 * ratio ** 0.85)[None, :] + nrm((L, RWKV_WIDTH), 0.1)
    rwkv_w2 = nrm((L, RWKV_DECAY_RANK, RWKV_WIDTH), 0.1 * RWKV_DECAY_RANK ** -0.5)
    rwkv_a0 = nrm((L, RWKV_WIDTH), 0.1)
    rwkv_a2 = nrm((L, RWKV_A_RANK, RWKV_WIDTH), 0.1 * RWKV_A_RANK ** -0.5)
    rwkv_g2 = nrm((L, RWKV_GATE_RANK, RWKV_WIDTH), RWKV_GATE_RANK ** -0.5)
    rwkv_k_k = 0.85 + nrm((L, RWKV_WIDTH), 0.02)
    rwkv_k_a = 1.0 + nrm((L, RWKV_WIDTH), 0.02)
    rwkv_r_k = nrm((L, RWKV_HEADS, RWKV_HEAD_DIM), 0.1)
    rwkv_ln_w = 1.0 + nrm((L, RWKV_WIDTH), 0.02)
    rwkv_ln_b = nrm((L, RWKV_WIDTH), 0.02)
    lru_conv_w = nrm((L, LRU_CONV, LRU_WIDTH), LRU_CONV ** -0.5)
    lru_conv_b = nrm((L, LRU_WIDTH), 0.02)
    lru_wa = nrm((L, LRU_BLOCKS, LRU_BLOCK_DIM, LRU_BLOCK_DIM), LRU_BLOCK_DIM ** -0.5)
    lru_ba = nrm((L, LRU_WIDTH), 0.02)
    lru_wx = nrm((L, LRU_BLOCKS, LRU_BLOCK_DIM, LRU_BLOCK_DIM), LRU_BLOCK_DIM ** -0.5)
    lru_bx = nrm((L, LRU_WIDTH), 0.02)
    s = unif((L, LRU_WIDTH), 0.9, 0.999) ** (1.0 / LRU_C)
    lru_lambda = jnp.log(s) - jnp.log1p(-s)
    ret_gn_w = 1.0 + nrm((L, RET_WIDTH), 0.02)
    ffn_w_gate = nrm((L, D_MODEL, D_FF), D_MODEL ** -0.5)
    ffn_w_up = nrm((L, D_MODEL, D_FF), D_MODEL ** -0.5)
    ffn_w_down = nrm((L, D_FF, D_MODEL), D_FF ** -0.5)
    return {"x": x, "meta_tokens": meta_tokens, "pre_mix_norm": pre_mix_norm,
            "post_mix_norm": post_mix_norm, "pre_ffn_norm": pre_ffn_norm,
            "post_ffn_norm": post_ffn_norm, "w_in": w_in, "w_out": w_out,
            "ssd_conv_w": ssd_conv_w, "ssd_conv_b": ssd_conv_b, "ssd_dt_bias": ssd_dt_bias,
            "ssd_a_log": ssd_a_log, "ssd_d": ssd_d, "ssd_norm_w": ssd_norm_w,
            "rwkv_mu": rwkv_mu, "rwkv_w0": rwkv_w0, "rwkv_w2": rwkv_w2, "rwkv_a0": rwkv_a0,
            "rwkv_a2": rwkv_a2, "rwkv_g2": rwkv_g2, "rwkv_k_k": rwkv_k_k, "rwkv_k_a": rwkv_k_a,
            "rwkv_r_k": rwkv_r_k, "rwkv_ln_w": rwkv_ln_w, "rwkv_ln_b": rwkv_ln_b,
            "lru_conv_w": lru_conv_w, "lru_conv_b": lru_conv_b, "lru_wa": lru_wa,
            "lru_ba": lru_ba, "lru_wx": lru_wx, "lru_bx": lru_bx, "lru_lambda": lru_lambda,
            "ret_gn_w": ret_gn_w, "ffn_w_gate": ffn_w_gate, "ffn_w_up": ffn_w_up,
            "ffn_w_down": ffn_w_down}


def reference(x, meta_tokens, pre_mix_norm, post_mix_norm, pre_ffn_norm, post_ffn_norm,
              w_in, w_out, ssd_conv_w, ssd_conv_b, ssd_dt_bias, ssd_a_log, ssd_d, ssd_norm_w,
              rwkv_mu, rwkv_w0, rwkv_w2, rwkv_a0, rwkv_a2, rwkv_g2, rwkv_k_k, rwkv_k_a,
              rwkv_r_k, rwkv_ln_w, rwkv_ln_b, lru_conv_w, lru_conv_b, lru_wa, lru_ba,
              lru_wx, lru_bx, lru_lambda, ret_gn_w, ffn_w_gate, ffn_w_up, ffn_w_down):
    Bsz = x.shape[0]
    meta = jnp.broadcast_to(meta_tokens.astype(x.dtype)[None], (Bsz, N_META, x.shape[-1]))
    h = jnp.concatenate([meta, x], axis=1)
    for l in range(DEPTH):
        hn = rms_norm(h, pre_mix_norm[l])
        proj = hn @ w_in[l]
        p_ssd, p_rwkv, p_lru, p_ret = split_last(proj, [SSD_IN, RWKV_IN, LRU_IN, RET_IN])
        y = jnp.concatenate([
            ssd_mix(p_ssd, ssd_conv_w[l], ssd_conv_b[l], ssd_dt_bias[l], ssd_a_log[l], ssd_d[l], ssd_norm_w[l]),
            rwkv7_mix(p_rwkv, rwkv_mu[l], rwkv_w0[l], rwkv_w2[l], rwkv_a0[l], rwkv_a2[l], rwkv_g2[l],
                      rwkv_k_k[l], rwkv_k_a[l], rwkv_r_k[l], rwkv_ln_w[l], rwkv_ln_b[l]),
            rglru_mix(p_lru, lru_conv_w[l], lru_conv_b[l], lru_wa[l], lru_ba[l], lru_wx[l], lru_bx[l], lru_lambda[l]),
            retention_mix(p_ret, ret_gn_w[l]),
        ], axis=-1).astype(h.dtype)
        h = h + rms_norm(y @ w_out[l], post_mix_norm[l])
        hn = rms_norm(h, pre_ffn_norm[l])
        f = (jax.nn.silu(hn @ ffn_w_gate[l]) * (hn @ ffn_w_up[l])) @ ffn_w_down[l]
        h = h + rms_norm(f, post_ffn_norm[l])
    return h[:, N_META:]
```

```python
import math
import os
import numpy as np
import ml_dtypes
import concourse.bass as bass
import concourse.mybir as mybir
from concourse.bass_utils import run_bass_kernel_spmd

F32 = mybir.dt.float32
BF16 = mybir.dt.bfloat16
AF = mybir.ActivationFunctionType
ALU = mybir.AluOpType

D = 1024
SEQ = 2048
DEPTH = 4
NMETA = 16
T = 2176
NCH = 17
PAD = 112
MT = 256
DFF = 2816
NJ = 22
EPS = 1e-6
C_W = 0.6065306597126334

CO = {}
_o = 0
for _n, _w in [("ident", 128), ("ones", 128), ("blk64", 128), ("triu", 128), ("maskneg", 128),
               ("msl", 128), ("msu", 128), ("miu", 128), ("ltT", 512), ("kdec", 128), ("qdec", 128),
               ("g128", 1), ("hm", 4), ("reset", 256)]:
    CO[_n] = (_o, _w)
    _o += _w
NCONST = ((_o + 63) // 64) * 64

PO = {}
_o = 0
for _n, _w in [("pre_mix", 8), ("post_mix", 8), ("pre_ffn", 8), ("post_ffn", 8),
               ("ssd_cw", 24), ("ssd_cb", 6), ("ssd_dtb", 4), ("ssd_alog", 4), ("ssd_d", 2), ("ssd_nw", 2),
               ("rw_mu", 8), ("rw_w0", 2), ("rw_a0", 2), ("rw_kk", 2), ("rw_ka", 2), ("rw_rk", 2),
               ("rw_lnw", 2), ("rw_lnb", 2),
               ("lru_cw", 8), ("lru_cb", 2), ("lru_ba", 2), ("lru_bx", 2), ("lru_lam", 2), ("ret_gnw", 2),
               ("d_aneg", 4), ("d_omka", 2), ("d_m8sp", 2), ("d_tmp", 4)]:
    PO[_n] = (_o, _w)
    _o += _w
NPAR = ((_o + 15) // 16) * 16

WIN_COLS = 3588


def _isz(dt):
    return 2 if dt == BF16 else 4


class Prog:
    ROT = 6000

    def __init__(self, nc):
        self.nc = nc
        self.ops = []
        self.track = {}

    @staticmethod
    def box(ap):
        isz = _isz(ap.dtype)
        pat = ap.ap
        pstride = pat[0][0] * isz
        offb = ap.offset * isz
        if pstride <= 0:
            p0, f0 = 0, offb
        else:
            p0, f0 = offb // pstride, offb % pstride
        ext = 1
        for st, cnt in pat[1:]:
            ext += (cnt - 1) * abs(st)
        nm = ap.tensor.name
        p1, b0, b1 = p0 + pat[0][1], f0, f0 + ext * isz
        if nm == "PS":
            p0, p1 = 0, 128
            b0 = (b0 // 2048) * 2048
            b1 = ((b1 + 2047) // 2048) * 2048
        return (nm, p0, p1, b0, b1)

    def add(self, eng, fn, reads, writes, kind="c", cost=300.0, tag=None):
        i = len(self.ops)
        if isinstance(eng, (list, tuple)):
            cands = list(eng)
            fns, costs = fn, cost
            eng = cands[0]
        else:
            cands, fns, costs = [eng], {eng: fn}, {eng: cost}
        deps = set()
        for ap in reads:
            nm, p0, p1, b0, b1 = self.box(ap)
            lst = self.track.setdefault(nm, [])
            for (j, q0, q1, c0, c1, w, e) in lst:
                if (w or nm == "PS") and q0 < p1 and p0 < q1 and c0 < b1 and b0 < c1:
                    deps.add(j)
        for ap in writes:
            nm, p0, p1, b0, b1 = self.box(ap)
            lst = self.track.setdefault(nm, [])
            nowar = os.environ.get("K_NOWAR") and nm == "arena"
            for (j, q0, q1, c0, c1, w, e) in lst:
                if q0 < p1 and p0 < q1 and c0 < b1 and b0 < c1 and not (nowar):
                    deps.add(j)
        for ap in reads:
            nm, p0, p1, b0, b1 = self.box(ap)
            if nm in ("Cc", "Cb", "eps_t"):
                continue
            self.track[nm].append((i, p0, p1, b0, b1, False, eng))
        for ap in writes:
            nm, p0, p1, b0, b1 = self.box(ap)
            lst = self.track[nm]
            lst[:] = [x for x in lst if not (p0 <= x[1] and x[2] <= p1 and b0 <= x[3] and x[4] <= b1)]
            lst.append((i, p0, p1, b0, b1, True, eng))
        deps.discard(i)
        self.ops.append(dict(eng=eng, fn=fns[eng], deps=deps, kind=kind, cost=costs[eng], cands=cands, fns=fns, costs=costs, tag=tag))
        return i

    def schedule(self):
        ops = self.ops
        n = len(ops)
        succ = [[] for _ in range(n)]
        indeg = [0] * n
        for i, o in enumerate(ops):
            indeg[i] = len(o["deps"])
            for j in o["deps"]:
                succ[j].append(i)
        fin = [0.0] * n
        engs = sorted(set(e for o in ops for e in o["cands"]))
        free = {e: 0.0 for e in engs}
        order = {e: [] for e in engs}
        ready = {}
        pend = []

        def release(i):
            o = ops[i]
            r = {}
            for e in o["cands"]:
                t = 0.0
                for j in o["deps"]:
                    tj = fin[j] + (60.0 if ops[j]["eng"] == e else 200.0)
                    if tj > t:
                        t = tj
                r[e] = t
            ready[i] = r
            pend.append(i)
        for i in range(n):
            if indeg[i] == 0:
                release(i)
        done = 0
        WIN = int(os.environ.get("K_WIN", "600"))
        lo = 0
        sched = [False] * n
        cur_tab = [None]
        TABLD = 1283.0
        while done < n:
            best = None
            for i in pend:
                if i > lo + WIN:
                    continue
                o = ops[i]
                r = ready[i]
                be = None
                for e in o["cands"]:
                    st = r[e] if r[e] > free[e] else free[e]
                    if e == "act" and o["tag"] is not None and o["tag"] != cur_tab[0]:
                        st += TABLD
                    f = st + o["costs"][e]
                    if be is None or f < be[0]:
                        be = (f, st, e)
                key = (be[1], i)
                if best is None or key < best[0]:
                    best = (key, i, be[2], be[1])
            if best is None:
                i = min(pend)
                o = ops[i]
                e = o["cands"][0]
                st = max(free[e], ready[i][e])
            else:
                _, i, e, st = best
                o = ops[i]
            pend.remove(i)
            del ready[i]
            o["eng"] = e
            o["fn"] = o["fns"][e]
            o["cost"] = o["costs"][e]
            if e == "act" and o["tag"] is not None:
                cur_tab[0] = o["tag"]
            f = st + o["cost"]
            free[e] = st + (o["cost"] if o["kind"] != "dma" else 60.0)
            fin[i] = f
            order[e].append(i)
            sched[i] = True
            done += 1
            while lo < n and sched[lo]:
                lo += 1
            for k2 in succ[i]:
                indeg[k2] -= 1
                if indeg[k2] == 0:
                    release(k2)
        self.est_ns = max(fin) if n else 0.0
        return order

    def plan(self, out_dma_ops):
        class _S:
            def __init__(self, i):
                self.idx = i
        cnt = [0]
        def gen():
            while True:
                cnt[0] += 1
                yield _S(cnt[0] - 1)
        self.order = self.schedule()
        self._plan = self._assign(gen(), out_dma_ops)
        return cnt[0]

    def emit(self, block, sems, out_dma_ops):
        red, sv, prevdma = self._plan
        ops = self.ops
        rs = lambda t: None if t is None else (sems[t[0].idx], t[1])
        sv = [rs(t) for t in sv]
        prevdma = [rs(t) for t in prevdma]
        self._emit(block, red, sv, prevdma, out_dma_ops)

    def _assign(self, si, out_dma_ops):
        ops = self.ops
        n = len(ops)
        need_inc = [False] * n
        pos = [0] * n
        for e, lst in self.order.items():
            for p_, i in enumerate(lst):
                pos[i] = p_
        red = []
        for i, o in enumerate(ops):
            best = {}
            dl = []
            for j in o["deps"]:
                oj = ops[j]
                if oj["kind"] == "dma":
                    dl.append(j)
                else:
                    if oj["eng"] == "pe" and o["eng"] == "pe" and o["kind"] == "c":
                        assert pos[j] < pos[i]
                        continue
                    b = best.get(oj["eng"])
                    if b is None or pos[b] < pos[j]:
                        best[oj["eng"]] = j
            dl += list(best.values())
            for j in dl:
                need_inc[j] = True
            red.append(dl)
        for j in out_dma_ops:
            need_inc[j] = True
        eng_sems = {}
        cnt = {}
        dq = {}
        dcnt = {}
        NDQ = 8
        sv = [None] * n
        prevdma = [None] * n
        seq = [i for e in sorted(self.order) for i in self.order[e]]
        for i in seq:
            o = ops[i]
            e = o["eng"]
            if o["kind"] == "dma":
                if e not in dq:
                    dq[e] = [next(si) for _ in range(NDQ)]
                    dcnt[e] = [0] * NDQ
                    cnt[("d", e)] = 0
                k = cnt[("d", e)] % NDQ
                cnt[("d", e)] += 1
                if dcnt[e][k] >= 16 * 1500:
                    dq[e][k] = next(si)
                    dcnt[e][k] = 0
                prevdma[i] = (dq[e][k], dcnt[e][k])
                dcnt[e][k] += 16
                sv[i] = (dq[e][k], dcnt[e][k])
            elif need_inc[i]:
                c = cnt.get(e, 0)
                if c % self.ROT == 0:
                    eng_sems[e] = next(si)
                cnt[e] = c + 1
                sv[i] = (eng_sems[e], c % self.ROT + 1)
        return red, sv, prevdma

    def _emit(self, block, red, sv, prevdma, out_dma_ops):
        ops = self.ops
        engmap = {"pe": block.tensor, "act": block.scalar, "dve": block.vector, "pool": block.gpsimd,
                  "sp": block.sync}
        for ename, deco in engmap.items():
            def body(eng, ename=ename):
                waited = {}
                for i in self.order.get(ename, []):
                    o = ops[i]
                    for j in red[i]:
                        s, v = sv[j]
                        if waited.get(s.num if hasattr(s, "num") else id(s), 0) >= v:
                            continue
                        eng.wait_ge(s, v)
                        waited[s.num if hasattr(s, "num") else id(s)] = v
                    if o["kind"] == "dma":
                        ps, pv = prevdma[i]
                        key = ps.num if hasattr(ps, "num") else id(ps)
                        if pv > 0 and waited.get(key, 0) < pv:
                            eng.wait_ge(ps, pv)
                            waited[key] = pv
                        o["fn"](eng).then_inc(sv[i][0], 16)
                    else:
                        ins = o["fn"](eng)
                        if sv[i] is not None:
                            ins.then_inc(sv[i][0], 1)
                if ename == "sp":
                    for j in out_dma_ops:
                        s, v = sv[j]
                        eng.wait_ge(s, v)
            deco(body)


def _fn(ap):
    n = 1
    for d in ap.shape[1:]:
        n *= d
    return n


def _ec(eng, out, ins, kind="tt"):
    n = _fn(out)
    if eng == "act":
        return 225.0 + n * 0.85
    if eng == "dve":
        return 150.0 + n * 1.15
    if kind == "ts":
        return 1100.0 + n * 1.2
    if kind == "cp":
        return 220.0 + n * 1.0
    return 330.0 + n * 1.95


class K:
    def __init__(self, nc, prog):
        self.nc = nc
        self.p = prog

    def mm(self, out, lhsT, rhs, start=True, stop=True):
        rd = [lhsT, rhs] + ([] if start else [out])
        nn = max(_fn(rhs), 64) * (4 if _isz(rhs.dtype) == 4 else 1)
        self.p.add("pe", lambda e: e.matmul(out, lhsT, rhs, start=start, stop=stop), rd, [out], cost=57.0 + nn / 2.4)

    def tr(self, out, in_, ident):
        self.p.add("pe", lambda e: e.transpose(out, in_, ident), [in_, ident], [out], cost=90.0 * (2 if _isz(in_.dtype) == 4 else 1))

    def act(self, out, in_, func, bias=None, scale=None, eng="act"):
        rd = [in_]
        kw = {}
        if bias is not None:
            kw["bias"] = bias
            if not isinstance(bias, float):
                rd.append(bias)
        if scale is not None:
            kw["scale"] = scale
            if not isinstance(scale, float):
                rd.append(scale)
        tag = None if func in (AF.Copy, AF.Identity) else ("explog" if func in (AF.Exp, AF.Ln) else str(func))
        self.p.add("act", lambda e: e.activation(out=out, in_=in_, func=func, **kw), rd, [out], cost=_ec("act", out, [in_]), tag=tag)

    @staticmethod
    def _cands(out, ins, act_ok=False):
        ps = any(a.tensor.name == "PS" for a in list(ins) + [out])
        c = ["dve"] if ps else ["dve", "pool"]
        if act_ok:
            c.append("act")
        return c

    def _flex(self, cands, mk, out, ins, reads, kind="tt"):
        fns = {e: mk(e) for e in cands}
        costs = {e: _ec(e, out, ins, kind) for e in cands}
        self.p.add(cands, fns, reads, [out], cost=costs)

    def tt(self, eng, out, in0, in1, op):
        mk = lambda en: (lambda e: e.tensor_tensor(out=out, in0=in0, in1=in1, op=op))
        self._flex(self._cands(out, [in0, in1]), mk, out, [in0, in1], [in0, in1])

    def ts(self, eng, out, in0, s1, s2=None, op0=ALU.mult, op1=None):
        rd = [in0] + [s for s in (s1, s2) if s is not None and not isinstance(s, float)]
        if op1 is None:
            mk = lambda en: (lambda e: e.tensor_scalar(out=out, in0=in0, scalar1=s1, scalar2=None, op0=op0))
        else:
            mk = lambda en: (lambda e: e.tensor_scalar(out=out, in0=in0, scalar1=s1, scalar2=s2, op0=op0, op1=op1))
        self._flex(self._cands(out, rd), mk, out, [in0], rd, kind="ts")

    def stt(self, eng, out, in0, scalar, in1, op0, op1):
        eng = "dve"
        rd = [in0, in1] + ([] if isinstance(scalar, float) else [scalar])
        self.p.add(eng, lambda e: e.scalar_tensor_tensor(out=out, in0=in0, scalar=scalar, in1=in1, op0=op0, op1=op1), rd, [out], cost=_ec(eng, out, [in0, in1]))

    def cp(self, eng, out, in_):
        def mk(en):
            if en == "act":
                return lambda e: e.activation(out=out, in_=in_, func=AF.Copy)
            return lambda e: e.tensor_copy(out=out, in_=in_)
        self._flex(self._cands(out, [in_], act_ok=True), mk, out, [in_], [in_], kind="cp")

    def ms(self, eng, ap, val):
        mk = lambda en: (lambda e: e.memset(ap, val))
        self._flex(self._cands(ap, []), mk, ap, [], [], kind="cp")

    def scan(self, eng, out, d0, d1, init):
        eng = "dve"
        rd = [d0, d1] + ([] if isinstance(init, float) else [init])
        self.p.add(eng, lambda e: e.tensor_tensor_scan(out=out, data0=d0, data1=d1, initial=init, op0=ALU.mult, op1=ALU.add), rd, [out], cost=100.0 + _fn(out) / 0.5)

    def dma_in(self, q, out, in_):
        return self.p.add(q, lambda e: e.dma_start(out=out, in_=in_), [], [out], kind="dma", cost=2500.0 + _fn(out) * 128 * 4 / 320.0)

    def dma_out(self, q, out, in_):
        return self.p.add(q, lambda e: e.dma_start(out=out, in_=in_), [in_], [], kind="dma", cost=2500.0 + _fn(in_) * 128 * 4 / 320.0)


class Arena:
    def __init__(self, ap_f32, nwords):
        self.ap = ap_f32
        self.n = nwords
        self.off = 0

    def reset(self, off=0):
        if os.environ.get("K_ARDBG") and getattr(self, "hi", 0):
            print("   arena hi", self.hi, "of", self.n)
        self.hi = 0
        self.off = off

    def f32(self, *shape):
        n = int(np.prod(shape))
        self.hi = max(getattr(self, "hi", 0), self.off + n)
        assert self.off + n <= self.n, ("arena overflow", self.off, n, self.n)
        v = self.ap[:, self.off:self.off + n]
        self.off += n
        return self._shape(v, shape)

    def bf16(self, *shape):
        n = int(np.prod(shape))
        nw = (n + 1) // 2
        self.hi = max(getattr(self, "hi", 0), self.off + nw)
        assert self.off + nw <= self.n, ("arena overflow", self.off, nw, self.n)
        v = self.ap[:, self.off:self.off + nw].bitcast(BF16)[:, 0:n]
        self.off += nw
        return self._shape(v, shape)

    @staticmethod
    def _shape(v, shape):
        if len(shape) == 1:
            return v
        if len(shape) == 2:
            return v.rearrange("p (a b) -> p a b", a=shape[0])
        if len(shape) == 3:
            return v.rearrange("p (a b c) -> p a b c", a=shape[0], b=shape[1])
        if len(shape) == 4:
            return v.rearrange("p (a b c d) -> p a b c d", a=shape[0], b=shape[1], c=shape[2])
        raise ValueError


def tiles_of(t0, t1, step):
    out = []
    t = t0
    while t < t1:
        out.append((t, min(t + step, t1)))
        t += step
    return out


def build(n_layers, debug=False, stages=("ssd", "rwkv", "lru", "ret", "wout", "ffn")):
    nc = bass.Bass("TRN2", target_bir_lowering=False)
    L = n_layers
    dr = {}

    anymix = any(st in stages for st in ("ssd", "rwkv", "lru", "ret"))
    need = {"x": True, "meta": not os.environ.get("K_NOMETA"), "consts": True, "rope": "ret" in stages,
            "params": not os.environ.get("K_NOSETUP"), "smat": not os.environ.get("K_NOSETUP"), "w_in": anymix,
            "w_out": "wout" in stages, "w_gu": "ffn" in stages, "w_d": "ffn" in stages}

    def din(name, shape, dt=F32):
        if not need[name]:
            return None
        dr[name] = nc.dram_tensor(name, shape, dt, kind="ExternalInput").ap()
        return dr[name]

    x_d = din("x", [SEQ, D])
    meta_d = din("meta", [NMETA, D])
    consts_d = din("consts", [128, NCONST])
    rope_d = din("rope", [4, 128, T])
    par_d = din("params", [L, 128, NPAR])
    smat_d = din("smat", [L, 128, 1024])
    win_d = din("w_in", [L, 8, 128, WIN_COLS])
    wout_d = din("w_out", [L, 128, 8, D])
    wgu_d = din("w_gu", [L, NJ, 128, 2, 8, 128])
    wd_d = din("w_d", [L, 8, 128, NJ, 128])
    out_d = nc.dram_tensor("out", [SEQ, D], F32, kind="ExternalOutput").ap()
    if debug:
        dbg_d = nc.dram_tensor("dbg", [128, 8, T], BF16, kind="ExternalOutput").ap()
        dbgh_d = nc.dram_tensor("dbgh", [128, 8, T], F32, kind="ExternalOutput").ap()

    ARW = int(os.environ.get("K_ARW", "21900"))
    from contextlib import ExitStack
    with ExitStack() as es:
        def sb(name, shape, dt):
            return es.enter_context(nc.sbuf_tensor(name, shape, dt))
        h = sb("h", [128, 8, T], F32)
        yT = sb("yT", [128, 8, T], BF16)
        rstd_all = sb("rstd_all", [128, T], F32)
        Cc = sb("Cc", [128, NCONST], F32)
        Cb = sb("Cb", [128, 384], BF16)
        PRM = sb("PRM", [128, NPAR], F32)
        SM = sb("SM", [128, 1024], BF16)
        arena_t = sb("arena", [128, ARW], F32)
        EPS_T = sb("eps_t", [128, 4], F32)
        PS = es.enter_context(nc.psum_tensor("PS", [128, 4096], F32))

        prog = Prog(nc)
        k = K(nc, prog)
        ar = Arena(arena_t[:, :], ARW)
        out_ops = []

        def cst(name, a=None, b=None):
            o, w = CO[name]
            if a is None:
                return Cc[:, o:o + w]
            return Cc[:, o + a:o + b]

        def par(name, c=0, w=1):
            o, _ = PO[name]
            return PRM[:, o + c:o + c + w]

        ident_f = cst("ident")
        ones_f = cst("ones")
        blk64_f = cst("blk64")
        ident_b = Cb[:, 0:128]
        ones_b = Cb[:, 128:256]
        blk64_b = Cb[:, 256:384]

        psn = {"all": 0, "proj": 0, "chain": 0}
        bank_mode = ["all"]

        def bank(nb=1, cls="chain"):
            if bank_mode[0] == "all":
                lo_, hi_ = 0, 8
                key = "all"
            elif cls == "proj":
                lo_, hi_ = 0, NPROJ_BANKS
                key = "proj"
            else:
                lo_, hi_ = NPROJ_BANKS, 8
                key = "chain"
            b = psn[key]
            if b < lo_ or b + nb > hi_:
                b = lo_
            psn[key] = b + nb
            return PS[:, b * 512:(b + nb) * 512]

        NPROJ_BANKS = int(os.environ.get("K_NPB", "2"))
        if debug:
            k.ms("pool", yT[:, :, :], 0.0)
        k.dma_in("sp", Cc[:, :], consts_d[:, :])
        k.dma_in("pool", Cb[:, :], consts_d[:, 0:384])

        ar.reset()
        stg = [ar.f32(D), ar.f32(D)]
        for ch in range(NCH):
            s = stg[ch % 2]
            if ch == 0:
                k.ms("pool", s, 0.0)
                if not os.environ.get("K_NOMETA"):
                    k.dma_in("sp", s[PAD:128, :], meta_d[:, :])
            else:
                k.dma_in("sp", s, x_d[(ch - 1) * 128:ch * 128, :])
            for half in range(2):
                pb = bank()
                for kk4 in range(4):
                    kc = half * 4 + kk4
                    k.tr(pb[:, kk4 * 128:(kk4 + 1) * 128], s[:, kc * 128:(kc + 1) * 128], ident_f)
                for kk4 in range(4):
                    kc = half * 4 + kk4
                    k.cp(os.environ.get("K_CPENG") or (("dve" if kk4 % 2 else "act") if os.environ.get("K_SWAP") else ("act" if kk4 % 2 else "dve")), h[:, kc, ch * 128:(ch + 1) * 128], pb[:, kk4 * 128:(kk4 + 1) * 128])

        def rms_stats(src_of_k, n, rs_out, nfeat_chunks=8, denom=1024.0, sqbuf=None):
            pb = bank()
            for kc in range(nfeat_chunks):
                sq = sqbuf[kc % 2]
                src = src_of_k(kc)
                eng = "pool" if kc % 2 else "dve"
                k.tt(eng, sq[:, 0:n], src, src, ALU.mult)
                k.mm(pb[:, 0:n], ones_b, sq[:, 0:n], start=(kc == 0), stop=(kc == nfeat_chunks - 1))
            k.act(rs_out, pb[:, 0:n], AF.Ln, bias=EPS_AP, scale=1.0 / denom)
            k.act(rs_out, rs_out, AF.Exp, scale=-0.5)

        def layer_setup(l):
            k.dma_in("sp", PRM[:, 0:PO["d_aneg"][0]], par_d[l, :, 0:PO["d_aneg"][0]])
            k.dma_in("pool", SM[:, :], smat_d[l, :, :])
            k.act(par("d_aneg", 0, 4), par("ssd_alog", 0, 4), AF.Exp)
            k.ts("dve", par("d_aneg", 0, 4), par("d_aneg", 0, 4), -1.0)
            k.ts("dve", par("d_omka", 0, 2), par("rw_ka", 0, 2), -1.0, 1.0, ALU.mult, ALU.add)
            k.act(par("d_m8sp", 0, 2), par("lru_lam", 0, 2), AF.Exp, scale=-1.0)
            k.act(par("d_m8sp", 0, 2), par("d_m8sp", 0, 2), AF.Ln, bias=ONE_AP)
            k.ts("dve", par("d_m8sp", 0, 2), par("d_m8sp", 0, 2), -8.0)

        if not os.environ.get("K_NOEPS"):
            k.ms("dve", EPS_T[:, 0:1], EPS)
            k.ms("dve", EPS_T[:, 1:2], 1.0)
            k.ms("dve", EPS_T[:, 2:3], 64e-5)
            k.ms("dve", EPS_T[:, 3:4], 1e-5)
        EPS_AP = EPS_T[:, 0:1]
        ONE_AP = EPS_T[:, 1:2]
        EPS_RW = EPS_T[:, 2:3]
        EPS_RET = EPS_T[:, 3:4]

        mtiles = tiles_of(0, T, MT)

        from types import SimpleNamespace
        ctx = SimpleNamespace(**locals())

        for l in range(L):
            if not os.environ.get("K_NOSETUP"):
                layer_setup(l)
            if "mix" in stages or any(s in stages for s in ("ssd", "rwkv", "lru", "ret")):
                mixer_phase(ctx, l, stages)
            if "wout" in stages:
                wout_phase(ctx, l)
            if "ffn" in stages:
                ffn_phase(ctx, l)
            if debug and l == 0:
                out_ops.append(k.dma_out("sp", dbg_d[:, :, :], yT[:, :, :]))
                out_ops.append(k.dma_out("sp", dbgh_d[:, :, :], h[:, :, :]))

        ar.reset()
        stg = [ar.f32(D), ar.f32(D)]
        for ch in range(1, NCH):
            s = stg[ch % 2]
            for half in range(2):
                pb = bank()
                for kk4 in range(4):
                    kc = half * 4 + kk4
                    k.tr(pb[:, kk4 * 128:(kk4 + 1) * 128], h[:, kc, ch * 128:(ch + 1) * 128], ident_f)
                k.cp("act" if half else "dve", s[:, half * 512:(half + 1) * 512], pb[:, 0:512])
            out_ops.append(k.dma_out("sp", out_d[(ch - 1) * 128:ch * 128, :], s))

        mx = int(os.environ.get("K_MAXOPS", "0"))
        if mx:
            prog.ops = prog.ops[:mx]
            out_ops = [j for j in out_ops if j < mx]
        nsem = prog.plan(out_ops)
        print("ops", len(prog.ops), "sems", nsem, "est_us", round(prog.est_ns / 1000.0, 1), flush=True)
        sems = [es.enter_context(nc.semaphore("s%d" % i)) for i in range(nsem)]
        block = es.enter_context(nc.Block())
        prog.emit(block, sems, out_ops)
    nc._in_names = list(dr.keys())
    return nc


def hn_tile(c, l, t0, n, hnT, first_pass, which="pre_mix"):
    k = c.k
    if first_pass:
        c.rms_stats(lambda kc: c.h[:, kc, t0:t0 + n], n, c.rstd_all[:, t0:t0 + n], sqbuf=c.sqbuf)
    for kc in range(8):
        eng = "pool" if kc % 2 else "dve"
        k.stt(eng, hnT[:, kc, 0:n], c.h[:, kc, t0:t0 + n], c.par(which, kc), c.rstd_all[:, t0:t0 + n], ALU.mult, ALU.mult)


def load_win(c, l, Wm, ranges):
    lo = 0
    for (a, b) in ranges:
        for kc in range(8):
            c.k.dma_in("pool", Wm[:, kc, lo:lo + (b - a)], c.win_d[l, kc, :, a:b])
        lo += b - a


def proj_F(c, Wm, hnT, n, col0, dst, evac="act"):
    pb = c.bank(cls="proj")
    for kc in range(8):
        c.k.mm(pb[:, 0:n], Wm[:, kc, col0:col0 + 128], hnT[:, kc, 0:n], start=(kc == 0), stop=(kc == 7))
    c.k.cp(evac, dst, pb[:, 0:n])


def head_norm_F(c, YF, n, eps_ap, tmp, rs):
    k = c.k
    pb = c.bank()
    k.mm(pb[:, 0:n], c.blk64_f, YF, start=True, stop=True)
    k.stt("dve", YF, pb[:, 0:n], -1.0 / 64, YF, ALU.mult, ALU.add)
    k.tt("pool", tmp, YF, YF, ALU.mult)
    pb2 = c.bank()
    k.mm(pb2[:, 0:n], c.blk64_f, tmp, start=True, stop=True)
    k.act(rs, pb2[:, 0:n], AF.Ln, bias=eps_ap, scale=1.0 / 64)
    k.act(rs, rs, AF.Exp, scale=-0.5)
    k.tt("dve", YF, YF, rs, ALU.mult)


def mixer_phase(c, l, stages):
    k, ar = c.k, c.ar
    c.bank_mode[0] = "split"
    ar.reset()
    hnT2 = [ar.bf16(8, MT), ar.bf16(8, MT)]
    hnT = hnT2[0]
    c.sqbuf = [ar.bf16(512), ar.bf16(512)]
    base0 = ar.off
    first = (l == 0) or ("ffn" not in stages) or bool(os.environ.get("K_NOPRE"))
    if "lru" in stages and "ret" in stages:
        ar.reset(base0)
        Wm = ar.bf16(8, 1536)
        load_win(c, l, Wm, [(2052, 2564), (2564, 3332), (3332, 3588)])
        lt = lru_pass(c, l, Wm, hnT2, None, wo=0)
        rt = ret_pass(c, l, Wm, hnT2, None, wo=512)
        for ti, (t0, t1) in enumerate(c.mtiles):
            hn_tile(c, l, t0, t1 - t0, hnT2[ti % 2], first)
            lt(ti, t0, t1)
            rt(ti, t0, t1)
        first = False
        todo = ("ssd", "rwkv")
    else:
        todo = ("lru", "ret", "ssd", "rwkv")
    for name in todo:
        if name not in stages:
            continue
        ar.reset(base0)
        Wm = ar.bf16(8, 1028)
        if name == "lru":
            load_win(c, l, Wm, [(2052, 2564)])
            lt = lru_pass(c, l, Wm, hnT2, None, wo=0)
            for ti, (t0, t1) in enumerate(c.mtiles):
                hn_tile(c, l, t0, t1 - t0, hnT2[ti % 2], first)
                lt(ti, t0, t1)
        elif name == "ret":
            load_win(c, l, Wm, [(2564, 3332), (3332, 3588)])
            rt = ret_pass(c, l, Wm, hnT2, None, wo=0)
            for ti, (t0, t1) in enumerate(c.mtiles):
                hn_tile(c, l, t0, t1 - t0, hnT2[ti % 2], first)
                rt(ti, t0, t1)
        elif name == "ssd":
            ssd_pass(c, l, Wm, hnT2, first)
        elif name == "rwkv":
            rwkv_pass(c, l, Wm, hnT, first)
        first = False


def conv4(c, eng, out, xbuf, n, wname, bname, ci):
    k = c.k
    k.ts(eng, out, xbuf[:, 3:3 + n], c.par(wname, ci * 4 + 3), c.par(bname, ci), ALU.mult, ALU.add)
    for j in (2, 1, 0):
        k.stt(eng, out, xbuf[:, j:j + n], c.par(wname, ci * 4 + j), out, ALU.mult, ALU.add)


def lru_pass(c, l, Wm, hnT, first, wo=0):
    k, ar = c.k, c.ar
    XB_ = [ar.f32(2, 3 + MT), ar.f32(2, 3 + MT)]
    GB_ = [ar.f32(2, MT), ar.f32(2, MT)]
    XC = ar.f32(2, MT)
    XCb = ar.bf16(2, MT)
    RG = ar.f32(MT)
    IG = ar.f32(MT)
    AA = ar.f32(MT)
    UU = ar.f32(MT)
    HT = ar.f32(MT)
    hlast = ar.f32(2)
    k.ms("dve", hlast, 0.0)

    hnT2 = hnT

    def tile(ti, t0, t1):
        hnT = hnT2[ti % 2]
        XB, GB, XBp = XB_[ti % 2], GB_[ti % 2], XB_[(ti + 1) % 2]
        n = t1 - t0
        if ti == 0:
            k.ms("pool", XB[:, :, 0:3], 0.0)
        else:
            k.cp("pool", XB[:, :, 0:3], XBp[:, :, MT:MT + 3])
        for ci in range(2):
            proj_F(c, Wm, hnT, n, wo + ci * 128, XB[:, ci, 3:3 + n], "act")
            proj_F(c, Wm, hnT, n, wo + 256 + ci * 128, GB[:, ci, 0:n], "act")
        for ci in range(2):
            conv4(c, "dve" if ci else "pool", XC[:, ci, 0:n], XB[:, ci, :], n, "lru_cw", "lru_cb", ci)
            k.cp("act", XCb[:, ci, 0:n], XC[:, ci, 0:n])
            pr = c.bank()
            k.mm(pr[:, 0:n], c.SM[:, 512 + ci * 128:512 + (ci + 1) * 128], XCb[:, ci, 0:n])
            pi = c.bank()
            k.mm(pi[:, 0:n], c.SM[:, 768 + ci * 128:768 + (ci + 1) * 128], XCb[:, ci, 0:n])
            k.act(RG[:, 0:n], pr[:, 0:n], AF.Sigmoid, bias=c.par("lru_ba", ci))
            k.act(IG[:, 0:n], pi[:, 0:n], AF.Sigmoid, bias=c.par("lru_bx", ci))
            k.act(AA[:, 0:n], RG[:, 0:n], AF.Exp, scale=c.par("d_m8sp", ci))
            k.tt("pool", UU[:, 0:n], AA[:, 0:n], AA[:, 0:n], ALU.mult)
            k.act(UU[:, 0:n], UU[:, 0:n], AF.Sqrt, bias=c.ONE_AP, scale=-1.0)
            k.tt("dve", UU[:, 0:n], UU[:, 0:n], IG[:, 0:n], ALU.mult)
            k.tt("dve", UU[:, 0:n], UU[:, 0:n], XC[:, ci, 0:n], ALU.mult)
            if ti == 0:
                k.ms("dve", UU[:, 0:PAD], 0.0)
            k.scan("dve", HT[:, 0:n], AA[:, 0:n], UU[:, 0:n], hlast[:, ci:ci + 1])
            k.cp("dve", hlast[:, ci:ci + 1], HT[:, n - 1:n])
            k.act(IG[:, 0:n], GB[:, ci, 0:n], AF.Gelu_apprx_tanh)
            k.tt("dve", c.yT[:, 4 + ci, t0:t1], HT[:, 0:n], IG[:, 0:n], ALU.mult)
    return tile


def ret_pass(c, l, Wm, hnT, first, wo=0):
    k, ar = c.k, c.ar
    PRj_ = [ar.f32(6, MT), ar.f32(6, MT)]
    ROP = ar.f32(4, MT)
    T1 = ar.f32(MT)
    T2 = ar.f32(MT)
    QR_ = [ar.bf16(MT), ar.bf16(MT)]
    KR_ = [ar.bf16(MT), ar.bf16(MT)]
    QD_ = [ar.bf16(MT), ar.bf16(MT)]
    KM_ = [ar.bf16(4, MT), ar.bf16(4, MT)]
    Vt_ = [ar.bf16(2, 256), ar.bf16(2, 256)]
    KD = ar.bf16(128)
    MS = ar.bf16(4, 128)
    YF = ar.f32(2, MT)
    Rm = ar.f32(4, 64)
    Rmb = ar.bf16(4, 64)
    TK = ar.f32(4, 64)
    k.ms("dve", Rm, 0.0)
    k.ms("dve", Rmb, 0.0)
    hm = c.cst("hm")
    srcs = [wo + 0, wo + 128, wo + 512, wo + 640, wo + 768, wo + 896]

    hnT2 = hnT

    def tile(ti, t0, t1):
        hnT = hnT2[ti % 2]
        PRj = PRj_[ti % 2]
        QR, KR, QD, KM, Vt = QR_[ti % 2], KR_[ti % 2], QD_[ti % 2], KM_[ti % 2], Vt_[ti % 2]
        n = t1 - t0
        nch = n // 128
        k.dma_in("sp", ROP[:, :, 0:n], c.rope_d[:, :, t0:t1].rearrange("a p t -> p a t"))
        for i, col in enumerate(srcs):
            proj_F(c, Wm, hnT, n, col, PRj[:, i, 0:n], "act" if i % 2 else "dve")
        for ci in range(nch):
            pv = c.bank()
            for kc in range(8):
                k.mm(pv[:, 0:256], hnT[:, kc, ci * 128:(ci + 1) * 128], Wm[:, kc, wo + 256:wo + 512], start=(kc == 0), stop=(kc == 7))
            k.cp("act", Vt[:, ci, :], pv[:, 0:256])
        k.tt("dve", T1[:, 0:n], PRj[:, 0, 0:n], ROP[:, 0, 0:n], ALU.mult)
        k.tt("pool", T2[:, 0:n], PRj[:, 4, 0:n], ROP[:, 1, 0:n], ALU.mult)
        k.tt("dve", T1[:, 0:n], T1[:, 0:n], T2[:, 0:n], ALU.add)
        k.cp("act", QR[:, 0:n], T1[:, 0:n])
        k.tt("dve", QD[:, 0:n].rearrange("p (a b) -> p a b", a=nch), T1[:, 0:n].rearrange("p (a b) -> p a b", a=nch),
             c.cst("qdec").unsqueeze(1).to_broadcast([128, nch, 128]), ALU.mult)
        k.tt("dve", T1[:, 0:n], PRj[:, 1, 0:n], ROP[:, 2, 0:n], ALU.mult)
        k.tt("pool", T2[:, 0:n], PRj[:, 5, 0:n], ROP[:, 3, 0:n], ALU.mult)
        k.tt("dve", T1[:, 0:n], T1[:, 0:n], T2[:, 0:n], ALU.add)
        k.cp("act", KR[:, 0:n], T1[:, 0:n])
        k.tt("dve", KM[:, :, 0:n], T1[:, 0:n].unsqueeze(1).to_broadcast([128, 4, n]),
             hm.unsqueeze(2).to_broadcast([128, 4, n]), ALU.mult)
        for ci in range(nch):
            co = ci * 128
            pt = c.bank()
            ptb = pt[:, 0:64].bitcast(BF16)
            k.tr(ptb, KR[:, co:co + 128], c.ident_b)
            k.tt("dve", KD, ptb, c.cst("kdec"), ALU.mult)
            psc = c.bank()
            for hh in range(4):
                k.mm(psc[:, hh * 128:(hh + 1) * 128], KM[:, hh, co:co + 128], QR[:, co:co + 128])
            k.tt("dve", MS.rearrange("p a b -> p (a b)"), psc[:, 0:512], c.cst("ltT"), ALU.mult)
            py = c.bank()
            for hh in range(4):
                pc, po = hh // 2, 64 * (hh % 2)
                o = py[po:po + 64, pc * 128:(pc + 1) * 128]
                k.mm(o, Vt[:, ci, hh * 64:(hh + 1) * 64], MS[:, hh, :], start=True, stop=False)
                k.mm(o, Rmb[:, hh, :], QD[:, co:co + 128], start=False, stop=True)
            k.cp("act", YF[:, :, co:co + 128], py[:, 0:256].rearrange("p (a b) -> p a b", a=2))
            pk = c.bank()
            for hh in range(4):
                k.mm(pk[:, hh * 64:(hh + 1) * 64], KD, Vt[:, ci, hh * 64:(hh + 1) * 64])
            k.tt("dve", TK, pk[:, 0:256].rearrange("p (a b) -> p a b", a=4), hm.unsqueeze(2).to_broadcast([128, 4, 64]), ALU.mult)
            k.stt("dve", Rm, Rm, c.cst("g128"), TK, ALU.mult, ALU.add)
            k.cp("act", Rmb, Rm)
        for ci2 in range(2):
            head_norm_F(c, YF[:, ci2, 0:n], n, c.EPS_RET, T1[:, 0:n], T2[:, 0:n])
            k.act(T1[:, 0:n], PRj[:, 2 + ci2, 0:n], AF.Silu)
            k.stt("dve", c.yT[:, 6 + ci2, t0:t1], YF[:, ci2, 0:n], c.par("ret_gnw", ci2), T1[:, 0:n], ALU.mult, ALU.mult)
    return tile


def ssd_pass(c, l, Wm, hnT, first):
    k, ar = c.k, c.ar
    load_win(c, l, Wm, [(0, 1028)])
    hnT2 = hnT
    ZB_ = [ar.f32(2, MT), ar.f32(2, MT)]
    XBC_ = [ar.f32(6, 3 + MT), ar.f32(6, 3 + MT)]
    ACC = ar.f32(MT)
    XCb_ = [ar.bf16(6, MT), ar.bf16(6, MT)]
    D4_ = [ar.f32(8, 4), ar.f32(8, 4)]
    TA_ = [ar.f32(4, 128), ar.f32(4, 128)]
    T1_ = [ar.f32(4, 128), ar.f32(4, 128)]
    LT_ = [ar.f32(4, 128), ar.f32(4, 128)]
    EC_ = [ar.f32(4, 128), ar.f32(4, 128)]
    XDT_ = [ar.bf16(4, 64), ar.bf16(4, 64)]
    XDC_ = [ar.bf16(4, 64), ar.bf16(4, 64)]
    BTt_ = [ar.bf16(2, 128), ar.bf16(2, 128)]
    MSK_ = [ar.bf16(4, 128), ar.bf16(4, 128)]
    CDC_ = [ar.bf16(4, 128), ar.bf16(4, 128)]
    S = ar.f32(4, 64)
    Sb = ar.bf16(4, 64)
    TS_ = ar.f32(4, 64)
    YF = ar.f32(2, MT)
    ZS = ar.f32(MT)
    RS = ar.f32(MT)
    SQ = ar.bf16(MT)
    k.ms("dve", S, 0.0)
    k.ms("dve", Sb, 0.0)
    triu = c.cst("triu")
    for ti, (t0, t1) in enumerate(c.mtiles):
        n = t1 - t0
        nch = n // 128
        hnT = hnT2[ti % 2]
        ZB, XBC, XBCp = ZB_[ti % 2], XBC_[ti % 2], XBC_[(ti + 1) % 2]
        XCb = XCb_[ti % 2]
        hn_tile(c, l, t0, n, hnT, first)
        if ti == 0:
            k.ms("pool", XBC[:, :, 0:3], 0.0)
        else:
            k.cp("pool", XBC[:, :, 0:3], XBCp[:, :, MT:MT + 3])
        for ci in range(2):
            proj_F(c, Wm, hnT, n, ci * 128, ZB[:, ci, 0:n], "act")
        for ci in range(6):
            proj_F(c, Wm, hnT, n, 256 + ci * 128, XBC[:, ci, 3:3 + n], "act" if ci % 2 else "dve")
        for ci in range(6):
            conv4(c, "pool" if ci % 2 else "dve", ACC[:, 0:n], XBC[:, ci, :], n, "ssd_cw", "ssd_cb", ci)
            k.act(XCb[:, ci, 0:n], ACC[:, 0:n], AF.Silu)
        if ti == 0:
            k.ms("pool", XCb[:, 0:2, 0:PAD], 0.0)
        for ci in range(nch):
            co = ci * 128
            q2 = ci % 2
            D4, TA, T1, LT, EC = D4_[q2], TA_[q2], T1_[q2], LT_[q2], EC_[q2]
            XDT, XDC, BTt, MSK, CDC = XDT_[q2], XDC_[q2], BTt_[q2], MSK_[q2], CDC_[q2]
            pdt = c.bank()
            for kc in range(8):
                k.mm(pdt[:, 0:4], hnT[:, kc, co:co + 128], Wm[:, kc, 1024:1028], start=(kc == 0), stop=(kc == 7))
            k.tt("dve", D4[:, 0, :], pdt[:, 0:4], c.par("ssd_dtb", 0, 4), ALU.add)
            k.act(D4[:, 0, :], D4[:, 0, :], AF.Exp)
            k.act(D4[:, 1, :], D4[:, 0, :], AF.Ln, bias=c.ONE_AP)
            k.tt("dve", D4[:, 2, :], D4[:, 1, :], c.par("d_aneg", 0, 4), ALU.mult)
            pcs = c.bank()
            k.mm(pcs[:, 0:4], triu, D4[:, 2, :])
            k.ts("dve", D4[:, 3, :], pcs[:, 0:4], -1.0)
            k.tt("dve", TA, triu.unsqueeze(1).to_broadcast([128, 4, 128]), D4[:, 2, :].unsqueeze(2).to_broadcast([128, 4, 128]), ALU.mult)
            pcb = c.bank()
            k.mm(pcb[:, 0:512], c.ones_f, TA.rearrange("p a b -> p (a b)"))
            pcb3 = pcb[:, 0:512].rearrange("p (a b) -> p a b", a=4)
            k.tt("dve", T1, pcb3, c.cst("maskneg").unsqueeze(1).to_broadcast([128, 4, 128]), ALU.add)
            k.tt("dve", T1, T1, D4[:, 3, :].unsqueeze(2).to_broadcast([128, 4, 128]), ALU.add)
            k.act(LT, T1, AF.Exp)
            k.act(EC, pcb3, AF.Exp)
            k.tt("dve", D4[:, 7, :], D4[:, 3, :], pcb3[:, :, 127], ALU.add)
            k.act(D4[:, 4, :], D4[:, 7, :], AF.Exp)
            k.act(D4[:, 6, :], pcb3[:, :, 127], AF.Exp)
            k.tt("dve", D4[:, 5, :], D4[:, 1, :], D4[:, 4, :], ALU.mult)
            ptr = c.bank()
            ptb = ptr[:, 0:256].bitcast(BF16)
            for j in range(4):
                k.tr(ptb[:, j * 128:(j + 1) * 128], XCb[:, j, co:co + 128], c.ident_b)
            xT = ptb[:, 0:256].rearrange("p (a b) -> p a b", a=4)
            k.tt("dve", XDT, xT, D4[:, 1, :].unsqueeze(2).to_broadcast([128, 4, 64]), ALU.mult)
            k.tt("dve", XDC, xT, D4[:, 5, :].unsqueeze(2).to_broadcast([128, 4, 64]), ALU.mult)
            k.cp("act", BTt.rearrange("p a b -> p (a b)"), ptb[:, 256:512])
            psc = c.bank()
            for g in range(2):
                k.mm(psc[:, g * 128:(g + 1) * 128], XCb[:, 2 + g, co:co + 128], XCb[:, 4 + g, co:co + 128])
            sc4 = psc[:, 0:256].rearrange("p (a b) -> p a b", a=2).unsqueeze(2).to_broadcast([128, 2, 2, 128])
            k.tt("dve", MSK.rearrange("p (a b) c -> p a b c", a=2), sc4, LT.rearrange("p (a b) c -> p a b c", a=2), ALU.mult)
            c4 = XCb[:, 4:6, co:co + 128].unsqueeze(2).to_broadcast([128, 2, 2, 128])
            k.tt("pool", CDC.rearrange("p (a b) c -> p a b c", a=2), c4, EC.rearrange("p (a b) c -> p a b c", a=2), ALU.mult)
            py = c.bank()
            for hh in range(4):
                pc, po = hh // 2, 64 * (hh % 2)
                o = py[po:po + 64, pc * 128:(pc + 1) * 128]
                k.mm(o, XDT[:, hh, :], MSK[:, hh, :], start=True, stop=False)
                k.mm(o, Sb[:, hh, :], CDC[:, hh, :], start=False, stop=True)
            k.cp("act", YF[:, :, co:co + 128], py[:, 0:256].rearrange("p (a b) -> p a b", a=2))
            pst = c.bank()
            for hh in range(4):
                k.mm(pst[:, hh * 64:(hh + 1) * 64], BTt[:, hh // 2, :], XDC[:, hh, :])
            k.tt("dve", TS_, S, D4[:, 6, :].unsqueeze(2).to_broadcast([128, 4, 64]), ALU.mult)
            k.tt("dve", S, TS_, pst[:, 0:256].rearrange("p (a b) -> p a b", a=4), ALU.add)
            k.cp("act", Sb, S)
        for ci2 in range(2):
            y = YF[:, ci2, 0:n]
            k.stt("dve", y, XCb[:, ci2, 0:n], c.par("ssd_d", ci2), y, ALU.mult, ALU.add)
            k.act(ZS[:, 0:n], ZB[:, ci2, 0:n], AF.Silu)
            k.tt("dve", y, y, ZS[:, 0:n], ALU.mult)
            k.tt("pool", SQ[:, 0:n], y, y, ALU.mult)
            pb = c.bank()
            k.mm(pb[:, 0:n], c.ones_b, SQ[:, 0:n])
            k.act(RS[:, 0:n], pb[:, 0:n], AF.Ln, bias=c.EPS_AP, scale=1.0 / 128)
            k.act(RS[:, 0:n], RS[:, 0:n], AF.Exp, scale=-0.5)
            k.stt("dve", c.yT[:, ci2, t0:t1], y, c.par("ssd_nw", ci2), RS[:, 0:n], ALU.mult, ALU.mult)


def rwkv_pass(c, l, Wm, hnT, first):
    k, ar = c.k, c.ar
    if os.environ.get("K_RWSPLIT", "0") == "0":
        c.bank_mode[0] = "all"
    load_win(c, l, Wm, [(1028, 2052)])
    NC2 = MT // 128
    P8 = ar.f32(8, 1 + MT)
    HIST = ar.f32(8, 1)
    TMPS = ar.f32(2, MT)
    TW = ar.bf16(MT)
    SG = ar.bf16(MT)
    Vb = ar.bf16(2, MT)
    AV = ar.f32(1, MT)
    GG = ar.bf16(2, MT)
    KKn = ar.f32(1, MT)
    KMD = ar.f32(1, MT)
    BON = ar.f32(2, MT)
    SGW = ar.f32(MT)
    CUM = ar.f32(MT)
    PE_ = ar.f32(MT)
    TT = ar.f32(MT)
    PM = ar.f32(2, MT)
    SQb = c.sqbuf[1]
    AR = ar.bf16(2, NC2, 2, 128)
    Bt = ar.bf16(2, MT)
    Kt = ar.bf16(2, MT)
    TL_ = [ar.bf16(6, 128), ar.bf16(6, 128)]
    Sx_ = [[ar.bf16(4, 128), ar.bf16(4, 128)] for _ in range(2)]
    STx_ = [[ar.bf16(4, 128), ar.bf16(4, 128)] for _ in range(2)]
    PTx_ = [[ar.bf16(4, 128), ar.bf16(4, 128)] for _ in range(2)]
    MRB_ = [ar.bf16(4, 128), ar.bf16(4, 128)]
    AAK_ = [ar.bf16(4, 128), ar.bf16(4, 128)]
    MRK_ = [ar.bf16(4, 128), ar.bf16(4, 128)]
    XZ_ = [ar.bf16(256), ar.bf16(256)]
    U_ = [ar.bf16(256), ar.bf16(256)]
    H = ar.f32(2, 64)
    H0p = ar.f32(2, 64)
    Hb = ar.bf16(2, 64)
    YF = ar.f32(2, MT)
    k.ms("dve", H, 0.0)
    k.ms("dve", Hb, 0.0)
    msl, msu, miu = c.cst("msl"), c.cst("msu"), c.cst("miu")
    identb4 = c.ident_b.unsqueeze(1).to_broadcast([128, 4, 128])
    SMw = c.SM
    for ti, (t0, t1) in enumerate(c.mtiles):
        n = t1 - t0
        nch = n // 128
        hn_tile(c, l, t0, n, hnT, first)
        if ti == 0:
            k.ms("pool", P8[:, :, 0:1], 0.0)
        else:
            k.cp("pool", P8[:, :, 0:1], HIST)
        for ci in range(8):
            proj_F(c, Wm, hnT, n, ci * 128, P8[:, ci, 1:1 + n], "act" if ci % 2 else "dve")
        k.cp("pool", HIST, P8[:, :, n:n + 1])
        MX = P8[:, :, 1:1 + MT]
        for g4 in range(4):
            sl = slice(g4 * 2, g4 * 2 + 2)
            k.tt("dve", TMPS[:, :, 0:n], P8[:, sl, 0:n], P8[:, sl, 1:1 + n], ALU.subtract)
            k.tt("pool", TMPS[:, :, 0:n], TMPS[:, :, 0:n], c.par("rw_mu", g4 * 2, 2).unsqueeze(2).to_broadcast([128, 2, n]), ALU.mult)
            k.tt("dve", P8[:, sl, 1:1 + n], TMPS[:, :, 0:n], P8[:, sl, 1:1 + n], ALU.add)
        k.act(TW[0:64, 0:n], MX[0:64, 6, 0:n], AF.Tanh)
        k.cp("act", TW[64:128, 0:n], MX[64:128, 6, 0:n])
        k.act(SG[:, 0:n], MX[:, 7, 0:n], AF.Sigmoid)
        for ci in range(2):
            r_, k_, v_ = MX[:, ci, 0:n], MX[:, 2 + ci, 0:n], MX[:, 4 + ci, 0:n]
            k.cp("pool", Vb[:, ci, 0:n], v_)
            pw = c.bank()
            k.mm(pw[:, 0:n], SMw[0:64, ci * 128:(ci + 1) * 128], TW[0:64, 0:n])
            pa = c.bank()
            k.mm(pa[:, 0:n], SMw[64:128, ci * 128:(ci + 1) * 128], TW[64:128, 0:n])
            pg = c.bank()
            k.mm(pg[:, 0:n], SMw[:, 256 + ci * 128:256 + (ci + 1) * 128], SG[:, 0:n])
            k.act(SGW[:, 0:n], pw[:, 0:n], AF.Sigmoid, bias=c.par("rw_w0", ci))
            k.act(AV[:, 0, 0:n], pa[:, 0:n], AF.Sigmoid, bias=c.par("rw_a0", ci))
            k.cp("act", GG[:, ci, 0:n], pg[:, 0:n])
            kkn = KKn[:, 0, 0:n]
            k.ts("dve", kkn, k_, c.par("rw_kk", ci))
            k.tt("pool", SQb[:, 0:n], kkn, kkn, ALU.mult)
            pss = c.bank()
            k.mm(pss[:, 0:n], c.blk64_b, SQb[:, 0:n])
            k.ts("dve", TT[:, 0:n], pss[:, 0:n], 1e-24, None, ALU.max)
            k.act(TT[:, 0:n], TT[:, 0:n], AF.Ln)
            k.act(TT[:, 0:n], TT[:, 0:n], AF.Exp, scale=-0.5)
            k.tt("dve", kkn, kkn, TT[:, 0:n], ALU.mult)
            kmd = KMD[:, 0, 0:n]
            k.ts("dve", TT[:, 0:n], AV[:, 0, 0:n], c.par("rw_ka", ci), c.par("d_omka", ci), ALU.mult, ALU.add)
            k.tt("dve", kmd, k_, TT[:, 0:n], ALU.mult)
            k.stt("dve", SQb[:, 0:n], r_, c.par("rw_rk", ci), kmd, ALU.mult, ALU.mult)
            pbn = c.bank()
            k.mm(pbn[:, 0:n], c.blk64_b, SQb[:, 0:n])
            k.tt("dve", BON[:, ci, 0:n], pbn[:, 0:n], v_, ALU.mult)
            k.scan("dve", CUM[:, 0:n], c.cst("reset", 0, n), SGW[:, 0:n], 0.0)
            k.act(PE_[:, 0:n], CUM[:, 0:n], AF.Exp, scale=C_W)
            k.act(PM[:, ci, 0:n], CUM[:, 0:n], AF.Exp, scale=-C_W)
            k.tt("pool", TT[:, 0:n], CUM[:, 0:n], SGW[:, 0:n], ALU.subtract)
            k.act(TT[:, 0:n], TT[:, 0:n], AF.Exp, scale=-C_W)
            v3 = lambda a: a.rearrange("p (a b) -> p a b", a=nch)
            k.stt("dve", AR[:, ci, 0:nch, 0, :], v3(kkn), -1.0, v3(TT[:, 0:n]), ALU.mult, ALU.mult)
            k.tt("dve", AR[:, ci, 0:nch, 1, :], v3(r_), v3(PM[:, ci, 0:n]), ALU.mult)
            k.tt("pool", TT[:, 0:n], kkn, AV[:, 0, 0:n], ALU.mult)
            k.tt("dve", Bt[:, ci, 0:n], TT[:, 0:n], PE_[:, 0:n], ALU.mult)
            k.tt("dve", Kt[:, ci, 0:n], kmd, PE_[:, 0:n], ALU.mult)
        for ci in range(nch):
            co = ci * 128
            par2 = ci % 2
            TL, Sx, STx, PTx = TL_[par2], Sx_[par2], STx_[par2], PTx_[par2]
            MRB, AAK, MRK, XZ, U = MRB_[par2], AAK_[par2], MRK_[par2], XZ_[par2], U_[par2]
            ptr = c.bank()
            ptb = ptr[:, 0:384].bitcast(BF16)
            for j, src in enumerate((Vb, Bt, Kt)):
                for cc in range(2):
                    k.tr(ptb[:, (2 * j + cc) * 128:(2 * j + cc + 1) * 128], src[:, cc, co:co + 128], c.ident_b)
            k.cp("act", TL.rearrange("p a b -> p (a b)"), ptb)
            Vh = lambda hh: TL[:, hh // 2, 64 * (hh % 2):64 * (hh % 2) + 64]
            Bh = lambda hh: TL[:, 2 + hh // 2, 64 * (hh % 2):64 * (hh % 2) + 64]
            Kh = lambda hh: TL[:, 4 + hh // 2, 64 * (hh % 2):64 * (hh % 2) + 64]
            pA = c.bank(2)
            pB = c.bank(2)
            pC = c.bank(2)
            for hh in range(4):
                pc, po = hh // 2, 64 * (hh % 2)
                q, j = hh % 2, hh // 2
                At = AR[po:po + 64, pc, ci, 0, :]
                ARf = AR[po:po + 64, pc, ci, :, :].rearrange("p a b -> p (a b)")
                Bth = Bt[po:po + 64, pc, co:co + 128]
                Kth = Kt[po:po + 64, pc, co:co + 128]
                k.mm(pA[:, q * 512 + j * 128:q * 512 + (j + 1) * 128], At, Bth)
                k.mm(pB[:, q * 512 + j * 256:q * 512 + (j + 1) * 256], Bth, ARf)
                k.mm(pC[:, q * 512 + j * 256:q * 512 + (j + 1) * 256], Kth, ARf)
            S, ST, PT = Sx[0], STx[0], PTx[0]
            pA4 = pA.rearrange("p (q j b) -> p q j b", q=2, j=4)[:, :, 0:2, :]
            pB4 = pB.rearrange("p (a b c) -> p a b c", a=4, b=2)
            pC4 = pC.rearrange("p (a b c) -> p a b c", a=4, b=2)
            m3 = lambda m: m.unsqueeze(1).to_broadcast([128, 4, 128])
            m22 = lambda m: m.unsqueeze(1).unsqueeze(1).to_broadcast([128, 2, 2, 128])
            k.tt("dve", S.rearrange("p (q j) b -> p q j b", q=2), pA4, m22(msl), ALU.mult)
            k.tt("dve", ST, pB4[:, :, 0, :], m3(msu), ALU.mult)
            k.tt("dve", MRB, pB4[:, :, 1, :], m3(miu), ALU.mult)
            k.tt("dve", AAK, pC4[:, :, 0, :], m3(msu), ALU.mult)
            k.tt("dve", MRK, pC4[:, :, 1, :], m3(miu), ALU.mult)
            k.tt("pool", PT, ST, identb4, ALU.add)
            hp = lambda hh: (hh % 2) * 2 + hh // 2
            cur = 0
            for lev in range(6):
                nxt = 1 - cur
                S, ST, PT = Sx[cur], STx[cur], PTx[cur]
                Sn, STn, PTn = Sx[nxt], STx[nxt], PTx[nxt]
                pS = c.bank()
                for hh in range(4):
                    k.mm(pS[:, hp(hh) * 128:(hp(hh) + 1) * 128], ST[:, hp(hh), :], S[:, hp(hh), :])
                k.cp("act", Sn.rearrange("p a b -> p (a b)"), pS[:, 0:512])
                if lev < 5:
                    pT = c.bank()
                    for hh in range(4):
                        k.mm(pT[:, hp(hh) * 128:(hp(hh) + 1) * 128], S[:, hp(hh), :], ST[:, hp(hh), :])
                    k.cp("dve", STn.rearrange("p a b -> p (a b)"), pT[:, 0:512])
                pP = c.bank()
                for hh in range(4):
                    k.mm(pP[:, hp(hh) * 128:(hp(hh) + 1) * 128], Sn[:, hp(hh), :], PT[:, hp(hh), :])
                k.tt("dve", PTn.rearrange("p a b -> p (a b)"), pP[:, 0:512], PT.rearrange("p a b -> p (a b)"), ALU.add)
                cur = nxt
            PT = PTx[cur]
            pX = c.bank()
            for hh in range(4):
                pc, po = hh // 2, 64 * (hh % 2)
                o = pX[:, hh * 64:(hh + 1) * 64]
                k.mm(o, AAK[:, hp(hh), :], Vh(hh), start=True, stop=False)
                k.mm(o, AR[po:po + 64, pc, ci, 0, :], Hb[po:po + 64, pc, :], start=False, stop=True)
            k.cp("act", XZ, pX[:, 0:256])
            pU = c.bank()
            for hh in range(4):
                k.mm(pU[:, hh * 64:(hh + 1) * 64], PT[:, hp(hh), :], XZ[:, hh * 64:(hh + 1) * 64])
            k.cp("act", U, pU[:, 0:256])
            pY = c.bank()
            for hh in range(4):
                pc, po = hh // 2, 64 * (hh % 2)
                o = pY[po:po + 64, pc * 128:(pc + 1) * 128]
                k.mm(o, Hb[po:po + 64, pc, :], AR[po:po + 64, pc, ci, 1, :], start=True, stop=False)
                k.mm(o, U[:, hh * 64:(hh + 1) * 64], MRB[:, hp(hh), :], start=False, stop=False)
                k.mm(o, Vh(hh), MRK[:, hp(hh), :], start=False, stop=True)
            k.cp("act", YF[:, :, co:co + 128], pY[:, 0:256].rearrange("p (a b) -> p a b", a=2))
            pH = c.bank()
            for hh in range(4):
                pc, po = hh // 2, 64 * (hh % 2)
                o = pH[po:po + 64, pc * 64:(pc + 1) * 64]
                k.mm(o, Bh(hh), U[:, hh * 64:(hh + 1) * 64], start=True, stop=False)
                k.mm(o, Kh(hh), Vh(hh), start=False, stop=True)
            for pc in range(2):
                pl = PM[:, pc, co + 127:co + 128]
                k.ts("pool", H0p[:, pc, :], H[:, pc, :], pl)
                k.stt("dve", H[:, pc, :], pH[:, pc * 64:(pc + 1) * 64], pl, H0p[:, pc, :], ALU.mult, ALU.add)
            k.cp("act", Hb, H)
        for ci2 in range(2):
            y = YF[:, ci2, 0:n]
            head_norm_F(c, y, n, c.EPS_RW, TT[:, 0:n], CUM[:, 0:n])
            k.ts("dve", y, y, c.par("rw_lnw", ci2), c.par("rw_lnb", ci2), ALU.mult, ALU.add)
            k.tt("dve", y, y, BON[:, ci2, 0:n], ALU.add)
            k.tt("dve", c.yT[:, 2 + ci2, t0:t1], y, GG[:, ci2, 0:n], ALU.mult)


def post_norm_residual(c, O, tiles_local, t0g, wname, SQ2, RS, after_tile=None):
    k = c.k
    for (a, b) in tiles_local:
        n = b - a
        pb = c.bank()
        for m in range(8):
            sq = SQ2[m % 2]
            k.tt("pool" if m % 2 else "dve", sq[:, 0:n], O[:, m, a:b], O[:, m, a:b], ALU.mult)
            k.mm(pb[:, 0:n], c.ones_b, sq[:, 0:n], start=(m == 0), stop=(m == 7))
        k.act(RS[:, 0:n], pb[:, 0:n], AF.Ln, bias=c.EPS_AP, scale=1.0 / 1024)
        k.act(RS[:, 0:n], RS[:, 0:n], AF.Exp, scale=-0.5)
        lo = 0
        if t0g + a < PAD:
            lo = PAD - (t0g + a)
        for m in range(8):
            eng = "pool" if m % 2 else "dve"
            k.stt(eng, O[:, m, a + lo:b], O[:, m, a + lo:b], c.par(wname, m), RS[:, lo:n], ALU.mult, ALU.mult)
            k.tt(eng, c.h[:, m, t0g + a + lo:t0g + b], c.h[:, m, t0g + a + lo:t0g + b], O[:, m, a + lo:b], ALU.add)
        if after_tile is not None:
            after_tile(t0g + a, t0g + b)


def wout_phase(c, l):
    k, ar = c.k, c.ar
    c.bank_mode[0] = "all"
    ar.reset()
    Wo = ar.bf16(8, D)
    O_ = [ar.f32(8, 512), ar.f32(8, 512)]
    SQ2 = [ar.bf16(512), ar.bf16(512)]
    RS_ = [ar.f32(512), ar.f32(512)]
    for kc in range(8):
        k.dma_in("pool", Wo[:, kc, :], c.wout_d[l, :, kc, :])
    for ti, (t0, t1) in enumerate(tiles_of(0, T, 512)):
        n = t1 - t0
        O, RS = O_[ti % 2], RS_[ti % 2]
        for m in range(8):
            pb = c.bank()
            for kc in range(8):
                k.mm(pb[:, 0:n], Wo[:, kc, m * 128:(m + 1) * 128], c.yT[:, kc, t0:t1], start=(kc == 0), stop=(kc == 7))
            k.cp("act", O[:, m, 0:n], pb[:, 0:n])
        post_norm_residual(c, O, [(0, n)], t0, "post_mix", SQ2, RS,
                           after_tile=lambda ta, tb: c.rms_stats(lambda kc: c.h[:, kc, ta:tb], tb - ta, c.rstd_all[:, ta:tb], sqbuf=SQ2))


def ffn_phase(c, l):
    k, ar = c.k, c.ar
    ar.reset()
    GT = 768
    hn2x = [ar.bf16(8, GT), ar.bf16(8, GT)]
    Fo = ar.f32(8, GT)
    Wgu = [ar.bf16(2, 8, 128), ar.bf16(2, 8, 128)]
    Wd = [ar.bf16(NJ, 128), ar.bf16(NJ, 128)]
    SQ2 = [ar.bf16(512), ar.bf16(512)]
    RS = ar.f32(512)
    SIL = [ar.f32(512), ar.f32(512)]
    aT = c.yT[:, :, :].rearrange("p a b -> p (a b)")[:, 0:NJ * GT].rearrange("p (a b) -> p a b", a=NJ)
    groups = tiles_of(0, T, GT)

    def make_hn2(gi):
        g0, g1 = groups[gi]
        hn2 = hn2x[gi % 2]
        for (a, b) in tiles_of(0, g1 - g0, 512):
            for kc in range(8):
                k.stt("dve", hn2[:, kc, a:b], c.h[:, kc, g0 + a:g0 + b], c.par("pre_ffn", kc), c.rstd_all[:, g0 + a:g0 + b], ALU.mult, ALU.mult)

    make_hn2(0)
    for gi, (g0, g1) in enumerate(groups):
        ng = g1 - g0
        lt = tiles_of(0, ng, 512)
        hn2 = hn2x[gi % 2]
        for j in range(NJ):
            W = Wgu[j % 2]
            k.dma_in("pool", W.rearrange("p a b c -> p (a b c)"), c.wgu_d[l, j, :, :, :, :].rearrange("p a b c -> p (a b c)"))
            for (a, b) in lt:
                n = b - a
                pg = c.bank()
                pu = c.bank()
                for kc in range(8):
                    k.mm(pg[:, 0:n], W[:, 0, kc, :], hn2[:, kc, a:b], start=(kc == 0), stop=(kc == 7))
                for kc in range(8):
                    k.mm(pu[:, 0:n], W[:, 1, kc, :], hn2[:, kc, a:b], start=(kc == 0), stop=(kc == 7))
                sl = SIL[(j + (a > 0)) % 2]
                k.act(sl[:, 0:n], pg[:, 0:n], AF.Silu)
                k.tt("dve", aT[:, j, a:b], sl[:, 0:n], pu[:, 0:n], ALU.mult)
        if gi + 1 < len(groups):
            make_hn2(gi + 1)
        for m in range(8):
            W = Wd[m % 2]
            k.dma_in("pool", W.rearrange("p a b -> p (a b)"), c.wd_d[l, m, :, :, :].rearrange("p a b -> p (a b)"))
            for (a, b) in lt:
                n = b - a
                pb = c.bank()
                for j in range(NJ):
                    k.mm(pb[:, 0:n], W[:, j, :], aT[:, j, a:b], start=(j == 0), stop=(j == NJ - 1))
                k.cp("act", Fo[:, m, a:b], pb[:, 0:n])
        nxt = None
        if l + 1 < c.L and not os.environ.get("K_NOPRE"):
            nxt = lambda ta, tb: c.rms_stats(lambda kc: c.h[:, kc, ta:tb], tb - ta, c.rstd_all[:, ta:tb], sqbuf=SQ2)
        post_norm_residual(c, Fo, lt, g0, "post_ffn", SQ2, RS, after_tile=nxt)


def _cols(v):
    v = np.asarray(v, np.float32)
    return np.ascontiguousarray(v.reshape(-1, 128).T)


def make_consts():
    Cn = np.zeros((128, NCONST), np.float32)
    i = np.arange(128)

    def put(name, a):
        o, w = CO[name]
        Cn[:, o:o + w] = a
    put("ident", np.eye(128))
    put("ones", np.ones((128, 128)))
    put("blk64", np.kron(np.eye(2), np.ones((64, 64))))
    put("triu", (i[:, None] <= i[None, :]).astype(np.float32))
    put("maskneg", np.where(i[None, :] >= i[:, None], 0.0, -30000.0))
    put("msl", (i[:, None] > i[None, :]).astype(np.float32))
    put("msu", (i[:, None] < i[None, :]).astype(np.float32))
    put("miu", (i[:, None] <= i[None, :]).astype(np.float32))
    log_g = np.log1p(-np.exp2(-5.0 - np.arange(4, dtype=np.float64)))
    lt = np.zeros((128, 4, 128))
    for hh in range(4):
        rel = i[None, :] - i[:, None]
        lt[:, hh, :] = np.where(rel >= 0, np.exp(np.maximum(rel, 0) * log_g[hh]), 0.0)
    put("ltT", lt.reshape(128, 512))
    kd = np.zeros((128, 128))
    qd = np.zeros((128, 128))
    g128 = np.zeros((128, 1))
    hm = np.zeros((128, 4))
    for hh in range(4):
        kd[:, hh * 32:(hh + 1) * 32] = np.exp((127 - i) * log_g[hh])[:, None]
        qd[hh * 32:(hh + 1) * 32, :] = np.exp((i + 1) * log_g[hh])[None, :]
        g128[hh * 32:(hh + 1) * 32, 0] = np.exp(128 * log_g[hh])
        hm[hh * 32:(hh + 1) * 32, hh] = 1.0
    put("kdec", kd)
    put("qdec", qd)
    put("g128", g128)
    put("hm", hm)
    rs = np.ones((128, 256))
    rs[:, 0] = 0.0
    rs[:, 128] = 0.0
    put("reset", rs)
    half = 16
    freqs = 10000.0 ** (-np.arange(half, dtype=np.float64) / half)
    pos = np.arange(T, dtype=np.float64) - PAD
    ang = pos[None, :] * freqs[:, None]
    cos = np.concatenate([np.cos(ang), np.cos(ang)], 0)
    sin = np.concatenate([-np.sin(ang), np.sin(ang)], 0)
    cos = np.tile(cos, (4, 1))
    sin = np.tile(sin, (4, 1))
    sc = 32 ** -0.5
    rope = np.stack([cos, sin, cos * sc, sin * sc]).astype(np.float32)
    rope[:, :, :PAD] = 0.0
    return Cn, rope


def prep_weights(inp, L):
    g = lambda n: np.asarray(inp[n], np.float32)
    w_in = g("w_in")[:L]
    ret0 = 2564
    idx = np.arange(128)
    sw = (idx // 32) * 32 + ((idx % 32) + 16) % 32
    qsw = w_in[:, :, ret0 + sw]
    ksw = w_in[:, :, ret0 + 128 + sw]
    w_in_p = np.concatenate([w_in, qsw, ksw], axis=2)
    w_in_p = np.ascontiguousarray(w_in_p.reshape(L, 8, 128, WIN_COLS))
    w_out = np.ascontiguousarray(g("w_out")[:L].reshape(L, 8, 128, D).transpose(0, 2, 1, 3))
    wg = g("ffn_w_gate")[:L].reshape(L, 8, 128, NJ, 128)
    wu = g("ffn_w_up")[:L].reshape(L, 8, 128, NJ, 128)
    wgu = np.stack([wg, wu], axis=0)
    wgu = np.ascontiguousarray(wgu.transpose(1, 4, 3, 0, 2, 5))
    wd = g("ffn_w_down")[:L].reshape(L, NJ, 128, 8, 128)
    wd = np.ascontiguousarray(wd.transpose(0, 3, 2, 1, 4))
    params = np.zeros((L, 128, NPAR), np.float32)
    smat = np.zeros((L, 128, 1024), np.float32)
    for l in range(L):
        def put(name, a):
            o, w = PO[name]
            params[l, :, o:o + w] = a
        put("pre_mix", _cols(g("pre_mix_norm")[l]))
        put("post_mix", _cols(g("post_mix_norm")[l]))
        put("pre_ffn", _cols(g("pre_ffn_norm")[l]))
        put("post_ffn", _cols(g("post_ffn_norm")[l]))
        cw = g("ssd_conv_w")[l]
        put("ssd_cw", np.concatenate([np.stack([cw[j, ci * 128:(ci + 1) * 128] for j in range(4)], 1) for ci in range(6)], 1))
        put("ssd_cb", _cols(g("ssd_conv_b")[l]))
        put("ssd_dtb", np.tile(g("ssd_dt_bias")[l][None, :], (128, 1)))
        put("ssd_alog", np.tile(g("ssd_a_log")[l][None, :], (128, 1)))
        put("ssd_d", _cols(np.repeat(g("ssd_d")[l], 64)))
        put("ssd_nw", _cols(g("ssd_norm_w")[l]))
        put("rw_mu", _cols(g("rwkv_mu")[l]))
        put("rw_w0", _cols(g("rwkv_w0")[l]))
        put("rw_a0", _cols(g("rwkv_a0")[l]))
        put("rw_kk", _cols(g("rwkv_k_k")[l]))
        put("rw_ka", _cols(g("rwkv_k_a")[l]))
        put("rw_rk", _cols(g("rwkv_r_k")[l].reshape(-1)))
        put("rw_lnw", _cols(g("rwkv_ln_w")[l]))
        put("rw_lnb", _cols(g("rwkv_ln_b")[l]))
        lw = g("lru_conv_w")[l]
        put("lru_cw", np.concatenate([np.stack([lw[j, ci * 128:(ci + 1) * 128] for j in range(4)], 1) for ci in range(2)], 1))
        put("lru_cb", _cols(g("lru_conv_b")[l]))
        put("lru_ba", _cols(g("lru_ba")[l]))
        put("lru_bx", _cols(g("lru_bx")[l]))
        put("lru_lam", _cols(g("lru_lambda")[l]))
        put("ret_gnw", _cols(g("ret_gn_w")[l]))
        smat[l, 0:64, 0:256] = g("rwkv_w2")[l]
        smat[l, 64:128, 0:256] = g("rwkv_a2")[l]
        smat[l, :, 256:512] = g("rwkv_g2")[l]
        for nm, off in (("lru_wa", 512), ("lru_wx", 768)):
            w = g(nm)[l]
            for b in range(4):
                ci, po = b // 2, 64 * (b % 2)
                smat[l, po:po + 64, off + ci * 128 + po:off + ci * 128 + po + 64] = w[b]
    return dict(w_in=w_in_p, w_out=w_out, w_gu=wgu, w_d=wd, params=params, smat=smat)


_CACHE = {}


def kernel(**inputs):
    x = np.asarray(inputs["x"], np.float32)
    B = x.shape[0]
    Cn, rope = make_consts()
    wts = prep_weights(inputs, DEPTH)
    if "nc" not in _CACHE:
        st = os.environ.get("K_STAGES")
        _CACHE["nc"] = build(DEPTH) if st is None else build(DEPTH, stages=tuple(x for x in st.split(",") if x))
    nc = _CACHE["nc"]
    meta = np.ascontiguousarray(np.asarray(inputs["meta_tokens"], np.float32))
    in_maps = []
    for b in range(B):
        m = dict(x=np.ascontiguousarray(x[b]), meta=meta, consts=Cn, rope=rope)
        m.update(wts)
        in_maps.append({k_: v_ for k_, v_ in m.items() if k_ in nc._in_names})
    res = run_bass_kernel_spmd(nc, in_maps, core_ids=list(range(B)))
    return np.stack([np.asarray(r["out"], np.float32) for r in res.results], axis=0)
```

```python
import math
import os
import numpy as np
import ml_dtypes
import concourse.bass as bass
import concourse.mybir as mybir
from concourse.bass_utils import run_bass_kernel_spmd

F32 = mybir.dt.float32
BF16 = mybir.dt.bfloat16
AF = mybir.ActivationFunctionType
ALU = mybir.AluOpType

D = 1024
SEQ = 2048
DEPTH = 4
NMETA = 16
T = 2176
NCH = 17
PAD = 112
MT = 256
DFF = 2816
NJ = 22
EPS = 1e-6
C_W = 0.6065306597126334

CO = {}
_o = 0
for _n, _w in [("ident", 128), ("ones", 128), ("blk64", 128), ("triu", 128), ("maskneg", 128),
               ("msl", 128), ("msu", 128), ("miu", 128), ("ltT", 512), ("kdec", 128), ("qdec", 128),
               ("g128", 1), ("hm", 4), ("reset", 256)]:
    CO[_n] = (_o, _w)
    _o += _w
NCONST = ((_o + 63) // 64) * 64

PO = {}
_o = 0
for _n, _w in [("pre_mix", 8), ("post_mix", 8), ("pre_ffn", 8), ("post_ffn", 8),
               ("ssd_cw", 24), ("ssd_cb", 6), ("ssd_dtb", 4), ("ssd_alog", 4), ("ssd_d", 2), ("ssd_nw", 2),
               ("rw_mu", 8), ("rw_w0", 2), ("rw_a0", 2), ("rw_kk", 2), ("rw_ka", 2), ("rw_rk", 2),
               ("rw_lnw", 2), ("rw_lnb", 2),
               ("lru_cw", 8), ("lru_cb", 2), ("lru_ba", 2), ("lru_bx", 2), ("lru_lam", 2), ("ret_gnw", 2),
               ("d_aneg", 4), ("d_omka", 2), ("d_m8sp", 2), ("d_tmp", 4)]:
    PO[_n] = (_o, _w)
    _o += _w
NPAR = ((_o + 15) // 16) * 16

WIN_COLS = 3588


def _isz(dt):
    return 2 if dt == BF16 else 4


class Prog:
    ROT = 6000

    def __init__(self, nc):
        self.nc = nc
        self.ops = []
        self.track = {}

    @staticmethod
    def box(ap):
        isz = _isz(ap.dtype)
        pat = ap.ap
        pstride = pat[0][0] * isz
        offb = ap.offset * isz
        if pstride <= 0:
            p0, f0 = 0, offb
        else:
            p0, f0 = offb // pstride, offb % pstride
        ext = 1
        for st, cnt in pat[1:]:
            ext += (cnt - 1) * abs(st)
        nm = ap.tensor.name
        p1, b0, b1 = p0 + pat[0][1], f0, f0 + ext * isz
        if nm == "PS":
            p0, p1 = 0, 128
            b0 = (b0 // 2048) * 2048
            b1 = ((b1 + 2047) // 2048) * 2048
        return (nm, p0, p1, b0, b1)

    def add(self, eng, fn, reads, writes, kind="c", cost=300.0, tag=None):
        i = len(self.ops)
        if isinstance(eng, (list, tuple)):
            cands = list(eng)
            fns, costs = fn, cost
            eng = cands[0]
        else:
            cands, fns, costs = [eng], {eng: fn}, {eng: cost}
        deps = set()
        for ap in reads:
            nm, p0, p1, b0, b1 = self.box(ap)
            lst = self.track.setdefault(nm, [])
            for (j, q0, q1, c0, c1, w, e) in lst:
                if (w or nm == "PS") and q0 < p1 and p0 < q1 and c0 < b1 and b0 < c1:
                    deps.add(j)
        for ap in writes:
            nm, p0, p1, b0, b1 = self.box(ap)
            lst = self.track.setdefault(nm, [])
            nowar = os.environ.get("K_NOWAR") and nm == "arena"
            for (j, q0, q1, c0, c1, w, e) in lst:
                if q0 < p1 and p0 < q1 and c0 < b1 and b0 < c1 and not (nowar):
                    deps.add(j)
        for ap in reads:
            nm, p0, p1, b0, b1 = self.box(ap)
            if nm in ("Cc", "Cb", "eps_t"):
                continue
            self.track[nm].append((i, p0, p1, b0, b1, False, eng))
        for ap in writes:
            nm, p0, p1, b0, b1 = self.box(ap)
            lst = self.track[nm]
            lst[:] = [x for x in lst if not (p0 <= x[1] and x[2] <= p1 and b0 <= x[3] and x[4] <= b1)]
            lst.append((i, p0, p1, b0, b1, True, eng))
        deps.discard(i)
        self.ops.append(dict(eng=eng, fn=fns[eng], deps=deps, kind=kind, cost=costs[eng], cands=cands, fns=fns, costs=costs, tag=tag))
        return i

    def schedule(self):
        ops = self.ops
        n = len(ops)
        succ = [[] for _ in range(n)]
        indeg = [0] * n
        for i, o in enumerate(ops):
            indeg[i] = len(o["deps"])
            for j in o["deps"]:
                succ[j].append(i)
        fin = [0.0] * n
        engs = sorted(set(e for o in ops for e in o["cands"]))
        free = {e: 0.0 for e in engs}
        order = {e: [] for e in engs}
        ready = {}
        pend = []

        def release(i):
            o = ops[i]
            r = {}
            for e in o["cands"]:
                t = 0.0
                for j in o["deps"]:
                    tj = fin[j] + (60.0 if ops[j]["eng"] == e else 200.0)
                    if tj > t:
                        t = tj
                r[e] = t
            ready[i] = r
            pend.append(i)
        for i in range(n):
            if indeg[i] == 0:
                release(i)
        done = 0
        WIN = int(os.environ.get("K_WIN", "600"))
        lo = 0
        sched = [False] * n
        cur_tab = [None]
        TABLD = 1283.0
        while done < n:
            best = None
            for i in pend:
                if i > lo + WIN:
                    continue
                o = ops[i]
                r = ready[i]
                be = None
                for e in o["cands"]:
                    st = r[e] if r[e] > free[e] else free[e]
                    if e == "act" and o["tag"] is not None and o["tag"] != cur_tab[0]:
                        st += TABLD
                    f = st + o["costs"][e]
                    if be is None or f < be[0]:
                        be = (f, st, e)
                key = (be[1], i)
                if best is None or key < best[0]:
                    best = (key, i, be[2], be[1])
            if best is None:
                i = min(pend)
                o = ops[i]
                e = o["cands"][0]
                st = max(free[e], ready[i][e])
            else:
                _, i, e, st = best
                o = ops[i]
            pend.remove(i)
            del ready[i]
            o["eng"] = e
            o["fn"] = o["fns"][e]
            o["cost"] = o["costs"][e]
            if e == "act" and o["tag"] is not None:
                cur_tab[0] = o["tag"]
            f = st + o["cost"]
            free[e] = st + (o["cost"] if o["kind"] != "dma" else 60.0)
            fin[i] = f
            order[e].append(i)
            sched[i] = True
            done += 1
            while lo < n and sched[lo]:
                lo += 1
            for k2 in succ[i]:
                indeg[k2] -= 1
                if indeg[k2] == 0:
                    release(k2)
        self.est_ns = max(fin) if n else 0.0
        return order

    def plan(self, out_dma_ops):
        class _S:
            def __init__(self, i):
                self.idx = i
        cnt = [0]
        def gen():
            while True:
                cnt[0] += 1
                yield _S(cnt[0] - 1)
        self.order = self.schedule()
        self._plan = self._assign(gen(), out_dma_ops)
        return cnt[0]

    def emit(self, block, sems, out_dma_ops):
        red, sv, prevdma = self._plan
        ops = self.ops
        rs = lambda t: None if t is None else (sems[t[0].idx], t[1])
        sv = [rs(t) for t in sv]
        prevdma = [rs(t) for t in prevdma]
        self._emit(block, red, sv, prevdma, out_dma_ops)

    def _assign(self, si, out_dma_ops):
        ops = self.ops
        n = len(ops)
        need_inc = [False] * n
        pos = [0] * n
        for e, lst in self.order.items():
            for p_, i in enumerate(lst):
                pos[i] = p_
        red = []
        for i, o in enumerate(ops):
            best = {}
            dl = []
            for j in o["deps"]:
                oj = ops[j]
                if oj["kind"] == "dma":
                    dl.append(j)
                else:
                    if oj["eng"] == "pe" and o["eng"] == "pe" and o["kind"] == "c":
                        assert pos[j] < pos[i]
                        continue
                    b = best.get(oj["eng"])
                    if b is None or pos[b] < pos[j]:
                        best[oj["eng"]] = j
            dl += list(best.values())
            for j in dl:
                need_inc[j] = True
            red.append(dl)
        for j in out_dma_ops:
            need_inc[j] = True
        eng_sems = {}
        cnt = {}
        dq = {}
        dcnt = {}
        NDQ = 8
        sv = [None] * n
        prevdma = [None] * n
        seq = [i for e in sorted(self.order) for i in self.order[e]]
        for i in seq:
            o = ops[i]
            e = o["eng"]
            if o["kind"] == "dma":
                if e not in dq:
                    dq[e] = [next(si) for _ in range(NDQ)]
                    dcnt[e] = [0] * NDQ
                    cnt[("d", e)] = 0
                k = cnt[("d", e)] % NDQ
                cnt[("d", e)] += 1
                if dcnt[e][k] >= 16 * 1500:
                    dq[e][k] = next(si)
                    dcnt[e][k] = 0
                prevdma[i] = (dq[e][k], dcnt[e][k])
                dcnt[e][k] += 16
                sv[i] = (dq[e][k], dcnt[e][k])
            elif need_inc[i]:
                c = cnt.get(e, 0)
                if c % self.ROT == 0:
                    eng_sems[e] = next(si)
                cnt[e] = c + 1
                sv[i] = (eng_sems[e], c % self.ROT + 1)
        return red, sv, prevdma

    def _emit(self, block, red, sv, prevdma, out_dma_ops):
        ops = self.ops
        engmap = {"pe": block.tensor, "act": block.scalar, "dve": block.vector, "pool": block.gpsimd,
                  "sp": block.sync}
        for ename, deco in engmap.items():
            def body(eng, ename=ename):
                waited = {}
                for i in self.order.get(ename, []):
                    o = ops[i]
                    for j in red[i]:
                        s, v = sv[j]
                        if waited.get(s.num if hasattr(s, "num") else id(s), 0) >= v:
                            continue
                        eng.wait_ge(s, v)
                        waited[s.num if hasattr(s, "num") else id(s)] = v
                    if o["kind"] == "dma":
                        ps, pv = prevdma[i]
                        key = ps.num if hasattr(ps, "num") else id(ps)
                        if pv > 0 and waited.get(key, 0) < pv:
                            eng.wait_ge(ps, pv)
                            waited[key] = pv
                        o["fn"](eng).then_inc(sv[i][0], 16)
                    else:
                        ins = o["fn"](eng)
                        if sv[i] is not None:
                            ins.then_inc(sv[i][0], 1)
                if ename == "sp":
                    for j in out_dma_ops:
                        s, v = sv[j]
                        eng.wait_ge(s, v)
            deco(body)


def _fn(ap):
    n = 1
    for d in ap.shape[1:]:
        n *= d
    return n


def _ec(eng, out, ins, kind="tt"):
    n = _fn(out)
    if eng == "act":
        return 225.0 + n * 0.85
    if eng == "dve":
        return 150.0 + n * 1.15
    if kind == "ts":
        return 1100.0 + n * 1.2
    if kind == "cp":
        return 220.0 + n * 1.0
    return 330.0 + n * 1.95


class K:
    def __init__(self, nc, prog):
        self.nc = nc
        self.p = prog

    def mm(self, out, lhsT, rhs, start=True, stop=True):
        rd = [lhsT, rhs] + ([] if start else [out])
        nn = max(_fn(rhs), 64) * (4 if _isz(rhs.dtype) == 4 else 1)
        self.p.add("pe", lambda e: e.matmul(out, lhsT, rhs, start=start, stop=stop), rd, [out], cost=57.0 + nn / 2.4)

    def tr(self, out, in_, ident):
        self.p.add("pe", lambda e: e.transpose(out, in_, ident), [in_, ident], [out], cost=90.0 * (2 if _isz(in_.dtype) == 4 else 1))

    def act(self, out, in_, func, bias=None, scale=None, eng="act"):
        rd = [in_]
        kw = {}
        if bias is not None:
            kw["bias"] = bias
            if not isinstance(bias, float):
                rd.append(bias)
        if scale is not None:
            kw["scale"] = scale
            if not isinstance(scale, float):
                rd.append(scale)
        tag = None if func in (AF.Copy, AF.Identity) else ("explog" if func in (AF.Exp, AF.Ln) else str(func))
        self.p.add("act", lambda e: e.activation(out=out, in_=in_, func=func, **kw), rd, [out], cost=_ec("act", out, [in_]), tag=tag)

    @staticmethod
    def _cands(out, ins, act_ok=False):
        ps = any(a.tensor.name == "PS" for a in list(ins) + [out])
        c = ["dve"] if (ps or not os.environ.get("K_POOLCOMPUTE")) else ["dve", "pool"]
        if act_ok:
            c.append("act")
        return c

    def _flex(self, cands, mk, out, ins, reads, kind="tt"):
        fns = {e: mk(e) for e in cands}
        costs = {e: _ec(e, out, ins, kind) for e in cands}
        self.p.add(cands, fns, reads, [out], cost=costs)

    def tt(self, eng, out, in0, in1, op):
        mk = lambda en: (lambda e: e.tensor_tensor(out=out, in0=in0, in1=in1, op=op))
        self._flex(self._cands(out, [in0, in1]), mk, out, [in0, in1], [in0, in1])

    def ts(self, eng, out, in0, s1, s2=None, op0=ALU.mult, op1=None):
        rd = [in0] + [s for s in (s1, s2) if s is not None and not isinstance(s, float)]
        if op1 is None:
            mk = lambda en: (lambda e: e.tensor_scalar(out=out, in0=in0, scalar1=s1, scalar2=None, op0=op0))
        else:
            mk = lambda en: (lambda e: e.tensor_scalar(out=out, in0=in0, scalar1=s1, scalar2=s2, op0=op0, op1=op1))
        self._flex(self._cands(out, rd), mk, out, [in0], rd, kind="ts")

    def stt(self, eng, out, in0, scalar, in1, op0, op1):
        eng = "dve"
        rd = [in0, in1] + ([] if isinstance(scalar, float) else [scalar])
        self.p.add(eng, lambda e: e.scalar_tensor_tensor(out=out, in0=in0, scalar=scalar, in1=in1, op0=op0, op1=op1), rd, [out], cost=_ec(eng, out, [in0, in1]))

    def cp(self, eng, out, in_):
        def mk(en):
            if en == "act":
                return lambda e: e.activation(out=out, in_=in_, func=AF.Copy)
            return lambda e: e.tensor_copy(out=out, in_=in_)
        self._flex(self._cands(out, [in_], act_ok=True), mk, out, [in_], [in_], kind="cp")

    def ms(self, eng, ap, val):
        mk = lambda en: (lambda e: e.memset(ap, val))
        self._flex(self._cands(ap, []), mk, ap, [], [], kind="cp")

    def scan(self, eng, out, d0, d1, init):
        eng = "dve"
        rd = [d0, d1] + ([] if isinstance(init, float) else [init])
        self.p.add(eng, lambda e: e.tensor_tensor_scan(out=out, data0=d0, data1=d1, initial=init, op0=ALU.mult, op1=ALU.add), rd, [out], cost=100.0 + _fn(out) / 0.5)

    def dma_in(self, q, out, in_):
        return self.p.add(q, lambda e: e.dma_start(out=out, in_=in_), [], [out], kind="dma", cost=2500.0 + _fn(out) * 128 * 4 / 320.0)

    def dma_out(self, q, out, in_):
        return self.p.add(q, lambda e: e.dma_start(out=out, in_=in_), [in_], [], kind="dma", cost=2500.0 + _fn(in_) * 128 * 4 / 320.0)


class Arena:
    def __init__(self, ap_f32, nwords):
        self.ap = ap_f32
        self.n = nwords
        self.off = 0

    def reset(self, off=0):
        if os.environ.get("K_ARDBG") and getattr(self, "hi", 0):
            print("   arena hi", self.hi, "of", self.n)
        self.hi = 0
        self.off = off

    def f32(self, *shape):
        n = int(np.prod(shape))
        self.hi = max(getattr(self, "hi", 0), self.off + n)
        assert self.off + n <= self.n, ("arena overflow", self.off, n, self.n)
        v = self.ap[:, self.off:self.off + n]
        self.off += n
        return self._shape(v, shape)

    def bf16(self, *shape):
        n = int(np.prod(shape))
        nw = (n + 1) // 2
        self.hi = max(getattr(self, "hi", 0), self.off + nw)
        assert self.off + nw <= self.n, ("arena overflow", self.off, nw, self.n)
        v = self.ap[:, self.off:self.off + nw].bitcast(BF16)[:, 0:n]
        self.off += nw
        return self._shape(v, shape)

    @staticmethod
    def _shape(v, shape):
        if len(shape) == 1:
            return v
        if len(shape) == 2:
            return v.rearrange("p (a b) -> p a b", a=shape[0])
        if len(shape) == 3:
            return v.rearrange("p (a b c) -> p a b c", a=shape[0], b=shape[1])
        if len(shape) == 4:
            return v.rearrange("p (a b c d) -> p a b c d", a=shape[0], b=shape[1], c=shape[2])
        raise ValueError


def tiles_of(t0, t1, step):
    out = []
    t = t0
    while t < t1:
        out.append((t, min(t + step, t1)))
        t += step
    return out


def build(n_layers, debug=False, stages=("ssd", "rwkv", "lru", "ret", "wout", "ffn")):
    nc = bass.Bass("TRN2", target_bir_lowering=False)
    L = n_layers
    dr = {}

    anymix = any(st in stages for st in ("ssd", "rwkv", "lru", "ret"))
    need = {"x": True, "meta": not os.environ.get("K_NOMETA"), "consts": True, "rope": "ret" in stages,
            "params": not os.environ.get("K_NOSETUP"), "smat": not os.environ.get("K_NOSETUP"), "w_in": anymix,
            "w_out": "wout" in stages, "w_gu": "ffn" in stages, "w_d": "ffn" in stages}

    def din(name, shape, dt=F32):
        if not need[name]:
            return None
        dr[name] = nc.dram_tensor(name, shape, dt, kind="ExternalInput").ap()
        return dr[name]

    x_d = din("x", [SEQ, D])
    meta_d = din("meta", [NMETA, D])
    consts_d = din("consts", [128, NCONST])
    rope_d = din("rope", [4, 128, T])
    par_d = din("params", [L, 128, NPAR])
    smat_d = din("smat", [L, 128, 1024])
    win_d = din("w_in", [L, 8, 128, WIN_COLS])
    wout_d = din("w_out", [L, 128, 8, D])
    wgu_d = din("w_gu", [L, NJ, 128, 2, 8, 128])
    wd_d = din("w_d", [L, 8, 128, NJ, 128])
    out_d = nc.dram_tensor("out", [SEQ, D], F32, kind="ExternalOutput").ap()
    if debug:
        dbg_d = nc.dram_tensor("dbg", [128, 8, T], BF16, kind="ExternalOutput").ap()
        dbgh_d = nc.dram_tensor("dbgh", [128, 8, T], F32, kind="ExternalOutput").ap()

    ARW = int(os.environ.get("K_ARW", "21900"))
    from contextlib import ExitStack
    with ExitStack() as es:
        def sb(name, shape, dt):
            return es.enter_context(nc.sbuf_tensor(name, shape, dt))
        h = sb("h", [128, 8, T], F32)
        yT = sb("yT", [128, 8, T], BF16)
        rstd_all = sb("rstd_all", [128, T], F32)
        Cc = sb("Cc", [128, NCONST], F32)
        Cb = sb("Cb", [128, 384], BF16)
        PRM = sb("PRM", [128, NPAR], F32)
        SM = sb("SM", [128, 1024], BF16)
        arena_t = sb("arena", [128, ARW], F32)
        EPS_T = sb("eps_t", [128, 4], F32)
        PS = es.enter_context(nc.psum_tensor("PS", [128, 4096], F32))

        prog = Prog(nc)
        k = K(nc, prog)
        ar = Arena(arena_t[:, :], ARW)
        out_ops = []

        def cst(name, a=None, b=None):
            o, w = CO[name]
            if a is None:
                return Cc[:, o:o + w]
            return Cc[:, o + a:o + b]

        def par(name, c=0, w=1):
            o, _ = PO[name]
            return PRM[:, o + c:o + c + w]

        ident_f = cst("ident")
        ones_f = cst("ones")
        blk64_f = cst("blk64")
        ident_b = Cb[:, 0:128]
        ones_b = Cb[:, 128:256]
        blk64_b = Cb[:, 256:384]

        psn = {"all": 0, "proj": 0, "chain": 0}
        bank_mode = ["all"]

        def bank(nb=1, cls="chain"):
            if bank_mode[0] == "all":
                lo_, hi_ = 0, 8
                key = "all"
            elif cls == "proj":
                lo_, hi_ = 0, NPROJ_BANKS
                key = "proj"
            else:
                lo_, hi_ = NPROJ_BANKS, 8
                key = "chain"
            b = psn[key]
            if b < lo_ or b + nb > hi_:
                b = lo_
            psn[key] = b + nb
            return PS[:, b * 512:(b + nb) * 512]

        NPROJ_BANKS = int(os.environ.get("K_NPB", "2"))
        if debug:
            k.ms("pool", yT[:, :, :], 0.0)
        k.dma_in("sp", Cc[:, :], consts_d[:, :])
        k.dma_in("pool", Cb[:, :], consts_d[:, 0:384])

        ar.reset()
        stg = [ar.f32(D), ar.f32(D)]
        for ch in range(NCH):
            s = stg[ch % 2]
            if ch == 0:
                k.ms("pool", s, 0.0)
                if not os.environ.get("K_NOMETA"):
                    k.dma_in("sp", s[PAD:128, :], meta_d[:, :])
            else:
                k.dma_in("sp", s, x_d[(ch - 1) * 128:ch * 128, :])
            for half in range(2):
                pb = bank()
                for kk4 in range(4):
                    kc = half * 4 + kk4
                    k.tr(pb[:, kk4 * 128:(kk4 + 1) * 128], s[:, kc * 128:(kc + 1) * 128], ident_f)
                for kk4 in range(4):
                    kc = half * 4 + kk4
                    k.cp(os.environ.get("K_CPENG") or (("dve" if kk4 % 2 else "act") if os.environ.get("K_SWAP") else ("act" if kk4 % 2 else "dve")), h[:, kc, ch * 128:(ch + 1) * 128], pb[:, kk4 * 128:(kk4 + 1) * 128])

        def rms_stats(src_of_k, n, rs_out, nfeat_chunks=8, denom=1024.0, sqbuf=None):
            pb = bank()
            for kc in range(nfeat_chunks):
                sq = sqbuf[kc % 2]
                src = src_of_k(kc)
                eng = "pool" if kc % 2 else "dve"
                k.tt(eng, sq[:, 0:n], src, src, ALU.mult)
                k.mm(pb[:, 0:n], ones_b, sq[:, 0:n], start=(kc == 0), stop=(kc == nfeat_chunks - 1))
            k.act(rs_out, pb[:, 0:n], AF.Ln, bias=EPS_AP, scale=1.0 / denom)
            k.act(rs_out, rs_out, AF.Exp, scale=-0.5)

        def layer_setup(l):
            k.dma_in("sp", PRM[:, 0:PO["d_aneg"][0]], par_d[l, :, 0:PO["d_aneg"][0]])
            k.dma_in("pool", SM[:, :], smat_d[l, :, :])
            k.act(par("d_aneg", 0, 4), par("ssd_alog", 0, 4), AF.Exp)
            k.ts("dve", par("d_aneg", 0, 4), par("d_aneg", 0, 4), -1.0)
            k.ts("dve", par("d_omka", 0, 2), par("rw_ka", 0, 2), -1.0, 1.0, ALU.mult, ALU.add)
            k.act(par("d_m8sp", 0, 2), par("lru_lam", 0, 2), AF.Exp, scale=-1.0)
            k.act(par("d_m8sp", 0, 2), par("d_m8sp", 0, 2), AF.Ln, bias=ONE_AP)
            k.ts("dve", par("d_m8sp", 0, 2), par("d_m8sp", 0, 2), -8.0)

        if not os.environ.get("K_NOEPS"):
            k.ms("dve", EPS_T[:, 0:1], EPS)
            k.ms("dve", EPS_T[:, 1:2], 1.0)
            k.ms("dve", EPS_T[:, 2:3], 64e-5)
            k.ms("dve", EPS_T[:, 3:4], 1e-5)
        EPS_AP = EPS_T[:, 0:1]
        ONE_AP = EPS_T[:, 1:2]
        EPS_RW = EPS_T[:, 2:3]
        EPS_RET = EPS_T[:, 3:4]

        mtiles = tiles_of(0, T, MT)

        from types import SimpleNamespace
        ctx = SimpleNamespace(**locals())

        for l in range(L):
            if not os.environ.get("K_NOSETUP"):
                layer_setup(l)
            if "mix" in stages or any(s in stages for s in ("ssd", "rwkv", "lru", "ret")):
                mixer_phase(ctx, l, stages)
            if "wout" in stages:
                wout_phase(ctx, l)
            if "ffn" in stages:
                ffn_phase(ctx, l)
            if debug and l == 0:
                out_ops.append(k.dma_out("sp", dbg_d[:, :, :], yT[:, :, :]))
                out_ops.append(k.dma_out("sp", dbgh_d[:, :, :], h[:, :, :]))

        ar.reset()
        stg = [ar.f32(D), ar.f32(D)]
        for ch in range(1, NCH):
            s = stg[ch % 2]
            for half in range(2):
                pb = bank()
                for kk4 in range(4):
                    kc = half * 4 + kk4
                    k.tr(pb[:, kk4 * 128:(kk4 + 1) * 128], h[:, kc, ch * 128:(ch + 1) * 128], ident_f)
                k.cp("act" if half else "dve", s[:, half * 512:(half + 1) * 512], pb[:, 0:512])
            out_ops.append(k.dma_out("sp", out_d[(ch - 1) * 128:ch * 128, :], s))

        mx = int(os.environ.get("K_MAXOPS", "0"))
        if mx:
            prog.ops = prog.ops[:mx]
            out_ops = [j for j in out_ops if j < mx]
        nsem = prog.plan(out_ops)
        print("ops", len(prog.ops), "sems", nsem, "est_us", round(prog.est_ns / 1000.0, 1), flush=True)
        sems = [es.enter_context(nc.semaphore("s%d" % i)) for i in range(nsem)]
        block = es.enter_context(nc.Block())
        prog.emit(block, sems, out_ops)
    nc._in_names = list(dr.keys())
    return nc


def hn_tile(c, l, t0, n, hnT, first_pass, which="pre_mix"):
    k = c.k
    if first_pass:
        c.rms_stats(lambda kc: c.h[:, kc, t0:t0 + n], n, c.rstd_all[:, t0:t0 + n], sqbuf=c.sqbuf)
    for kc in range(8):
        eng = "pool" if kc % 2 else "dve"
        k.stt(eng, hnT[:, kc, 0:n], c.h[:, kc, t0:t0 + n], c.par(which, kc), c.rstd_all[:, t0:t0 + n], ALU.mult, ALU.mult)


def load_win(c, l, Wm, ranges):
    lo = 0
    for (a, b) in ranges:
        for kc in range(8):
            c.k.dma_in("pool", Wm[:, kc, lo:lo + (b - a)], c.win_d[l, kc, :, a:b])
        lo += b - a


def proj_F(c, Wm, hnT, n, col0, dst, evac="act"):
    pb = c.bank(cls="proj")
    for kc in range(8):
        c.k.mm(pb[:, 0:n], Wm[:, kc, col0:col0 + 128], hnT[:, kc, 0:n], start=(kc == 0), stop=(kc == 7))
    c.k.cp(evac, dst, pb[:, 0:n])


def head_norm_F(c, YF, n, eps_ap, tmp, rs):
    k = c.k
    pb = c.bank()
    k.mm(pb[:, 0:n], c.blk64_f, YF, start=True, stop=True)
    k.stt("dve", YF, pb[:, 0:n], -1.0 / 64, YF, ALU.mult, ALU.add)
    k.tt("pool", tmp, YF, YF, ALU.mult)
    pb2 = c.bank()
    k.mm(pb2[:, 0:n], c.blk64_f, tmp, start=True, stop=True)
    k.act(rs, pb2[:, 0:n], AF.Ln, bias=eps_ap, scale=1.0 / 64)
    k.act(rs, rs, AF.Exp, scale=-0.5)
    k.tt("dve", YF, YF, rs, ALU.mult)


def mixer_phase(c, l, stages):
    k, ar = c.k, c.ar
    c.bank_mode[0] = "split"
    ar.reset()
    hnT2 = [ar.bf16(8, MT), ar.bf16(8, MT)]
    hnT = hnT2[0]
    c.sqbuf = [ar.bf16(512), ar.bf16(512)]
    base0 = ar.off
    first = (l == 0) or ("ffn" not in stages) or bool(os.environ.get("K_NOPRE"))
    if "lru" in stages and "ret" in stages:
        ar.reset(base0)
        Wm = ar.bf16(8, 1536)
        load_win(c, l, Wm, [(2052, 2564), (2564, 3332), (3332, 3588)])
        lt = lru_pass(c, l, Wm, hnT2, None, wo=0)
        rt = ret_pass(c, l, Wm, hnT2, None, wo=512)
        for ti, (t0, t1) in enumerate(c.mtiles):
            hn_tile(c, l, t0, t1 - t0, hnT2[ti % 2], first)
            lt(ti, t0, t1)
            rt(ti, t0, t1)
        first = False
        todo = ("ssd", "rwkv")
    else:
        todo = ("lru", "ret", "ssd", "rwkv")
    for name in todo:
        if name not in stages:
            continue
        ar.reset(base0)
        Wm = ar.bf16(8, 1028)
        if name == "lru":
            load_win(c, l, Wm, [(2052, 2564)])
            lt = lru_pass(c, l, Wm, hnT2, None, wo=0)
            for ti, (t0, t1) in enumerate(c.mtiles):
                hn_tile(c, l, t0, t1 - t0, hnT2[ti % 2], first)
                lt(ti, t0, t1)
        elif name == "ret":
            load_win(c, l, Wm, [(2564, 3332), (3332, 3588)])
            rt = ret_pass(c, l, Wm, hnT2, None, wo=0)
            for ti, (t0, t1) in enumerate(c.mtiles):
                hn_tile(c, l, t0, t1 - t0, hnT2[ti % 2], first)
                rt(ti, t0, t1)
        elif name == "ssd":
            ssd_pass(c, l, Wm, hnT2, first)
        elif name == "rwkv":
            rwkv_pass(c, l, Wm, hnT, first)
        first = False


def conv4(c, eng, out, xbuf, n, wname, bname, ci):
    k = c.k
    k.ts(eng, out, xbuf[:, 3:3 + n], c.par(wname, ci * 4 + 3), c.par(bname, ci), ALU.mult, ALU.add)
    for j in (2, 1, 0):
        k.stt(eng, out, xbuf[:, j:j + n], c.par(wname, ci * 4 + j), out, ALU.mult, ALU.add)


def lru_pass(c, l, Wm, hnT, first, wo=0):
    k, ar = c.k, c.ar
    XB_ = [ar.f32(2, 3 + MT), ar.f32(2, 3 + MT)]
    GB_ = [ar.f32(2, MT), ar.f32(2, MT)]
    XC = ar.f32(2, MT)
    XCb = ar.bf16(2, MT)
    RG = ar.f32(MT)
    IG = ar.f32(MT)
    AA = ar.f32(MT)
    UU = ar.f32(MT)
    HT = ar.f32(MT)
    hlast = ar.f32(2)
    k.ms("dve", hlast, 0.0)

    hnT2 = hnT

    def tile(ti, t0, t1):
        hnT = hnT2[ti % 2]
        XB, GB, XBp = XB_[ti % 2], GB_[ti % 2], XB_[(ti + 1) % 2]
        n = t1 - t0
        if ti == 0:
            k.ms("pool", XB[:, :, 0:3], 0.0)
        else:
            k.cp("pool", XB[:, :, 0:3], XBp[:, :, MT:MT + 3])
        for ci in range(2):
            proj_F(c, Wm, hnT, n, wo + ci * 128, XB[:, ci, 3:3 + n], "act")
            proj_F(c, Wm, hnT, n, wo + 256 + ci * 128, GB[:, ci, 0:n], "act")
        for ci in range(2):
            conv4(c, "dve" if ci else "pool", XC[:, ci, 0:n], XB[:, ci, :], n, "lru_cw", "lru_cb", ci)
            k.cp("act", XCb[:, ci, 0:n], XC[:, ci, 0:n])
            pr = c.bank()
            k.mm(pr[:, 0:n], c.SM[:, 512 + ci * 128:512 + (ci + 1) * 128], XCb[:, ci, 0:n])
            pi = c.bank()
            k.mm(pi[:, 0:n], c.SM[:, 768 + ci * 128:768 + (ci + 1) * 128], XCb[:, ci, 0:n])
            k.act(RG[:, 0:n], pr[:, 0:n], AF.Sigmoid, bias=c.par("lru_ba", ci))
            k.act(IG[:, 0:n], pi[:, 0:n], AF.Sigmoid, bias=c.par("lru_bx", ci))
            k.act(AA[:, 0:n], RG[:, 0:n], AF.Exp, scale=c.par("d_m8sp", ci))
            k.tt("pool", UU[:, 0:n], AA[:, 0:n], AA[:, 0:n], ALU.mult)
            k.act(UU[:, 0:n], UU[:, 0:n], AF.Sqrt, bias=c.ONE_AP, scale=-1.0)
            k.tt("dve", UU[:, 0:n], UU[:, 0:n], IG[:, 0:n], ALU.mult)
            k.tt("dve", UU[:, 0:n], UU[:, 0:n], XC[:, ci, 0:n], ALU.mult)
            if ti == 0:
                k.ms("dve", UU[:, 0:PAD], 0.0)
            k.scan("dve", HT[:, 0:n], AA[:, 0:n], UU[:, 0:n], hlast[:, ci:ci + 1])
            k.cp("dve", hlast[:, ci:ci + 1], HT[:, n - 1:n])
            k.act(IG[:, 0:n], GB[:, ci, 0:n], AF.Gelu_apprx_tanh)
            k.tt("dve", c.yT[:, 4 + ci, t0:t1], HT[:, 0:n], IG[:, 0:n], ALU.mult)
    return tile


def ret_pass(c, l, Wm, hnT, first, wo=0):
    k, ar = c.k, c.ar
    PRj_ = [ar.f32(6, MT), ar.f32(6, MT)]
    ROP = ar.f32(4, MT)
    T1 = ar.f32(MT)
    T2 = ar.f32(MT)
    QR_ = [ar.bf16(MT), ar.bf16(MT)]
    KR_ = [ar.bf16(MT), ar.bf16(MT)]
    QD_ = [ar.bf16(MT), ar.bf16(MT)]
    KM_ = [ar.bf16(4, MT), ar.bf16(4, MT)]
    Vt_ = [ar.bf16(2, 256), ar.bf16(2, 256)]
    KD = ar.bf16(128)
    MS = ar.bf16(4, 128)
    YF = ar.f32(2, MT)
    Rm = ar.f32(4, 64)
    Rmb = ar.bf16(4, 64)
    TK = ar.f32(4, 64)
    k.ms("dve", Rm, 0.0)
    k.ms("dve", Rmb, 0.0)
    hm = c.cst("hm")
    srcs = [wo + 0, wo + 128, wo + 512, wo + 640, wo + 768, wo + 896]

    hnT2 = hnT

    def tile(ti, t0, t1):
        hnT = hnT2[ti % 2]
        PRj = PRj_[ti % 2]
        QR, KR, QD, KM, Vt = QR_[ti % 2], KR_[ti % 2], QD_[ti % 2], KM_[ti % 2], Vt_[ti % 2]
        n = t1 - t0
        nch = n // 128
        k.dma_in("sp", ROP[:, :, 0:n], c.rope_d[:, :, t0:t1].rearrange("a p t -> p a t"))
        for i, col in enumerate(srcs):
            proj_F(c, Wm, hnT, n, col, PRj[:, i, 0:n], "act" if i % 2 else "dve")
        for ci in range(nch):
            pv = c.bank()
            for kc in range(8):
                k.mm(pv[:, 0:256], hnT[:, kc, ci * 128:(ci + 1) * 128], Wm[:, kc, wo + 256:wo + 512], start=(kc == 0), stop=(kc == 7))
            k.cp("act", Vt[:, ci, :], pv[:, 0:256])
        k.tt("dve", T1[:, 0:n], PRj[:, 0, 0:n], ROP[:, 0, 0:n], ALU.mult)
        k.tt("pool", T2[:, 0:n], PRj[:, 4, 0:n], ROP[:, 1, 0:n], ALU.mult)
        k.tt("dve", T1[:, 0:n], T1[:, 0:n], T2[:, 0:n], ALU.add)
        k.cp("act", QR[:, 0:n], T1[:, 0:n])
        k.tt("dve", QD[:, 0:n].rearrange("p (a b) -> p a b", a=nch), T1[:, 0:n].rearrange("p (a b) -> p a b", a=nch),
             c.cst("qdec").unsqueeze(1).to_broadcast([128, nch, 128]), ALU.mult)
        k.tt("dve", T1[:, 0:n], PRj[:, 1, 0:n], ROP[:, 2, 0:n], ALU.mult)
        k.tt("pool", T2[:, 0:n], PRj[:, 5, 0:n], ROP[:, 3, 0:n], ALU.mult)
        k.tt("dve", T1[:, 0:n], T1[:, 0:n], T2[:, 0:n], ALU.add)
        k.cp("act", KR[:, 0:n], T1[:, 0:n])
        k.tt("dve", KM[:, :, 0:n], T1[:, 0:n].unsqueeze(1).to_broadcast([128, 4, n]),
             hm.unsqueeze(2).to_broadcast([128, 4, n]), ALU.mult)
        for ci in range(nch):
            co = ci * 128
            pt = c.bank()
            ptb = pt[:, 0:64].bitcast(BF16)
            k.tr(ptb, KR[:, co:co + 128], c.ident_b)
            k.tt("dve", KD, ptb, c.cst("kdec"), ALU.mult)
            psc = c.bank()
            for hh in range(4):
                k.mm(psc[:, hh * 128:(hh + 1) * 128], KM[:, hh, co:co + 128], QR[:, co:co + 128])
            k.tt("dve", MS.rearrange("p a b -> p (a b)"), psc[:, 0:512], c.cst("ltT"), ALU.mult)
            py = c.bank()
            for hh in range(4):
                pc, po = hh // 2, 64 * (hh % 2)
                o = py[po:po + 64, pc * 128:(pc + 1) * 128]
                k.mm(o, Vt[:, ci, hh * 64:(hh + 1) * 64], MS[:, hh, :], start=True, stop=False)
                k.mm(o, Rmb[:, hh, :], QD[:, co:co + 128], start=False, stop=True)
            k.cp("act", YF[:, :, co:co + 128], py[:, 0:256].rearrange("p (a b) -> p a b", a=2))
            pk = c.bank()
            for hh in range(4):
                k.mm(pk[:, hh * 64:(hh + 1) * 64], KD, Vt[:, ci, hh * 64:(hh + 1) * 64])
            k.tt("dve", TK, pk[:, 0:256].rearrange("p (a b) -> p a b", a=4), hm.unsqueeze(2).to_broadcast([128, 4, 64]), ALU.mult)
            k.stt("dve", Rm, Rm, c.cst("g128"), TK, ALU.mult, ALU.add)
            k.cp("act", Rmb, Rm)
        for ci2 in range(2):
            head_norm_F(c, YF[:, ci2, 0:n], n, c.EPS_RET, T1[:, 0:n], T2[:, 0:n])
            k.act(T1[:, 0:n], PRj[:, 2 + ci2, 0:n], AF.Silu)
            k.stt("dve", c.yT[:, 6 + ci2, t0:t1], YF[:, ci2, 0:n], c.par("ret_gnw", ci2), T1[:, 0:n], ALU.mult, ALU.mult)
    return tile


def ssd_pass(c, l, Wm, hnT, first):
    k, ar = c.k, c.ar
    load_win(c, l, Wm, [(0, 1028)])
    hnT2 = hnT
    ZB_ = [ar.f32(2, MT), ar.f32(2, MT)]
    XBC_ = [ar.f32(6, 3 + MT), ar.f32(6, 3 + MT)]
    ACC = ar.f32(MT)
    XCb_ = [ar.bf16(6, MT), ar.bf16(6, MT)]
    D4_ = [ar.f32(8, 4), ar.f32(8, 4)]
    TA_ = [ar.f32(4, 128), ar.f32(4, 128)]
    T1_ = [ar.f32(4, 128), ar.f32(4, 128)]
    LT_ = [ar.f32(4, 128), ar.f32(4, 128)]
    EC_ = [ar.f32(4, 128), ar.f32(4, 128)]
    XDT_ = [ar.bf16(4, 64), ar.bf16(4, 64)]
    XDC_ = [ar.bf16(4, 64), ar.bf16(4, 64)]
    BTt_ = [ar.bf16(2, 128), ar.bf16(2, 128)]
    MSK_ = [ar.bf16(4, 128), ar.bf16(4, 128)]
    CDC_ = [ar.bf16(4, 128), ar.bf16(4, 128)]
    S = ar.f32(4, 64)
    Sb = ar.bf16(4, 64)
    TS_ = ar.f32(4, 64)
    YF = ar.f32(2, MT)
    ZS = ar.f32(MT)
    RS = ar.f32(MT)
    SQ = ar.bf16(MT)
    k.ms("dve", S, 0.0)
    k.ms("dve", Sb, 0.0)
    triu = c.cst("triu")
    for ti, (t0, t1) in enumerate(c.mtiles):
        n = t1 - t0
        nch = n // 128
        hnT = hnT2[ti % 2]
        ZB, XBC, XBCp = ZB_[ti % 2], XBC_[ti % 2], XBC_[(ti + 1) % 2]
        XCb = XCb_[ti % 2]
        hn_tile(c, l, t0, n, hnT, first)
        if ti == 0:
            k.ms("pool", XBC[:, :, 0:3], 0.0)
        else:
            k.cp("pool", XBC[:, :, 0:3], XBCp[:, :, MT:MT + 3])
        for ci in range(2):
            proj_F(c, Wm, hnT, n, ci * 128, ZB[:, ci, 0:n], "act")
        for ci in range(6):
            proj_F(c, Wm, hnT, n, 256 + ci * 128, XBC[:, ci, 3:3 + n], "act" if ci % 2 else "dve")
        for ci in range(6):
            conv4(c, "pool" if ci % 2 else "dve", ACC[:, 0:n], XBC[:, ci, :], n, "ssd_cw", "ssd_cb", ci)
            k.act(XCb[:, ci, 0:n], ACC[:, 0:n], AF.Silu)
        if ti == 0:
            k.ms("pool", XCb[:, 0:2, 0:PAD], 0.0)
        for ci in range(nch):
            co = ci * 128
            q2 = ci % 2
            D4, TA, T1, LT, EC = D4_[q2], TA_[q2], T1_[q2], LT_[q2], EC_[q2]
            XDT, XDC, BTt, MSK, CDC = XDT_[q2], XDC_[q2], BTt_[q2], MSK_[q2], CDC_[q2]
            pdt = c.bank()
            for kc in range(8):
                k.mm(pdt[:, 0:4], hnT[:, kc, co:co + 128], Wm[:, kc, 1024:1028], start=(kc == 0), stop=(kc == 7))
            k.tt("dve", D4[:, 0, :], pdt[:, 0:4], c.par("ssd_dtb", 0, 4), ALU.add)
            k.act(D4[:, 0, :], D4[:, 0, :], AF.Exp)
            k.act(D4[:, 1, :], D4[:, 0, :], AF.Ln, bias=c.ONE_AP)
            k.tt("dve", D4[:, 2, :], D4[:, 1, :], c.par("d_aneg", 0, 4), ALU.mult)
            pcs = c.bank()
            k.mm(pcs[:, 0:4], triu, D4[:, 2, :])
            k.ts("dve", D4[:, 3, :], pcs[:, 0:4], -1.0)
            k.tt("dve", TA, triu.unsqueeze(1).to_broadcast([128, 4, 128]), D4[:, 2, :].unsqueeze(2).to_broadcast([128, 4, 128]), ALU.mult)
            pcb = c.bank()
            k.mm(pcb[:, 0:512], c.ones_f, TA.rearrange("p a b -> p (a b)"))
            pcb3 = pcb[:, 0:512].rearrange("p (a b) -> p a b", a=4)
            k.tt("dve", T1, pcb3, c.cst("maskneg").unsqueeze(1).to_broadcast([128, 4, 128]), ALU.add)
            k.tt("dve", T1, T1, D4[:, 3, :].unsqueeze(2).to_broadcast([128, 4, 128]), ALU.add)
            k.act(LT, T1, AF.Exp)
            k.act(EC, pcb3, AF.Exp)
            k.tt("dve", D4[:, 7, :], D4[:, 3, :], pcb3[:, :, 127], ALU.add)
            k.act(D4[:, 4, :], D4[:, 7, :], AF.Exp)
            k.act(D4[:, 6, :], pcb3[:, :, 127], AF.Exp)
            k.tt("dve", D4[:, 5, :], D4[:, 1, :], D4[:, 4, :], ALU.mult)
            ptr = c.bank()
            ptb = ptr[:, 0:256].bitcast(BF16)
            for j in range(4):
                k.tr(ptb[:, j * 128:(j + 1) * 128], XCb[:, j, co:co + 128], c.ident_b)
            xT = ptb[:, 0:256].rearrange("p (a b) -> p a b", a=4)
            k.tt("dve", XDT, xT, D4[:, 1, :].unsqueeze(2).to_broadcast([128, 4, 64]), ALU.mult)
            k.tt("dve", XDC, xT, D4[:, 5, :].unsqueeze(2).to_broadcast([128, 4, 64]), ALU.mult)
            k.cp("act", BTt.rearrange("p a b -> p (a b)"), ptb[:, 256:512])
            psc = c.bank()
            for g in range(2):
                k.mm(psc[:, g * 128:(g + 1) * 128], XCb[:, 2 + g, co:co + 128], XCb[:, 4 + g, co:co + 128])
            sc4 = psc[:, 0:256].rearrange("p (a b) -> p a b", a=2).unsqueeze(2).to_broadcast([128, 2, 2, 128])
            k.tt("dve", MSK.rearrange("p (a b) c -> p a b c", a=2), sc4, LT.rearrange("p (a b) c -> p a b c", a=2), ALU.mult)
            c4 = XCb[:, 4:6, co:co + 128].unsqueeze(2).to_broadcast([128, 2, 2, 128])
            k.tt("pool", CDC.rearrange("p (a b) c -> p a b c", a=2), c4, EC.rearrange("p (a b) c -> p a b c", a=2), ALU.mult)
            py = c.bank()
            for hh in range(4):
                pc, po = hh // 2, 64 * (hh % 2)
                o = py[po:po + 64, pc * 128:(pc + 1) * 128]
                k.mm(o, XDT[:, hh, :], MSK[:, hh, :], start=True, stop=False)
                k.mm(o, Sb[:, hh, :], CDC[:, hh, :], start=False, stop=True)
            k.cp("act", YF[:, :, co:co + 128], py[:, 0:256].rearrange("p (a b) -> p a b", a=2))
            pst = c.bank()
            for hh in range(4):
                k.mm(pst[:, hh * 64:(hh + 1) * 64], BTt[:, hh // 2, :], XDC[:, hh, :])
            k.tt("dve", TS_, S, D4[:, 6, :].unsqueeze(2).to_broadcast([128, 4, 64]), ALU.mult)
            k.tt("dve", S, TS_, pst[:, 0:256].rearrange("p (a b) -> p a b", a=4), ALU.add)
            k.cp("act", Sb, S)
        for ci2 in range(2):
            y = YF[:, ci2, 0:n]
            k.stt("dve", y, XCb[:, ci2, 0:n], c.par("ssd_d", ci2), y, ALU.mult, ALU.add)
            k.act(ZS[:, 0:n], ZB[:, ci2, 0:n], AF.Silu)
            k.tt("dve", y, y, ZS[:, 0:n], ALU.mult)
            k.tt("pool", SQ[:, 0:n], y, y, ALU.mult)
            pb = c.bank()
            k.mm(pb[:, 0:n], c.ones_b, SQ[:, 0:n])
            k.act(RS[:, 0:n], pb[:, 0:n], AF.Ln, bias=c.EPS_AP, scale=1.0 / 128)
            k.act(RS[:, 0:n], RS[:, 0:n], AF.Exp, scale=-0.5)
            k.stt("dve", c.yT[:, ci2, t0:t1], y, c.par("ssd_nw", ci2), RS[:, 0:n], ALU.mult, ALU.mult)


def rwkv_pass(c, l, Wm, hnT, first):
    k, ar = c.k, c.ar
    if os.environ.get("K_RWSPLIT", "1") == "0":
        c.bank_mode[0] = "all"
    load_win(c, l, Wm, [(1028, 2052)])
    NC2 = MT // 128
    P8 = ar.f32(8, 1 + MT)
    HIST = ar.f32(8, 1)
    TMPS = ar.f32(2, MT)
    TW = ar.bf16(MT)
    SG = ar.bf16(MT)
    Vb = ar.bf16(2, MT)
    AV = ar.f32(1, MT)
    GG = ar.bf16(2, MT)
    KKn = ar.f32(1, MT)
    KMD = ar.f32(1, MT)
    BON = ar.f32(2, MT)
    SGW = ar.f32(MT)
    CUM = ar.f32(MT)
    PE_ = ar.f32(MT)
    TT = ar.f32(MT)
    PM = ar.f32(2, MT)
    SQb = c.sqbuf[1]
    AR = ar.bf16(2, NC2, 2, 128)
    Bt = ar.bf16(2, MT)
    Kt = ar.bf16(2, MT)
    TL_ = [ar.bf16(6, 128), ar.bf16(6, 128)]
    Sx_ = [[ar.bf16(4, 128), ar.bf16(4, 128)] for _ in range(2)]
    STx_ = [[ar.bf16(4, 128), ar.bf16(4, 128)] for _ in range(2)]
    PTx_ = [[ar.bf16(4, 128), ar.bf16(4, 128)] for _ in range(2)]
    MRB_ = [ar.bf16(4, 128), ar.bf16(4, 128)]
    AAK_ = [ar.bf16(4, 128), ar.bf16(4, 128)]
    MRK_ = [ar.bf16(4, 128), ar.bf16(4, 128)]
    XZ_ = [ar.bf16(256), ar.bf16(256)]
    U_ = [ar.bf16(256), ar.bf16(256)]
    H = ar.f32(2, 64)
    H0p = ar.f32(2, 64)
    Hb = ar.bf16(2, 64)
    YF = ar.f32(2, MT)
    k.ms("dve", H, 0.0)
    k.ms("dve", Hb, 0.0)
    msl, msu, miu = c.cst("msl"), c.cst("msu"), c.cst("miu")
    identb4 = c.ident_b.unsqueeze(1).to_broadcast([128, 4, 128])
    SMw = c.SM
    for ti, (t0, t1) in enumerate(c.mtiles):
        n = t1 - t0
        nch = n // 128
        hn_tile(c, l, t0, n, hnT, first)
        if ti == 0:
            k.ms("pool", P8[:, :, 0:1], 0.0)
        else:
            k.cp("pool", P8[:, :, 0:1], HIST)
        for ci in range(8):
            proj_F(c, Wm, hnT, n, ci * 128, P8[:, ci, 1:1 + n], "act" if ci % 2 else "dve")
        k.cp("pool", HIST, P8[:, :, n:n + 1])
        MX = P8[:, :, 1:1 + MT]
        for g4 in range(4):
            sl = slice(g4 * 2, g4 * 2 + 2)
            k.tt("dve", TMPS[:, :, 0:n], P8[:, sl, 0:n], P8[:, sl, 1:1 + n], ALU.subtract)
            k.tt("pool", TMPS[:, :, 0:n], TMPS[:, :, 0:n], c.par("rw_mu", g4 * 2, 2).unsqueeze(2).to_broadcast([128, 2, n]), ALU.mult)
            k.tt("dve", P8[:, sl, 1:1 + n], TMPS[:, :, 0:n], P8[:, sl, 1:1 + n], ALU.add)
        k.act(TW[0:64, 0:n], MX[0:64, 6, 0:n], AF.Tanh)
        k.cp("act", TW[64:128, 0:n], MX[64:128, 6, 0:n])
        k.act(SG[:, 0:n], MX[:, 7, 0:n], AF.Sigmoid)
        for ci in range(2):
            r_, k_, v_ = MX[:, ci, 0:n], MX[:, 2 + ci, 0:n], MX[:, 4 + ci, 0:n]
            k.cp("pool", Vb[:, ci, 0:n], v_)
            pw = c.bank()
            k.mm(pw[:, 0:n], SMw[0:64, ci * 128:(ci + 1) * 128], TW[0:64, 0:n])
            pa = c.bank()
            k.mm(pa[:, 0:n], SMw[64:128, ci * 128:(ci + 1) * 128], TW[64:128, 0:n])
            pg = c.bank()
            k.mm(pg[:, 0:n], SMw[:, 256 + ci * 128:256 + (ci + 1) * 128], SG[:, 0:n])
            k.act(SGW[:, 0:n], pw[:, 0:n], AF.Sigmoid, bias=c.par("rw_w0", ci))
            k.act(AV[:, 0, 0:n], pa[:, 0:n], AF.Sigmoid, bias=c.par("rw_a0", ci))
            k.cp("act", GG[:, ci, 0:n], pg[:, 0:n])
            kkn = KKn[:, 0, 0:n]
            k.ts("dve", kkn, k_, c.par("rw_kk", ci))
            k.tt("pool", SQb[:, 0:n], kkn, kkn, ALU.mult)
            pss = c.bank()
            k.mm(pss[:, 0:n], c.blk64_b, SQb[:, 0:n])
            k.ts("dve", TT[:, 0:n], pss[:, 0:n], 1e-24, None, ALU.max)
            k.act(TT[:, 0:n], TT[:, 0:n], AF.Ln)
            k.act(TT[:, 0:n], TT[:, 0:n], AF.Exp, scale=-0.5)
            k.tt("dve", kkn, kkn, TT[:, 0:n], ALU.mult)
            kmd = KMD[:, 0, 0:n]
            k.ts("dve", TT[:, 0:n], AV[:, 0, 0:n], c.par("rw_ka", ci), c.par("d_omka", ci), ALU.mult, ALU.add)
            k.tt("dve", kmd, k_, TT[:, 0:n], ALU.mult)
            k.stt("dve", SQb[:, 0:n], r_, c.par("rw_rk", ci), kmd, ALU.mult, ALU.mult)
            pbn = c.bank()
            k.mm(pbn[:, 0:n], c.blk64_b, SQb[:, 0:n])
            k.tt("dve", BON[:, ci, 0:n], pbn[:, 0:n], v_, ALU.mult)
            k.scan("dve", CUM[:, 0:n], c.cst("reset", 0, n), SGW[:, 0:n], 0.0)
            k.act(PE_[:, 0:n], CUM[:, 0:n], AF.Exp, scale=C_W)
            k.act(PM[:, ci, 0:n], CUM[:, 0:n], AF.Exp, scale=-C_W)
            k.tt("pool", TT[:, 0:n], CUM[:, 0:n], SGW[:, 0:n], ALU.subtract)
            k.act(TT[:, 0:n], TT[:, 0:n], AF.Exp, scale=-C_W)
            v3 = lambda a: a.rearrange("p (a b) -> p a b", a=nch)
            k.stt("dve", AR[:, ci, 0:nch, 0, :], v3(kkn), -1.0, v3(TT[:, 0:n]), ALU.mult, ALU.mult)
            k.tt("dve", AR[:, ci, 0:nch, 1, :], v3(r_), v3(PM[:, ci, 0:n]), ALU.mult)
            k.tt("pool", TT[:, 0:n], kkn, AV[:, 0, 0:n], ALU.mult)
            k.tt("dve", Bt[:, ci, 0:n], TT[:, 0:n], PE_[:, 0:n], ALU.mult)
            k.tt("dve", Kt[:, ci, 0:n], kmd, PE_[:, 0:n], ALU.mult)
        for ci in range(nch):
            co = ci * 128
            par2 = ci % 2
            TL, Sx, STx, PTx = TL_[par2], Sx_[par2], STx_[par2], PTx_[par2]
            MRB, AAK, MRK, XZ, U = MRB_[par2], AAK_[par2], MRK_[par2], XZ_[par2], U_[par2]
            ptr = c.bank()
            ptb = ptr[:, 0:384].bitcast(BF16)
            for j, src in enumerate((Vb, Bt, Kt)):
                for cc in range(2):
                    k.tr(ptb[:, (2 * j + cc) * 128:(2 * j + cc + 1) * 128], src[:, cc, co:co + 128], c.ident_b)
            k.cp("act", TL.rearrange("p a b -> p (a b)"), ptb)
            Vh = lambda hh: TL[:, hh // 2, 64 * (hh % 2):64 * (hh % 2) + 64]
            Bh = lambda hh: TL[:, 2 + hh // 2, 64 * (hh % 2):64 * (hh % 2) + 64]
            Kh = lambda hh: TL[:, 4 + hh // 2, 64 * (hh % 2):64 * (hh % 2) + 64]
            pA = c.bank(2)
            pB = c.bank(2)
            pC = c.bank(2)
            for hh in range(4):
                pc, po = hh // 2, 64 * (hh % 2)
                q, j = hh % 2, hh // 2
                At = AR[po:po + 64, pc, ci, 0, :]
                ARf = AR[po:po + 64, pc, ci, :, :].rearrange("p a b -> p (a b)")
                Bth = Bt[po:po + 64, pc, co:co + 128]
                Kth = Kt[po:po + 64, pc, co:co + 128]
                k.mm(pA[:, q * 512 + j * 128:q * 512 + (j + 1) * 128], At, Bth)
                k.mm(pB[:, q * 512 + j * 256:q * 512 + (j + 1) * 256], Bth, ARf)
                k.mm(pC[:, q * 512 + j * 256:q * 512 + (j + 1) * 256], Kth, ARf)
            S, ST, PT = Sx[0], STx[0], PTx[0]
            pA4 = pA.rearrange("p (q j b) -> p q j b", q=2, j=4)[:, :, 0:2, :]
            pB4 = pB.rearrange("p (a b c) -> p a b c", a=4, b=2)
            pC4 = pC.rearrange("p (a b c) -> p a b c", a=4, b=2)
            m3 = lambda m: m.unsqueeze(1).to_broadcast([128, 4, 128])
            m22 = lambda m: m.unsqueeze(1).unsqueeze(1).to_broadcast([128, 2, 2, 128])
            k.tt("dve", S.rearrange("p (q j) b -> p q j b", q=2), pA4, m22(msl), ALU.mult)
            k.tt("dve", ST, pB4[:, :, 0, :], m3(msu), ALU.mult)
            k.tt("dve", MRB, pB4[:, :, 1, :], m3(miu), ALU.mult)
            k.tt("dve", AAK, pC4[:, :, 0, :], m3(msu), ALU.mult)
            k.tt("dve", MRK, pC4[:, :, 1, :], m3(miu), ALU.mult)
            k.tt("pool", PT, ST, identb4, ALU.add)
            hp = lambda hh: (hh % 2) * 2 + hh // 2
            cur = 0
            for lev in range(6):
                nxt = 1 - cur
                S, ST, PT = Sx[cur], STx[cur], PTx[cur]
                Sn, STn, PTn = Sx[nxt], STx[nxt], PTx[nxt]
                pS = c.bank()
                for hh in range(4):
                    k.mm(pS[:, hp(hh) * 128:(hp(hh) + 1) * 128], ST[:, hp(hh), :], S[:, hp(hh), :])
                k.cp("act", Sn.rearrange("p a b -> p (a b)"), pS[:, 0:512])
                if lev < 5:
                    pT = c.bank()
                    for hh in range(4):
                        k.mm(pT[:, hp(hh) * 128:(hp(hh) + 1) * 128], S[:, hp(hh), :], ST[:, hp(hh), :])
                    k.cp("dve", STn.rearrange("p a b -> p (a b)"), pT[:, 0:512])
                pP = c.bank()
                for hh in range(4):
                    k.mm(pP[:, hp(hh) * 128:(hp(hh) + 1) * 128], Sn[:, hp(hh), :], PT[:, hp(hh), :])
                k.tt("dve", PTn.rearrange("p a b -> p (a b)"), pP[:, 0:512], PT.rearrange("p a b -> p (a b)"), ALU.add)
                cur = nxt
            PT = PTx[cur]
            pX = c.bank()
            for hh in range(4):
                pc, po = hh // 2, 64 * (hh % 2)
                o = pX[:, hh * 64:(hh + 1) * 64]
                k.mm(o, AAK[:, hp(hh), :], Vh(hh), start=True, stop=False)
                k.mm(o, AR[po:po + 64, pc, ci, 0, :], Hb[po:po + 64, pc, :], start=False, stop=True)
            k.cp("act", XZ, pX[:, 0:256])
            pU = c.bank()
            for hh in range(4):
                k.mm(pU[:, hh * 64:(hh + 1) * 64], PT[:, hp(hh), :], XZ[:, hh * 64:(hh + 1) * 64])
            k.cp("act", U, pU[:, 0:256])
            pY = c.bank()
            for hh in range(4):
                pc, po = hh // 2, 64 * (hh % 2)
                o = pY[po:po + 64, pc * 128:(pc + 1) * 128]
                k.mm(o, Hb[po:po + 64, pc, :], AR[po:po + 64, pc, ci, 1, :], start=True, stop=False)
                k.mm(o, U[:, hh * 64:(hh + 1) * 64], MRB[:, hp(hh), :], start=False, stop=False)
                k.mm(o, Vh(hh), MRK[:, hp(hh), :], start=False, stop=True)
            k.cp("act", YF[:, :, co:co + 128], pY[:, 0:256].rearrange("p (a b) -> p a b", a=2))
            pH = c.bank()
            for hh in range(4):
                pc, po = hh // 2, 64 * (hh % 2)
                o = pH[po:po + 64, pc * 64:(pc + 1) * 64]
                k.mm(o, Bh(hh), U[:, hh * 64:(hh + 1) * 64], start=True, stop=False)
                k.mm(o, Kh(hh), Vh(hh), start=False, stop=True)
            for pc in range(2):
                pl = PM[:, pc, co + 127:co + 128]
                k.ts("pool", H0p[:, pc, :], H[:, pc, :], pl)
                k.stt("dve", H[:, pc, :], pH[:, pc * 64:(pc + 1) * 64], pl, H0p[:, pc, :], ALU.mult, ALU.add)
            k.cp("act", Hb, H)
        for ci2 in range(2):
            y = YF[:, ci2, 0:n]
            head_norm_F(c, y, n, c.EPS_RW, TT[:, 0:n], CUM[:, 0:n])
            k.ts("dve", y, y, c.par("rw_lnw", ci2), c.par("rw_lnb", ci2), ALU.mult, ALU.add)
            k.tt("dve", y, y, BON[:, ci2, 0:n], ALU.add)
            k.tt("dve", c.yT[:, 2 + ci2, t0:t1], y, GG[:, ci2, 0:n], ALU.mult)


def post_norm_residual(c, O, tiles_local, t0g, wname, SQ2, RS, after_tile=None):
    k = c.k
    for (a, b) in tiles_local:
        n = b - a
        pb = c.bank()
        for m in range(8):
            sq = SQ2[m % 2]
            k.tt("pool" if m % 2 else "dve", sq[:, 0:n], O[:, m, a:b], O[:, m, a:b], ALU.mult)
            k.mm(pb[:, 0:n], c.ones_b, sq[:, 0:n], start=(m == 0), stop=(m == 7))
        k.act(RS[:, 0:n], pb[:, 0:n], AF.Ln, bias=c.EPS_AP, scale=1.0 / 1024)
        k.act(RS[:, 0:n], RS[:, 0:n], AF.Exp, scale=-0.5)
        lo = 0
        if t0g + a < PAD:
            lo = PAD - (t0g + a)
        for m in range(8):
            eng = "pool" if m % 2 else "dve"
            k.stt(eng, O[:, m, a + lo:b], O[:, m, a + lo:b], c.par(wname, m), RS[:, lo:n], ALU.mult, ALU.mult)
            k.tt(eng, c.h[:, m, t0g + a + lo:t0g + b], c.h[:, m, t0g + a + lo:t0g + b], O[:, m, a + lo:b], ALU.add)
        if after_tile is not None:
            after_tile(t0g + a, t0g + b)


def wout_phase(c, l):
    k, ar = c.k, c.ar
    c.bank_mode[0] = "all"
    ar.reset()
    Wo = ar.bf16(8, D)
    O_ = [ar.f32(8, 512), ar.f32(8, 512)]
    SQ2 = [ar.bf16(512), ar.bf16(512)]
    RS_ = [ar.f32(512), ar.f32(512)]
    for kc in range(8):
        k.dma_in("pool", Wo[:, kc, :], c.wout_d[l, :, kc, :])
    for ti, (t0, t1) in enumerate(tiles_of(0, T, 512)):
        n = t1 - t0
        O, RS = O_[ti % 2], RS_[ti % 2]
        for m in range(8):
            pb = c.bank()
            for kc in range(8):
                k.mm(pb[:, 0:n], Wo[:, kc, m * 128:(m + 1) * 128], c.yT[:, kc, t0:t1], start=(kc == 0), stop=(kc == 7))
            k.cp("act", O[:, m, 0:n], pb[:, 0:n])
        post_norm_residual(c, O, [(0, n)], t0, "post_mix", SQ2, RS,
                           after_tile=lambda ta, tb: c.rms_stats(lambda kc: c.h[:, kc, ta:tb], tb - ta, c.rstd_all[:, ta:tb], sqbuf=SQ2))


def ffn_phase(c, l):
    k, ar = c.k, c.ar
    ar.reset()
    GT = 768
    hn2x = [ar.bf16(8, GT), ar.bf16(8, GT)]
    Fo = ar.f32(8, GT)
    Wgu = [ar.bf16(2, 8, 128), ar.bf16(2, 8, 128)]
    Wd = [ar.bf16(NJ, 128), ar.bf16(NJ, 128)]
    SQ2 = [ar.bf16(512), ar.bf16(512)]
    RS = ar.f32(512)
    SIL = [ar.f32(512), ar.f32(512)]
    aT = c.yT[:, :, :].rearrange("p a b -> p (a b)")[:, 0:NJ * GT].rearrange("p (a b) -> p a b", a=NJ)
    groups = tiles_of(0, T, GT)

    def make_hn2(gi):
        g0, g1 = groups[gi]
        hn2 = hn2x[gi % 2]
        for (a, b) in tiles_of(0, g1 - g0, 512):
            for kc in range(8):
                k.stt("dve", hn2[:, kc, a:b], c.h[:, kc, g0 + a:g0 + b], c.par("pre_ffn", kc), c.rstd_all[:, g0 + a:g0 + b], ALU.mult, ALU.mult)

    make_hn2(0)
    for gi, (g0, g1) in enumerate(groups):
        ng = g1 - g0
        lt = tiles_of(0, ng, 512)
        hn2 = hn2x[gi % 2]
        for j in range(NJ):
            W = Wgu[j % 2]
            k.dma_in("pool", W.rearrange("p a b c -> p (a b c)"), c.wgu_d[l, j, :, :, :, :].rearrange("p a b c -> p (a b c)"))
            for (a, b) in lt:
                n = b - a
                pg = c.bank()
                pu = c.bank()
                for kc in range(8):
                    k.mm(pg[:, 0:n], W[:, 0, kc, :], hn2[:, kc, a:b], start=(kc == 0), stop=(kc == 7))
                for kc in range(8):
                    k.mm(pu[:, 0:n], W[:, 1, kc, :], hn2[:, kc, a:b], start=(kc == 0), stop=(kc == 7))
                sl = SIL[(j + (a > 0)) % 2]
                k.act(sl[:, 0:n], pg[:, 0:n], AF.Silu)
                k.tt("dve", aT[:, j, a:b], sl[:, 0:n], pu[:, 0:n], ALU.mult)
        if gi + 1 < len(groups):
            make_hn2(gi + 1)
        for m in range(8):
            W = Wd[m % 2]
            k.dma_in("pool", W.rearrange("p a b -> p (a b)"), c.wd_d[l, m, :, :, :].rearrange("p a b -> p (a b)"))
            for (a, b) in lt:
                n = b - a
                pb = c.bank()
                for j in range(NJ):
                    k.mm(pb[:, 0:n], W[:, j, :], aT[:, j, a:b], start=(j == 0), stop=(j == NJ - 1))
                k.cp("act", Fo[:, m, a:b], pb[:, 0:n])
        nxt = None
        if l + 1 < c.L and not os.environ.get("K_NOPRE"):
            nxt = lambda ta, tb: c.rms_stats(lambda kc: c.h[:, kc, ta:tb], tb - ta, c.rstd_all[:, ta:tb], sqbuf=SQ2)
        post_norm_residual(c, Fo, lt, g0, "post_ffn", SQ2, RS, after_tile=nxt)


def _cols(v):
    v = np.asarray(v, np.float32)
    return np.ascontiguousarray(v.reshape(-1, 128).T)


def make_consts():
    Cn = np.zeros((128, NCONST), np.float32)
    i = np.arange(128)

    def put(name, a):
        o, w = CO[name]
        Cn[:, o:o + w] = a
    put("ident", np.eye(128))
    put("ones", np.ones((128, 128)))
    put("blk64", np.kron(np.eye(2), np.ones((64, 64))))
    put("triu", (i[:, None] <= i[None, :]).astype(np.float32))
    put("maskneg", np.where(i[None, :] >= i[:, None], 0.0, -30000.0))
    put("msl", (i[:, None] > i[None, :]).astype(np.float32))
    put("msu", (i[:, None] < i[None, :]).astype(np.float32))
    put("miu", (i[:, None] <= i[None, :]).astype(np.float32))
    log_g = np.log1p(-np.exp2(-5.0 - np.arange(4, dtype=np.float64)))
    lt = np.zeros((128, 4, 128))
    for hh in range(4):
        rel = i[None, :] - i[:, None]
        lt[:, hh, :] = np.where(rel >= 0, np.exp(np.maximum(rel, 0) * log_g[hh]), 0.0)
    put("ltT", lt.reshape(128, 512))
    kd = np.zeros((128, 128))
    qd = np.zeros((128, 128))
    g128 = np.zeros((128, 1))
    hm = np.zeros((128, 4))
    for hh in range(4):
        kd[:, hh * 32:(hh + 1) * 32] = np.exp((127 - i) * log_g[hh])[:, None]
        qd[hh * 32:(hh + 1) * 32, :] = np.exp((i + 1) * log_g[hh])[None, :]
        g128[hh * 32:(hh + 1) * 32, 0] = np.exp(128 * log_g[hh])
        hm[hh * 32:(hh + 1) * 32, hh] = 1.0
    put("kdec", kd)
    put("qdec", qd)
    put("g128", g128)
    put("hm", hm)
    rs = np.ones((128, 256))
    rs[:, 0] = 0.0
    rs[:, 128] = 0.0
    put("reset", rs)
    half = 16
    freqs = 10000.0 ** (-np.arange(half, dtype=np.float64) / half)
    pos = np.arange(T, dtype=np.float64) - PAD
    ang = pos[None, :] * freqs[:, None]
    cos = np.concatenate([np.cos(ang), np.cos(ang)], 0)
    sin = np.concatenate([-np.sin(ang), np.sin(ang)], 0)
    cos = np.tile(cos, (4, 1))
    sin = np.tile(sin, (4, 1))
    sc = 32 ** -0.5
    rope = np.stack([cos, sin, cos * sc, sin * sc]).astype(np.float32)
    rope[:, :, :PAD] = 0.0
    return Cn, rope


def prep_weights(inp, L):
    g = lambda n: np.asarray(inp[n], np.float32)
    w_in = g("w_in")[:L]
    ret0 = 2564
    idx = np.arange(128)
    sw = (idx // 32) * 32 + ((idx % 32) + 16) % 32
    qsw = w_in[:, :, ret0 + sw]
    ksw = w_in[:, :, ret0 + 128 + sw]
    w_in_p = np.concatenate([w_in, qsw, ksw], axis=2)
    w_in_p = np.ascontiguousarray(w_in_p.reshape(L, 8, 128, WIN_COLS))
    w_out = np.ascontiguousarray(g("w_out")[:L].reshape(L, 8, 128, D).transpose(0, 2, 1, 3))
    wg = g("ffn_w_gate")[:L].reshape(L, 8, 128, NJ, 128)
    wu = g("ffn_w_up")[:L].reshape(L, 8, 128, NJ, 128)
    wgu = np.stack([wg, wu], axis=0)
    wgu = np.ascontiguousarray(wgu.transpose(1, 4, 3, 0, 2, 5))
    wd = g("ffn_w_down")[:L].reshape(L, NJ, 128, 8, 128)
    wd = np.ascontiguousarray(wd.transpose(0, 3, 2, 1, 4))
    params = np.zeros((L, 128, NPAR), np.float32)
    smat = np.zeros((L, 128, 1024), np.float32)
    for l in range(L):
        def put(name, a):
            o, w = PO[name]
            params[l, :, o:o + w] = a
        put("pre_mix", _cols(g("pre_mix_norm")[l]))
        put("post_mix", _cols(g("post_mix_norm")[l]))
        put("pre_ffn", _cols(g("pre_ffn_norm")[l]))
        put("post_ffn", _cols(g("post_ffn_norm")[l]))
        cw = g("ssd_conv_w")[l]
        put("ssd_cw", np.concatenate([np.stack([cw[j, ci * 128:(ci + 1) * 128] for j in range(4)], 1) for ci in range(6)], 1))
        put("ssd_cb", _cols(g("ssd_conv_b")[l]))
        put("ssd_dtb", np.tile(g("ssd_dt_bias")[l][None, :], (128, 1)))
        put("ssd_alog", np.tile(g("ssd_a_log")[l][None, :], (128, 1)))
        put("ssd_d", _cols(np.repeat(g("ssd_d")[l], 64)))
        put("ssd_nw", _cols(g("ssd_norm_w")[l]))
        put("rw_mu", _cols(g("rwkv_mu")[l]))
        put("rw_w0", _cols(g("rwkv_w0")[l]))
        put("rw_a0", _cols(g("rwkv_a0")[l]))
        put("rw_kk", _cols(g("rwkv_k_k")[l]))
        put("rw_ka", _cols(g("rwkv_k_a")[l]))
        put("rw_rk", _cols(g("rwkv_r_k")[l].reshape(-1)))
        put("rw_lnw", _cols(g("rwkv_ln_w")[l]))
        put("rw_lnb", _cols(g("rwkv_ln_b")[l]))
        lw = g("lru_conv_w")[l]
        put("lru_cw", np.concatenate([np.stack([lw[j, ci * 128:(ci + 1) * 128] for j in range(4)], 1) for ci in range(2)], 1))
        put("lru_cb", _cols(g("lru_conv_b")[l]))
        put("lru_ba", _cols(g("lru_ba")[l]))
        put("lru_bx", _cols(g("lru_bx")[l]))
        put("lru_lam", _cols(g("lru_lambda")[l]))
        put("ret_gnw", _cols(g("ret_gn_w")[l]))
        smat[l, 0:64, 0:256] = g("rwkv_w2")[l]
        smat[l, 64:128, 0:256] = g("rwkv_a2")[l]
        smat[l, :, 256:512] = g("rwkv_g2")[l]
        for nm, off in (("lru_wa", 512), ("lru_wx", 768)):
            w = g(nm)[l]
            for b in range(4):
                ci, po = b // 2, 64 * (b % 2)
                smat[l, po:po + 64, off + ci * 128 + po:off + ci * 128 + po + 64] = w[b]
    return dict(w_in=w_in_p, w_out=w_out, w_gu=wgu, w_d=wd, params=params, smat=smat)


_CACHE = {}


def kernel(**inputs):
    x = np.asarray(inputs["x"], np.float32)
    B = x.shape[0]
    Cn, rope = make_consts()
    wts = prep_weights(inputs, DEPTH)
    if "nc" not in _CACHE:
        st = os.environ.get("K_STAGES")
        _CACHE["nc"] = build(DEPTH) if st is None else build(DEPTH, stages=tuple(x for x in st.split(",") if x))
    nc = _CACHE["nc"]
    meta = np.ascontiguousarray(np.asarray(inputs["meta_tokens"], np.float32))
    in_maps = []
    for b in range(B):
        m = dict(x=np.ascontiguousarray(x[b]), meta=meta, consts=Cn, rope=rope)
        m.update(wts)
        in_maps.append({k_: v_ for k_, v_ in m.items() if k_ in nc._in_names})
    res = run_bass_kernel_spmd(nc, in_maps, core_ids=list(range(B)))
    return np.stack([np.asarray(r["out"], np.float32) for r in res.results], axis=0)
```

```python
import math
import os
import numpy as np
import ml_dtypes
import concourse.bass as bass
import concourse.mybir as mybir
from concourse.bass_utils import run_bass_kernel_spmd

F32 = mybir.dt.float32
BF16 = mybir.dt.bfloat16
AF = mybir.ActivationFunctionType
ALU = mybir.AluOpType

D = 1024
SEQ = 2048
DEPTH = 4
NMETA = 16
T = 2176
NCH = 17
PAD = 112
MT = 256
DFF = 2816
NJ = 22
EPS = 1e-6
C_W = 0.6065306597126334

CO = {}
_o = 0
for _n, _w in [("ident", 128), ("ones", 128), ("blk64", 128), ("triu", 128), ("maskneg", 128),
               ("msl", 128), ("msu", 128), ("miu", 128), ("ltT", 512), ("kdec", 128), ("qdec", 128),
               ("g128", 1), ("hm", 4), ("reset", 256)]:
    CO[_n] = (_o, _w)
    _o += _w
NCONST = ((_o + 63) // 64) * 64

PO = {}
_o = 0
for _n, _w in [("pre_mix", 8), ("post_mix", 8), ("pre_ffn", 8), ("post_ffn", 8),
               ("ssd_cw", 24), ("ssd_cb", 6), ("ssd_dtb", 4), ("ssd_alog", 4), ("ssd_d", 2), ("ssd_nw", 2),
               ("rw_mu", 8), ("rw_w0", 2), ("rw_a0", 2), ("rw_kk", 2), ("rw_ka", 2), ("rw_rk", 2),
               ("rw_lnw", 2), ("rw_lnb", 2),
               ("lru_cw", 8), ("lru_cb", 2), ("lru_ba", 2), ("lru_bx", 2), ("lru_lam", 2), ("ret_gnw", 2),
               ("d_aneg", 4), ("d_omka", 2), ("d_m8sp", 2), ("d_tmp", 4)]:
    PO[_n] = (_o, _w)
    _o += _w
NPAR = ((_o + 15) // 16) * 16

WIN_COLS = 3588


def _isz(dt):
    return 2 if dt == BF16 else 4


class Prog:
    ROT = 6000

    def __init__(self, nc):
        self.nc = nc
        self.ops = []
        self.track = {}

    @staticmethod
    def box(ap):
        isz = _isz(ap.dtype)
        pat = ap.ap
        pstride = pat[0][0] * isz
        offb = ap.offset * isz
        if pstride <= 0:
            p0, f0 = 0, offb
        else:
            p0, f0 = offb // pstride, offb % pstride
        ext = 1
        for st, cnt in pat[1:]:
            ext += (cnt - 1) * abs(st)
        nm = ap.tensor.name
        p1, b0, b1 = p0 + pat[0][1], f0, f0 + ext * isz
        if nm == "PS":
            p0, p1 = 0, 128
            b0 = (b0 // 2048) * 2048
            b1 = ((b1 + 2047) // 2048) * 2048
        return (nm, p0, p1, b0, b1)

    def add(self, eng, fn, reads, writes, kind="c", cost=300.0, tag=None):
        i = len(self.ops)
        if isinstance(eng, (list, tuple)):
            cands = list(eng)
            fns, costs = fn, cost
            eng = cands[0]
        else:
            cands, fns, costs = [eng], {eng: fn}, {eng: cost}
        deps = set()
        for ap in reads:
            nm, p0, p1, b0, b1 = self.box(ap)
            lst = self.track.setdefault(nm, [])
            for (j, q0, q1, c0, c1, w, e) in lst:
                if (w or nm == "PS") and q0 < p1 and p0 < q1 and c0 < b1 and b0 < c1:
                    deps.add(j)
        for ap in writes:
            nm, p0, p1, b0, b1 = self.box(ap)
            lst = self.track.setdefault(nm, [])
            nowar = os.environ.get("K_NOWAR") and nm == "arena"
            for (j, q0, q1, c0, c1, w, e) in lst:
                if q0 < p1 and p0 < q1 and c0 < b1 and b0 < c1 and not (nowar):
                    deps.add(j)
        for ap in reads:
            nm, p0, p1, b0, b1 = self.box(ap)
            if nm in ("Cc", "Cb", "eps_t"):
                continue
            self.track[nm].append((i, p0, p1, b0, b1, False, eng))
        for ap in writes:
            nm, p0, p1, b0, b1 = self.box(ap)
            lst = self.track[nm]
            lst[:] = [x for x in lst if not (p0 <= x[1] and x[2] <= p1 and b0 <= x[3] and x[4] <= b1)]
            lst.append((i, p0, p1, b0, b1, True, eng))
        deps.discard(i)
        self.ops.append(dict(eng=eng, fn=fns[eng], deps=deps, kind=kind, cost=costs[eng], cands=cands, fns=fns, costs=costs, tag=tag))
        return i

    def schedule(self):
        ops = self.ops
        n = len(ops)
        succ = [[] for _ in range(n)]
        indeg = [0] * n
        for i, o in enumerate(ops):
            indeg[i] = len(o["deps"])
            for j in o["deps"]:
                succ[j].append(i)
        fin = [0.0] * n
        engs = sorted(set(e for o in ops for e in o["cands"]))
        free = {e: 0.0 for e in engs}
        order = {e: [] for e in engs}
        ready = {}
        pend = []

        def release(i):
            o = ops[i]
            r = {}
            for e in o["cands"]:
                t = 0.0
                for j in o["deps"]:
                    tj = fin[j] + (60.0 if ops[j]["eng"] == e else 200.0)
                    if tj > t:
                        t = tj
                r[e] = t
            ready[i] = r
            pend.append(i)
        for i in range(n):
            if indeg[i] == 0:
                release(i)
        done = 0
        WIN = int(os.environ.get("K_WIN", "600"))
        lo = 0
        sched = [False] * n
        cur_tab = [None]
        TABLD = 1283.0
        while done < n:
            best = None
            for i in pend:
                if i > lo + WIN:
                    continue
                o = ops[i]
                r = ready[i]
                be = None
                for e in o["cands"]:
                    st = r[e] if r[e] > free[e] else free[e]
                    if e == "act" and o["tag"] is not None and o["tag"] != cur_tab[0]:
                        st += TABLD
                    f = st + o["costs"][e]
                    if be is None or f < be[0]:
                        be = (f, st, e)
                key = (be[1], i)
                if best is None or key < best[0]:
                    best = (key, i, be[2], be[1])
            if best is None:
                i = min(pend)
                o = ops[i]
                e = o["cands"][0]
                st = max(free[e], ready[i][e])
            else:
                _, i, e, st = best
                o = ops[i]
            pend.remove(i)
            del ready[i]
            o["eng"] = e
            o["fn"] = o["fns"][e]
            o["cost"] = o["costs"][e]
            if e == "act" and o["tag"] is not None:
                cur_tab[0] = o["tag"]
            f = st + o["cost"]
            free[e] = st + (o["cost"] if o["kind"] != "dma" else 60.0)
            fin[i] = f
            order[e].append(i)
            sched[i] = True
            done += 1
            while lo < n and sched[lo]:
                lo += 1
            for k2 in succ[i]:
                indeg[k2] -= 1
                if indeg[k2] == 0:
                    release(k2)
        self.est_ns = max(fin) if n else 0.0
        return order

    def plan(self, out_dma_ops):
        class _S:
            def __init__(self, i):
                self.idx = i
        cnt = [0]
        def gen():
            while True:
                cnt[0] += 1
                yield _S(cnt[0] - 1)
        self.order = self.schedule()
        self._plan = self._assign(gen(), out_dma_ops)
        return cnt[0]

    def emit(self, block, sems, out_dma_ops):
        red, sv, prevdma = self._plan
        ops = self.ops
        rs = lambda t: None if t is None else (sems[t[0].idx], t[1])
        sv = [rs(t) for t in sv]
        prevdma = [rs(t) for t in prevdma]
        self._emit(block, red, sv, prevdma, out_dma_ops)

    def _assign(self, si, out_dma_ops):
        ops = self.ops
        n = len(ops)
        need_inc = [False] * n
        pos = [0] * n
        for e, lst in self.order.items():
            for p_, i in enumerate(lst):
                pos[i] = p_
        red = []
        for i, o in enumerate(ops):
            best = {}
            dl = []
            for j in o["deps"]:
                oj = ops[j]
                if oj["kind"] == "dma":
                    dl.append(j)
                else:
                    if oj["eng"] == "pe" and o["eng"] == "pe" and o["kind"] == "c":
                        assert pos[j] < pos[i]
                        continue
                    b = best.get(oj["eng"])
                    if b is None or pos[b] < pos[j]:
                        best[oj["eng"]] = j
            dl += list(best.values())
            for j in dl:
                need_inc[j] = True
            red.append(dl)
        for j in out_dma_ops:
            need_inc[j] = True
        eng_sems = {}
        cnt = {}
        dq = {}
        dcnt = {}
        NDQ = 8
        sv = [None] * n
        prevdma = [None] * n
        seq = [i for e in sorted(self.order) for i in self.order[e]]
        for i in seq:
            o = ops[i]
            e = o["eng"]
            if o["kind"] == "dma":
                if e not in dq:
                    dq[e] = [next(si) for _ in range(NDQ)]
                    dcnt[e] = [0] * NDQ
                    cnt[("d", e)] = 0
                k = cnt[("d", e)] % NDQ
                cnt[("d", e)] += 1
                if dcnt[e][k] >= 16 * 1500:
                    dq[e][k] = next(si)
                    dcnt[e][k] = 0
                prevdma[i] = (dq[e][k], dcnt[e][k])
                dcnt[e][k] += 16
                sv[i] = (dq[e][k], dcnt[e][k])
            elif need_inc[i]:
                c = cnt.get(e, 0)
                if c % self.ROT == 0:
                    eng_sems[e] = next(si)
                cnt[e] = c + 1
                sv[i] = (eng_sems[e], c % self.ROT + 1)
        return red, sv, prevdma

    def _emit(self, block, red, sv, prevdma, out_dma_ops):
        ops = self.ops
        engmap = {"pe": block.tensor, "act": block.scalar, "dve": block.vector, "pool": block.gpsimd,
                  "sp": block.sync}
        for ename, deco in engmap.items():
            def body(eng, ename=ename):
                waited = {}
                for i in self.order.get(ename, []):
                    o = ops[i]
                    for j in red[i]:
                        s, v = sv[j]
                        if waited.get(s.num if hasattr(s, "num") else id(s), 0) >= v:
                            continue
                        eng.wait_ge(s, v)
                        waited[s.num if hasattr(s, "num") else id(s)] = v
                    if o["kind"] == "dma":
                        ps, pv = prevdma[i]
                        key = ps.num if hasattr(ps, "num") else id(ps)
                        if pv > 0 and waited.get(key, 0) < pv:
                            eng.wait_ge(ps, pv)
                            waited[key] = pv
                        o["fn"](eng).then_inc(sv[i][0], 16)
                    else:
                        ins = o["fn"](eng)
                        if sv[i] is not None:
                            ins.then_inc(sv[i][0], 1)
                if ename == "sp":
                    for j in out_dma_ops:
                        s, v = sv[j]
                        eng.wait_ge(s, v)
            deco(body)


def _fn(ap):
    n = 1
    for d in ap.shape[1:]:
        n *= d
    return n


def _ec(eng, out, ins, kind="tt"):
    n = _fn(out)
    if eng == "act":
        return 225.0 + n * 0.85
    if eng == "dve":
        return 150.0 + n * 1.15
    if kind == "ts":
        return 1100.0 + n * 1.2
    if kind == "cp":
        return 220.0 + n * 1.0
    return 330.0 + n * 1.95


class K:
    def __init__(self, nc, prog):
        self.nc = nc
        self.p = prog

    def mm(self, out, lhsT, rhs, start=True, stop=True):
        rd = [lhsT, rhs] + ([] if start else [out])
        nn = max(_fn(rhs), 64) * (4 if _isz(rhs.dtype) == 4 else 1)
        self.p.add("pe", lambda e: e.matmul(out, lhsT, rhs, start=start, stop=stop), rd, [out], cost=57.0 + nn / 2.4)

    def tr(self, out, in_, ident):
        self.p.add("pe", lambda e: e.transpose(out, in_, ident), [in_, ident], [out], cost=90.0 * (2 if _isz(in_.dtype) == 4 else 1))

    def act(self, out, in_, func, bias=None, scale=None, eng="act"):
        rd = [in_]
        kw = {}
        if bias is not None:
            kw["bias"] = bias
            if not isinstance(bias, float):
                rd.append(bias)
        if scale is not None:
            kw["scale"] = scale
            if not isinstance(scale, float):
                rd.append(scale)
        tag = None if func in (AF.Copy, AF.Identity) else ("explog" if func in (AF.Exp, AF.Ln) else str(func))
        self.p.add("act", lambda e: e.activation(out=out, in_=in_, func=func, **kw), rd, [out], cost=_ec("act", out, [in_]), tag=tag)

    @staticmethod
    def _cands(out, ins, act_ok=False):
        ps = any(a.tensor.name == "PS" for a in list(ins) + [out])
        c = ["dve"] if (ps or not os.environ.get("K_POOLCOMPUTE")) else ["dve", "pool"]
        if act_ok:
            c.append("act")
        return c

    def _flex(self, cands, mk, out, ins, reads, kind="tt"):
        fns = {e: mk(e) for e in cands}
        costs = {e: _ec(e, out, ins, kind) for e in cands}
        self.p.add(cands, fns, reads, [out], cost=costs)

    def tt(self, eng, out, in0, in1, op):
        mk = lambda en: (lambda e: e.tensor_tensor(out=out, in0=in0, in1=in1, op=op))
        self._flex(self._cands(out, [in0, in1]), mk, out, [in0, in1], [in0, in1])

    def ts(self, eng, out, in0, s1, s2=None, op0=ALU.mult, op1=None):
        rd = [in0] + [s for s in (s1, s2) if s is not None and not isinstance(s, float)]
        if op1 is None:
            mk = lambda en: (lambda e: e.tensor_scalar(out=out, in0=in0, scalar1=s1, scalar2=None, op0=op0))
        else:
            mk = lambda en: (lambda e: e.tensor_scalar(out=out, in0=in0, scalar1=s1, scalar2=s2, op0=op0, op1=op1))
        self._flex(self._cands(out, rd), mk, out, [in0], rd, kind="ts")

    def stt(self, eng, out, in0, scalar, in1, op0, op1):
        eng = "dve"
        rd = [in0, in1] + ([] if isinstance(scalar, float) else [scalar])
        self.p.add(eng, lambda e: e.scalar_tensor_tensor(out=out, in0=in0, scalar=scalar, in1=in1, op0=op0, op1=op1), rd, [out], cost=_ec(eng, out, [in0, in1]))

    def cp(self, eng, out, in_):
        def mk(en):
            if en == "act":
                return lambda e: e.activation(out=out, in_=in_, func=AF.Copy)
            return lambda e: e.tensor_copy(out=out, in_=in_)
        self._flex(self._cands(out, [in_], act_ok=True), mk, out, [in_], [in_], kind="cp")

    def ms(self, eng, ap, val):
        mk = lambda en: (lambda e: e.memset(ap, val))
        self._flex(self._cands(ap, []), mk, ap, [], [], kind="cp")

    def scan(self, eng, out, d0, d1, init):
        eng = "dve"
        rd = [d0, d1] + ([] if isinstance(init, float) else [init])
        self.p.add(eng, lambda e: e.tensor_tensor_scan(out=out, data0=d0, data1=d1, initial=init, op0=ALU.mult, op1=ALU.add), rd, [out], cost=100.0 + _fn(out) / 0.5)

    def dma_in(self, q, out, in_):
        return self.p.add(q, lambda e: e.dma_start(out=out, in_=in_), [], [out], kind="dma", cost=2500.0 + _fn(out) * 128 * 4 / 320.0)

    def dma_out(self, q, out, in_):
        return self.p.add(q, lambda e: e.dma_start(out=out, in_=in_), [in_], [], kind="dma", cost=2500.0 + _fn(in_) * 128 * 4 / 320.0)


class Arena:
    def __init__(self, ap_f32, nwords):
        self.ap = ap_f32
        self.n = nwords
        self.off = 0

    def reset(self, off=0):
        if os.environ.get("K_ARDBG") and getattr(self, "hi", 0):
            print("   arena hi", self.hi, "of", self.n)
        self.hi = 0
        self.off = off

    def f32(self, *shape):
        n = int(np.prod(shape))
        self.hi = max(getattr(self, "hi", 0), self.off + n)
        assert self.off + n <= self.n, ("arena overflow", self.off, n, self.n)
        v = self.ap[:, self.off:self.off + n]
        self.off += n
        return self._shape(v, shape)

    def bf16(self, *shape):
        n = int(np.prod(shape))
        nw = (n + 1) // 2
        self.hi = max(getattr(self, "hi", 0), self.off + nw)
        assert self.off + nw <= self.n, ("arena overflow", self.off, nw, self.n)
        v = self.ap[:, self.off:self.off + nw].bitcast(BF16)[:, 0:n]
        self.off += nw
        return self._shape(v, shape)

    @staticmethod
    def _shape(v, shape):
        if len(shape) == 1:
            return v
        if len(shape) == 2:
            return v.rearrange("p (a b) -> p a b", a=shape[0])
        if len(shape) == 3:
            return v.rearrange("p (a b c) -> p a b c", a=shape[0], b=shape[1])
        if len(shape) == 4:
            return v.rearrange("p (a b c d) -> p a b c d", a=shape[0], b=shape[1], c=shape[2])
        raise ValueError


def tiles_of(t0, t1, step):
    out = []
    t = t0
    while t < t1:
        out.append((t, min(t + step, t1)))
        t += step
    return out


def build(n_layers, debug=False, stages=("ssd", "rwkv", "lru", "ret", "wout", "ffn")):
    nc = bass.Bass("TRN2", target_bir_lowering=False)
    L = n_layers
    dr = {}

    anymix = any(st in stages for st in ("ssd", "rwkv", "lru", "ret"))
    need = {"x": True, "meta": not os.environ.get("K_NOMETA"), "consts": True, "rope": "ret" in stages,
            "params": not os.environ.get("K_NOSETUP"), "smat": not os.environ.get("K_NOSETUP"), "w_in": anymix,
            "w_out": "wout" in stages, "w_gu": "ffn" in stages, "w_d": "ffn" in stages}

    def din(name, shape, dt=F32):
        if not need[name]:
            return None
        dr[name] = nc.dram_tensor(name, shape, dt, kind="ExternalInput").ap()
        return dr[name]

    x_d = din("x", [SEQ, D])
    meta_d = din("meta", [NMETA, D])
    consts_d = din("consts", [128, NCONST])
    rope_d = din("rope", [4, 128, T])
    par_d = din("params", [L, 128, NPAR])
    smat_d = din("smat", [L, 128, 1024])
    win_d = din("w_in", [L, 8, 128, WIN_COLS])
    wout_d = din("w_out", [L, 128, 8, D])
    wgu_d = din("w_gu", [L, NJ, 128, 2, 8, 128])
    wd_d = din("w_d", [L, 8, 128, NJ, 128])
    out_d = nc.dram_tensor("out", [SEQ, D], F32, kind="ExternalOutput").ap()
    if debug:
        dbg_d = nc.dram_tensor("dbg", [128, 8, T], BF16, kind="ExternalOutput").ap()
        dbgh_d = nc.dram_tensor("dbgh", [128, 8, T], F32, kind="ExternalOutput").ap()

    ARW = int(os.environ.get("K_ARW", "21900"))
    from contextlib import ExitStack
    with ExitStack() as es:
        def sb(name, shape, dt):
            return es.enter_context(nc.sbuf_tensor(name, shape, dt))
        h = sb("h", [128, 8, T], F32)
        yT = sb("yT", [128, 8, T], BF16)
        rstd_all = sb("rstd_all", [128, T], F32)
        Cc = sb("Cc", [128, NCONST], F32)
        Cb = sb("Cb", [128, 384], BF16)
        PRM = sb("PRM", [128, NPAR], F32)
        SM = sb("SM", [128, 1024], BF16)
        arena_t = sb("arena", [128, ARW], F32)
        EPS_T = sb("eps_t", [128, 4], F32)
        PS = es.enter_context(nc.psum_tensor("PS", [128, 4096], F32))

        prog = Prog(nc)
        k = K(nc, prog)
        ar = Arena(arena_t[:, :], ARW)
        out_ops = []

        def cst(name, a=None, b=None):
            o, w = CO[name]
            if a is None:
                return Cc[:, o:o + w]
            return Cc[:, o + a:o + b]

        def par(name, c=0, w=1):
            o, _ = PO[name]
            return PRM[:, o + c:o + c + w]

        ident_f = cst("ident")
        ones_f = cst("ones")
        blk64_f = cst("blk64")
        ident_b = Cb[:, 0:128]
        ones_b = Cb[:, 128:256]
        blk64_b = Cb[:, 256:384]

        psn = {"all": 0, "proj": 0, "chain": 0}
        bank_mode = ["all"]

        def bank(nb=1, cls="chain"):
            if bank_mode[0] == "all":
                lo_, hi_ = 0, 8
                key = "all"
            elif cls == "proj":
                lo_, hi_ = 0, NPROJ_BANKS
                key = "proj"
            else:
                lo_, hi_ = NPROJ_BANKS, 8
                key = "chain"
            b = psn[key]
            if b < lo_ or b + nb > hi_:
                b = lo_
            psn[key] = b + nb
            return PS[:, b * 512:(b + nb) * 512]

        NPROJ_BANKS = int(os.environ.get("K_NPB", "2"))
        if debug:
            k.ms("pool", yT[:, :, :], 0.0)
        k.dma_in("sp", Cc[:, :], consts_d[:, :])
        k.dma_in("pool", Cb[:, :], consts_d[:, 0:384])

        ar.reset()
        stg = [ar.f32(D), ar.f32(D)]
        for ch in range(NCH):
            s = stg[ch % 2]
            if ch == 0:
                k.ms("pool", s, 0.0)
                if not os.environ.get("K_NOMETA"):
                    k.dma_in("sp", s[PAD:128, :], meta_d[:, :])
            else:
                k.dma_in("sp", s, x_d[(ch - 1) * 128:ch * 128, :])
            for half in range(2):
                pb = bank()
                for kk4 in range(4):
                    kc = half * 4 + kk4
                    k.tr(pb[:, kk4 * 128:(kk4 + 1) * 128], s[:, kc * 128:(kc + 1) * 128], ident_f)
                for kk4 in range(4):
                    kc = half * 4 + kk4
                    k.cp(os.environ.get("K_CPENG") or (("dve" if kk4 % 2 else "act") if os.environ.get("K_SWAP") else ("act" if kk4 % 2 else "dve")), h[:, kc, ch * 128:(ch + 1) * 128], pb[:, kk4 * 128:(kk4 + 1) * 128])

        def rms_stats(src_of_k, n, rs_out, nfeat_chunks=8, denom=1024.0, sqbuf=None):
            pb = bank()
            for kc in range(nfeat_chunks):
                sq = sqbuf[kc % 2]
                src = src_of_k(kc)
                eng = "pool" if kc % 2 else "dve"
                k.tt(eng, sq[:, 0:n], src, src, ALU.mult)
                k.mm(pb[:, 0:n], ones_b, sq[:, 0:n], start=(kc == 0), stop=(kc == nfeat_chunks - 1))
            k.act(rs_out, pb[:, 0:n], AF.Ln, bias=EPS_AP, scale=1.0 / denom)
            k.act(rs_out, rs_out, AF.Exp, scale=-0.5)

        def layer_setup(l):
            k.dma_in("sp", PRM[:, 0:PO["d_aneg"][0]], par_d[l, :, 0:PO["d_aneg"][0]])
            k.dma_in("pool", SM[:, :], smat_d[l, :, :])
            k.act(par("d_aneg", 0, 4), par("ssd_alog", 0, 4), AF.Exp)
            k.ts("dve", par("d_aneg", 0, 4), par("d_aneg", 0, 4), -1.0)
            k.ts("dve", par("d_omka", 0, 2), par("rw_ka", 0, 2), -1.0, 1.0, ALU.mult, ALU.add)
            k.act(par("d_m8sp", 0, 2), par("lru_lam", 0, 2), AF.Exp, scale=-1.0)
            k.act(par("d_m8sp", 0, 2), par("d_m8sp", 0, 2), AF.Ln, bias=ONE_AP)
            k.ts("dve", par("d_m8sp", 0, 2), par("d_m8sp", 0, 2), -8.0)

        if not os.environ.get("K_NOEPS"):
            k.ms("dve", EPS_T[:, 0:1], EPS)
            k.ms("dve", EPS_T[:, 1:2], 1.0)
            k.ms("dve", EPS_T[:, 2:3], 64e-5)
            k.ms("dve", EPS_T[:, 3:4], 1e-5)
        EPS_AP = EPS_T[:, 0:1]
        ONE_AP = EPS_T[:, 1:2]
        EPS_RW = EPS_T[:, 2:3]
        EPS_RET = EPS_T[:, 3:4]

        mtiles = tiles_of(0, T, MT)

        from types import SimpleNamespace
        ctx = SimpleNamespace(**locals())

        for l in range(L):
            if not os.environ.get("K_NOSETUP"):
                layer_setup(l)
            if "mix" in stages or any(s in stages for s in ("ssd", "rwkv", "lru", "ret")):
                mixer_phase(ctx, l, stages)
            if "wout" in stages:
                wout_phase(ctx, l)
            if "ffn" in stages:
                ffn_phase(ctx, l)
            if debug and l == 0:
                out_ops.append(k.dma_out("sp", dbg_d[:, :, :], yT[:, :, :]))
                out_ops.append(k.dma_out("sp", dbgh_d[:, :, :], h[:, :, :]))

        ar.reset()
        stg = [ar.f32(D), ar.f32(D)]
        for ch in range(1, NCH):
            s = stg[ch % 2]
            for half in range(2):
                pb = bank()
                for kk4 in range(4):
                    kc = half * 4 + kk4
                    k.tr(pb[:, kk4 * 128:(kk4 + 1) * 128], h[:, kc, ch * 128:(ch + 1) * 128], ident_f)
                k.cp("act" if half else "dve", s[:, half * 512:(half + 1) * 512], pb[:, 0:512])
            out_ops.append(k.dma_out("sp", out_d[(ch - 1) * 128:ch * 128, :], s))

        mx = int(os.environ.get("K_MAXOPS", "0"))
        if mx:
            prog.ops = prog.ops[:mx]
            out_ops = [j for j in out_ops if j < mx]
        nsem = prog.plan(out_ops)
        print("ops", len(prog.ops), "sems", nsem, "est_us", round(prog.est_ns / 1000.0, 1), flush=True)
        sems = [es.enter_context(nc.semaphore("s%d" % i)) for i in range(nsem)]
        block = es.enter_context(nc.Block())
        prog.emit(block, sems, out_ops)
    nc._in_names = list(dr.keys())
    return nc


def hn_tile(c, l, t0, n, hnT, first_pass, which="pre_mix"):
    k = c.k
    if first_pass:
        c.rms_stats(lambda kc: c.h[:, kc, t0:t0 + n], n, c.rstd_all[:, t0:t0 + n], sqbuf=c.sqbuf)
    for kc in range(8):
        eng = "pool" if kc % 2 else "dve"
        k.stt(eng, hnT[:, kc, 0:n], c.h[:, kc, t0:t0 + n], c.par(which, kc), c.rstd_all[:, t0:t0 + n], ALU.mult, ALU.mult)


def load_win(c, l, Wm, ranges):
    lo = 0
    for (a, b) in ranges:
        for kc in range(8):
            c.k.dma_in("pool", Wm[:, kc, lo:lo + (b - a)], c.win_d[l, kc, :, a:b])
        lo += b - a


def proj_F(c, Wm, hnT, n, col0, dst, evac="act"):
    pb = c.bank(cls="proj")
    for kc in range(8):
        c.k.mm(pb[:, 0:n], Wm[:, kc, col0:col0 + 128], hnT[:, kc, 0:n], start=(kc == 0), stop=(kc == 7))
    c.k.cp(evac, dst, pb[:, 0:n])


def head_norm_F(c, YF, n, eps_ap, tmp, rs):
    k = c.k
    pb = c.bank()
    k.mm(pb[:, 0:n], c.blk64_f, YF, start=True, stop=True)
    k.stt("dve", YF, pb[:, 0:n], -1.0 / 64, YF, ALU.mult, ALU.add)
    k.tt("pool", tmp, YF, YF, ALU.mult)
    pb2 = c.bank()
    k.mm(pb2[:, 0:n], c.blk64_f, tmp, start=True, stop=True)
    k.act(rs, pb2[:, 0:n], AF.Ln, bias=eps_ap, scale=1.0 / 64)
    k.act(rs, rs, AF.Exp, scale=-0.5)
    k.tt("dve", YF, YF, rs, ALU.mult)


def mixer_phase(c, l, stages):
    k, ar = c.k, c.ar
    c.bank_mode[0] = "split"
    ar.reset()
    hnT2 = [ar.bf16(8, MT), ar.bf16(8, MT)]
    hnT = hnT2[0]
    c.sqbuf = [ar.bf16(512), ar.bf16(512)]
    base0 = ar.off
    first = (l == 0) or ("ffn" not in stages) or bool(os.environ.get("K_NOPRE"))
    if "lru" in stages and "ret" in stages:
        ar.reset(base0)
        Wm = ar.bf16(8, 1536)
        load_win(c, l, Wm, [(2052, 2564), (2564, 3332), (3332, 3588)])
        lt = lru_pass(c, l, Wm, hnT2, None, wo=0)
        rt = ret_pass(c, l, Wm, hnT2, None, wo=512)
        for ti, (t0, t1) in enumerate(c.mtiles):
            hn_tile(c, l, t0, t1 - t0, hnT2[ti % 2], first)
            lt(ti, t0, t1)
            rt(ti, t0, t1)
        first = False
        todo = ("ssd", "rwkv")
    else:
        todo = ("lru", "ret", "ssd", "rwkv")
    for name in todo:
        if name not in stages:
            continue
        ar.reset(base0)
        Wm = ar.bf16(8, 1028)
        if name == "lru":
            load_win(c, l, Wm, [(2052, 2564)])
            lt = lru_pass(c, l, Wm, hnT2, None, wo=0)
            for ti, (t0, t1) in enumerate(c.mtiles):
                hn_tile(c, l, t0, t1 - t0, hnT2[ti % 2], first)
                lt(ti, t0, t1)
        elif name == "ret":
            load_win(c, l, Wm, [(2564, 3332), (3332, 3588)])
            rt = ret_pass(c, l, Wm, hnT2, None, wo=0)
            for ti, (t0, t1) in enumerate(c.mtiles):
                hn_tile(c, l, t0, t1 - t0, hnT2[ti % 2], first)
                rt(ti, t0, t1)
        elif name == "ssd":
            ssd_pass(c, l, Wm, hnT2, first)
        elif name == "rwkv":
            rwkv_pass(c, l, Wm, hnT, first)
        first = False


def conv4(c, eng, out, xbuf, n, wname, bname, ci):
    k = c.k
    k.ts(eng, out, xbuf[:, 3:3 + n], c.par(wname, ci * 4 + 3), c.par(bname, ci), ALU.mult, ALU.add)
    for j in (2, 1, 0):
        k.stt(eng, out, xbuf[:, j:j + n], c.par(wname, ci * 4 + j), out, ALU.mult, ALU.add)


def lru_pass(c, l, Wm, hnT, first, wo=0):
    k, ar = c.k, c.ar
    XB_ = [ar.f32(2, 3 + MT), ar.f32(2, 3 + MT)]
    GB_ = [ar.f32(2, MT), ar.f32(2, MT)]
    XC = ar.f32(2, MT)
    XCb = ar.bf16(2, MT)
    RG = ar.f32(MT)
    IG = ar.f32(MT)
    AA = ar.f32(MT)
    UU = ar.f32(MT)
    HT = ar.f32(MT)
    hlast = ar.f32(2)
    k.ms("dve", hlast, 0.0)

    hnT2 = hnT

    def tile(ti, t0, t1):
        hnT = hnT2[ti % 2]
        XB, GB, XBp = XB_[ti % 2], GB_[ti % 2], XB_[(ti + 1) % 2]
        n = t1 - t0
        if ti == 0:
            k.ms("pool", XB[:, :, 0:3], 0.0)
        else:
            k.cp("pool", XB[:, :, 0:3], XBp[:, :, MT:MT + 3])
        for ci in range(2):
            proj_F(c, Wm, hnT, n, wo + ci * 128, XB[:, ci, 3:3 + n], "act")
            proj_F(c, Wm, hnT, n, wo + 256 + ci * 128, GB[:, ci, 0:n], "act")
        for ci in range(2):
            conv4(c, "dve" if ci else "pool", XC[:, ci, 0:n], XB[:, ci, :], n, "lru_cw", "lru_cb", ci)
            k.cp("act", XCb[:, ci, 0:n], XC[:, ci, 0:n])
            pr = c.bank()
            k.mm(pr[:, 0:n], c.SM[:, 512 + ci * 128:512 + (ci + 1) * 128], XCb[:, ci, 0:n])
            pi = c.bank()
            k.mm(pi[:, 0:n], c.SM[:, 768 + ci * 128:768 + (ci + 1) * 128], XCb[:, ci, 0:n])
            k.act(RG[:, 0:n], pr[:, 0:n], AF.Sigmoid, bias=c.par("lru_ba", ci))
            k.act(IG[:, 0:n], pi[:, 0:n], AF.Sigmoid, bias=c.par("lru_bx", ci))
            k.act(AA[:, 0:n], RG[:, 0:n], AF.Exp, scale=c.par("d_m8sp", ci))
            k.tt("pool", UU[:, 0:n], AA[:, 0:n], AA[:, 0:n], ALU.mult)
            k.act(UU[:, 0:n], UU[:, 0:n], AF.Sqrt, bias=c.ONE_AP, scale=-1.0)
            k.tt("dve", UU[:, 0:n], UU[:, 0:n], IG[:, 0:n], ALU.mult)
            k.tt("dve", UU[:, 0:n], UU[:, 0:n], XC[:, ci, 0:n], ALU.mult)
            if ti == 0:
                k.ms("dve", UU[:, 0:PAD], 0.0)
            k.scan("dve", HT[:, 0:n], AA[:, 0:n], UU[:, 0:n], hlast[:, ci:ci + 1])
            k.cp("dve", hlast[:, ci:ci + 1], HT[:, n - 1:n])
            k.act(IG[:, 0:n], GB[:, ci, 0:n], AF.Gelu_apprx_tanh)
            k.tt("dve", c.yT[:, 4 + ci, t0:t1], HT[:, 0:n], IG[:, 0:n], ALU.mult)
    return tile


def ret_pass(c, l, Wm, hnT, first, wo=0):
    k, ar = c.k, c.ar
    PRj_ = [ar.f32(6, MT), ar.f32(6, MT)]
    ROP = ar.f32(4, MT)
    T1 = ar.f32(MT)
    T2 = ar.f32(MT)
    QR_ = [ar.bf16(MT), ar.bf16(MT)]
    KR_ = [ar.bf16(MT), ar.bf16(MT)]
    QD_ = [ar.bf16(MT), ar.bf16(MT)]
    KM_ = [ar.bf16(4, MT), ar.bf16(4, MT)]
    Vt_ = [ar.bf16(2, 256), ar.bf16(2, 256)]
    KD = ar.bf16(128)
    MS = ar.bf16(4, 128)
    YF = ar.f32(2, MT)
    Rm = ar.f32(4, 64)
    Rmb = ar.bf16(4, 64)
    TK = ar.f32(4, 64)
    k.ms("dve", Rm, 0.0)
    k.ms("dve", Rmb, 0.0)
    hm = c.cst("hm")
    srcs = [wo + 0, wo + 128, wo + 512, wo + 640, wo + 768, wo + 896]

    hnT2 = hnT

    def tile(ti, t0, t1):
        hnT = hnT2[ti % 2]
        PRj = PRj_[ti % 2]
        QR, KR, QD, KM, Vt = QR_[ti % 2], KR_[ti % 2], QD_[ti % 2], KM_[ti % 2], Vt_[ti % 2]
        n = t1 - t0
        nch = n // 128
        k.dma_in("sp", ROP[:, :, 0:n], c.rope_d[:, :, t0:t1].rearrange("a p t -> p a t"))
        for i, col in enumerate(srcs):
            proj_F(c, Wm, hnT, n, col, PRj[:, i, 0:n], "act" if i % 2 else "dve")
        for ci in range(nch):
            pv = c.bank()
            for kc in range(8):
                k.mm(pv[:, 0:256], hnT[:, kc, ci * 128:(ci + 1) * 128], Wm[:, kc, wo + 256:wo + 512], start=(kc == 0), stop=(kc == 7))
            k.cp("act", Vt[:, ci, :], pv[:, 0:256])
        k.tt("dve", T1[:, 0:n], PRj[:, 0, 0:n], ROP[:, 0, 0:n], ALU.mult)
        k.tt("pool", T2[:, 0:n], PRj[:, 4, 0:n], ROP[:, 1, 0:n], ALU.mult)
        k.tt("dve", T1[:, 0:n], T1[:, 0:n], T2[:, 0:n], ALU.add)
        k.cp("act", QR[:, 0:n], T1[:, 0:n])
        k.tt("dve", QD[:, 0:n].rearrange("p (a b) -> p a b", a=nch), T1[:, 0:n].rearrange("p (a b) -> p a b", a=nch),
             c.cst("qdec").unsqueeze(1).to_broadcast([128, nch, 128]), ALU.mult)
        k.tt("dve", T1[:, 0:n], PRj[:, 1, 0:n], ROP[:, 2, 0:n], ALU.mult)
        k.tt("pool", T2[:, 0:n], PRj[:, 5, 0:n], ROP[:, 3, 0:n], ALU.mult)
        k.tt("dve", T1[:, 0:n], T1[:, 0:n], T2[:, 0:n], ALU.add)
        k.cp("act", KR[:, 0:n], T1[:, 0:n])
        k.tt("dve", KM[:, :, 0:n], T1[:, 0:n].unsqueeze(1).to_broadcast([128, 4, n]),
             hm.unsqueeze(2).to_broadcast([128, 4, n]), ALU.mult)
        for ci in range(nch):
            co = ci * 128
            pt = c.bank()
            ptb = pt[:, 0:64].bitcast(BF16)
            k.tr(ptb, KR[:, co:co + 128], c.ident_b)
            k.tt("dve", KD, ptb, c.cst("kdec"), ALU.mult)
            psc = c.bank()
            for hh in range(4):
                k.mm(psc[:, hh * 128:(hh + 1) * 128], KM[:, hh, co:co + 128], QR[:, co:co + 128])
            k.tt("dve", MS.rearrange("p a b -> p (a b)"), psc[:, 0:512], c.cst("ltT"), ALU.mult)
            py = c.bank()
            for hh in range(4):
                pc, po = hh // 2, 64 * (hh % 2)
                o = py[po:po + 64, pc * 128:(pc + 1) * 128]
                k.mm(o, Vt[:, ci, hh * 64:(hh + 1) * 64], MS[:, hh, :], start=True, stop=False)
                k.mm(o, Rmb[:, hh, :], QD[:, co:co + 128], start=False, stop=True)
            k.cp("act", YF[:, :, co:co + 128], py[:, 0:256].rearrange("p (a b) -> p a b", a=2))
            pk = c.bank()
            for hh in range(4):
                k.mm(pk[:, hh * 64:(hh + 1) * 64], KD, Vt[:, ci, hh * 64:(hh + 1) * 64])
            k.tt("dve", TK, pk[:, 0:256].rearrange("p (a b) -> p a b", a=4), hm.unsqueeze(2).to_broadcast([128, 4, 64]), ALU.mult)
            k.stt("dve", Rm, Rm, c.cst("g128"), TK, ALU.mult, ALU.add)
            k.cp("act", Rmb, Rm)
        for ci2 in range(2):
            head_norm_F(c, YF[:, ci2, 0:n], n, c.EPS_RET, T1[:, 0:n], T2[:, 0:n])
            k.act(T1[:, 0:n], PRj[:, 2 + ci2, 0:n], AF.Silu)
            k.stt("dve", c.yT[:, 6 + ci2, t0:t1], YF[:, ci2, 0:n], c.par("ret_gnw", ci2), T1[:, 0:n], ALU.mult, ALU.mult)
    return tile


def ssd_pass(c, l, Wm, hnT, first):
    k, ar = c.k, c.ar
    load_win(c, l, Wm, [(0, 1028)])
    hnT2 = hnT
    ZB_ = [ar.f32(2, MT), ar.f32(2, MT)]
    XBC_ = [ar.f32(6, 3 + MT), ar.f32(6, 3 + MT)]
    ACC = ar.f32(MT)
    XCb_ = [ar.bf16(6, MT), ar.bf16(6, MT)]
    D4_ = [ar.f32(8, 4), ar.f32(8, 4)]
    TA_ = [ar.f32(4, 128), ar.f32(4, 128)]
    T1_ = [ar.f32(4, 128), ar.f32(4, 128)]
    LT_ = [ar.f32(4, 128), ar.f32(4, 128)]
    EC_ = [ar.f32(4, 128), ar.f32(4, 128)]
    XDT_ = [ar.bf16(4, 64), ar.bf16(4, 64)]
    XDC_ = [ar.bf16(4, 64), ar.bf16(4, 64)]
    BTt_ = [ar.bf16(2, 128), ar.bf16(2, 128)]
    MSK_ = [ar.bf16(4, 128), ar.bf16(4, 128)]
    CDC_ = [ar.bf16(4, 128), ar.bf16(4, 128)]
    S = ar.f32(4, 64)
    Sb = ar.bf16(4, 64)
    TS_ = ar.f32(4, 64)
    YF = ar.f32(2, MT)
    ZS = ar.f32(MT)
    RS = ar.f32(MT)
    SQ = ar.bf16(MT)
    k.ms("dve", S, 0.0)
    k.ms("dve", Sb, 0.0)
    triu = c.cst("triu")
    for ti, (t0, t1) in enumerate(c.mtiles):
        n = t1 - t0
        nch = n // 128
        hnT = hnT2[ti % 2]
        ZB, XBC, XBCp = ZB_[ti % 2], XBC_[ti % 2], XBC_[(ti + 1) % 2]
        XCb = XCb_[ti % 2]
        hn_tile(c, l, t0, n, hnT, first)
        if ti == 0:
            k.ms("pool", XBC[:, :, 0:3], 0.0)
        else:
            k.cp("pool", XBC[:, :, 0:3], XBCp[:, :, MT:MT + 3])
        for ci in range(2):
            proj_F(c, Wm, hnT, n, ci * 128, ZB[:, ci, 0:n], "act")
        for ci in range(6):
            proj_F(c, Wm, hnT, n, 256 + ci * 128, XBC[:, ci, 3:3 + n], "act" if ci % 2 else "dve")
        for ci in range(6):
            conv4(c, "pool" if ci % 2 else "dve", ACC[:, 0:n], XBC[:, ci, :], n, "ssd_cw", "ssd_cb", ci)
            k.act(XCb[:, ci, 0:n], ACC[:, 0:n], AF.Silu)
        if ti == 0:
            k.ms("pool", XCb[:, 0:2, 0:PAD], 0.0)
        for ci in range(nch):
            co = ci * 128
            q2 = ci % 2
            D4, TA, T1, LT, EC = D4_[q2], TA_[q2], T1_[q2], LT_[q2], EC_[q2]
            XDT, XDC, BTt, MSK, CDC = XDT_[q2], XDC_[q2], BTt_[q2], MSK_[q2], CDC_[q2]
            pdt = c.bank()
            for kc in range(8):
                k.mm(pdt[:, 0:4], hnT[:, kc, co:co + 128], Wm[:, kc, 1024:1028], start=(kc == 0), stop=(kc == 7))
            k.tt("dve", D4[:, 0, :], pdt[:, 0:4], c.par("ssd_dtb", 0, 4), ALU.add)
            k.act(D4[:, 0, :], D4[:, 0, :], AF.Exp)
            k.act(D4[:, 1, :], D4[:, 0, :], AF.Ln, bias=c.ONE_AP)
            k.tt("dve", D4[:, 2, :], D4[:, 1, :], c.par("d_aneg", 0, 4), ALU.mult)
            pcs = c.bank()
            k.mm(pcs[:, 0:4], triu, D4[:, 2, :])
            k.ts("dve", D4[:, 3, :], pcs[:, 0:4], -1.0)
            k.tt("dve", TA, triu.unsqueeze(1).to_broadcast([128, 4, 128]), D4[:, 2, :].unsqueeze(2).to_broadcast([128, 4, 128]), ALU.mult)
            pcb = c.bank()
            k.mm(pcb[:, 0:512], c.ones_f, TA.rearrange("p a b -> p (a b)"))
            pcb3 = pcb[:, 0:512].rearrange("p (a b) -> p a b", a=4)
            k.tt("dve", T1, pcb3, c.cst("maskneg").unsqueeze(1).to_broadcast([128, 4, 128]), ALU.add)
            k.tt("dve", T1, T1, D4[:, 3, :].unsqueeze(2).to_broadcast([128, 4, 128]), ALU.add)
            k.act(LT, T1, AF.Exp)
            k.act(EC, pcb3, AF.Exp)
            k.tt("dve", D4[:, 7, :], D4[:, 3, :], pcb3[:, :, 127], ALU.add)
            k.act(D4[:, 4, :], D4[:, 7, :], AF.Exp)
            k.act(D4[:, 6, :], pcb3[:, :, 127], AF.Exp)
            k.tt("dve", D4[:, 5, :], D4[:, 1, :], D4[:, 4, :], ALU.mult)
            ptr = c.bank()
            ptb = ptr[:, 0:256].bitcast(BF16)
            for j in range(4):
                k.tr(ptb[:, j * 128:(j + 1) * 128], XCb[:, j, co:co + 128], c.ident_b)
            xT = ptb[:, 0:256].rearrange("p (a b) -> p a b", a=4)
            k.tt("dve", XDT, xT, D4[:, 1, :].unsqueeze(2).to_broadcast([128, 4, 64]), ALU.mult)
            k.tt("dve", XDC, xT, D4[:, 5, :].unsqueeze(2).to_broadcast([128, 4, 64]), ALU.mult)
            k.cp("act", BTt.rearrange("p a b -> p (a b)"), ptb[:, 256:512])
            psc = c.bank()
            for g in range(2):
                k.mm(psc[:, g * 128:(g + 1) * 128], XCb[:, 2 + g, co:co + 128], XCb[:, 4 + g, co:co + 128])
            sc4 = psc[:, 0:256].rearrange("p (a b) -> p a b", a=2).unsqueeze(2).to_broadcast([128, 2, 2, 128])
            k.tt("dve", MSK.rearrange("p (a b) c -> p a b c", a=2), sc4, LT.rearrange("p (a b) c -> p a b c", a=2), ALU.mult)
            c4 = XCb[:, 4:6, co:co + 128].unsqueeze(2).to_broadcast([128, 2, 2, 128])
            k.tt("pool", CDC.rearrange("p (a b) c -> p a b c", a=2), c4, EC.rearrange("p (a b) c -> p a b c", a=2), ALU.mult)
            py = c.bank()
            for hh in range(4):
                pc, po = hh // 2, 64 * (hh % 2)
                o = py[po:po + 64, pc * 128:(pc + 1) * 128]
                k.mm(o, XDT[:, hh, :], MSK[:, hh, :], start=True, stop=False)
                k.mm(o, Sb[:, hh, :], CDC[:, hh, :], start=False, stop=True)
            k.cp("act", YF[:, :, co:co + 128], py[:, 0:256].rearrange("p (a b) -> p a b", a=2))
            pst = c.bank()
            for hh in range(4):
                k.mm(pst[:, hh * 64:(hh + 1) * 64], BTt[:, hh // 2, :], XDC[:, hh, :])
            k.tt("dve", TS_, S, D4[:, 6, :].unsqueeze(2).to_broadcast([128, 4, 64]), ALU.mult)
            k.tt("dve", S, TS_, pst[:, 0:256].rearrange("p (a b) -> p a b", a=4), ALU.add)
            k.cp("act", Sb, S)
        for ci2 in range(2):
            y = YF[:, ci2, 0:n]
            k.stt("dve", y, XCb[:, ci2, 0:n], c.par("ssd_d", ci2), y, ALU.mult, ALU.add)
            k.act(ZS[:, 0:n], ZB[:, ci2, 0:n], AF.Silu)
            k.tt("dve", y, y, ZS[:, 0:n], ALU.mult)
            k.tt("pool", SQ[:, 0:n], y, y, ALU.mult)
            pb = c.bank()
            k.mm(pb[:, 0:n], c.ones_b, SQ[:, 0:n])
            k.act(RS[:, 0:n], pb[:, 0:n], AF.Ln, bias=c.EPS_AP, scale=1.0 / 128)
            k.act(RS[:, 0:n], RS[:, 0:n], AF.Exp, scale=-0.5)
            k.stt("dve", c.yT[:, ci2, t0:t1], y, c.par("ssd_nw", ci2), RS[:, 0:n], ALU.mult, ALU.mult)


def rwkv_pass(c, l, Wm, hnT, first):
    k, ar = c.k, c.ar
    if os.environ.get("K_RWSPLIT", "1") == "0":
        c.bank_mode[0] = "all"
    load_win(c, l, Wm, [(1028, 2052)])
    NC2 = MT // 128
    P8 = ar.f32(8, 1 + MT)
    HIST = ar.f32(8, 1)
    TMPS = ar.f32(2, MT)
    TW = ar.bf16(MT)
    SG = ar.bf16(MT)
    Vb = ar.bf16(2, MT)
    AV = ar.f32(1, MT)
    GG = ar.bf16(2, MT)
    KKn = ar.f32(1, MT)
    KMD = ar.f32(1, MT)
    BON = ar.f32(2, MT)
    SGW = ar.f32(MT)
    CUM = ar.f32(MT)
    PE_ = ar.f32(MT)
    TT = ar.f32(MT)
    PM = ar.f32(2, MT)
    SQb = c.sqbuf[1]
    AR = ar.bf16(2, NC2, 2, 128)
    Bt = ar.bf16(2, MT)
    Kt = ar.bf16(2, MT)
    TL_ = [ar.bf16(6, 128), ar.bf16(6, 128)]
    Sx_ = [[ar.bf16(4, 128), ar.bf16(4, 128)] for _ in range(2)]
    STx_ = [[ar.bf16(4, 128), ar.bf16(4, 128)] for _ in range(2)]
    PTx_ = [[ar.bf16(4, 128), ar.bf16(4, 128)] for _ in range(2)]
    MRB_ = [ar.bf16(4, 128), ar.bf16(4, 128)]
    AAK_ = [ar.bf16(4, 128), ar.bf16(4, 128)]
    MRK_ = [ar.bf16(4, 128), ar.bf16(4, 128)]
    XZ_ = [ar.bf16(256), ar.bf16(256)]
    U_ = [ar.bf16(256), ar.bf16(256)]
    H = ar.f32(2, 64)
    H0p = ar.f32(2, 64)
    Hb = ar.bf16(2, 64)
    YF = ar.f32(2, MT)
    k.ms("dve", H, 0.0)
    k.ms("dve", Hb, 0.0)
    msl, msu, miu = c.cst("msl"), c.cst("msu"), c.cst("miu")
    identb4 = c.ident_b.unsqueeze(1).to_broadcast([128, 4, 128])
    SMw = c.SM
    for ti, (t0, t1) in enumerate(c.mtiles):
        n = t1 - t0
        nch = n // 128
        hn_tile(c, l, t0, n, hnT, first)
        if ti == 0:
            k.ms("pool", P8[:, :, 0:1], 0.0)
        else:
            k.cp("pool", P8[:, :, 0:1], HIST)
        for ci in range(8):
            proj_F(c, Wm, hnT, n, ci * 128, P8[:, ci, 1:1 + n], "act" if ci % 2 else "dve")
        k.cp("pool", HIST, P8[:, :, n:n + 1])
        MX = P8[:, :, 1:1 + MT]
        for g4 in range(4):
            sl = slice(g4 * 2, g4 * 2 + 2)
            k.tt("dve", TMPS[:, :, 0:n], P8[:, sl, 0:n], P8[:, sl, 1:1 + n], ALU.subtract)
            k.tt("pool", TMPS[:, :, 0:n], TMPS[:, :, 0:n], c.par("rw_mu", g4 * 2, 2).unsqueeze(2).to_broadcast([128, 2, n]), ALU.mult)
            k.tt("dve", P8[:, sl, 1:1 + n], TMPS[:, :, 0:n], P8[:, sl, 1:1 + n], ALU.add)
        k.act(TW[0:64, 0:n], MX[0:64, 6, 0:n], AF.Tanh)
        k.cp("act", TW[64:128, 0:n], MX[64:128, 6, 0:n])
        k.act(SG[:, 0:n], MX[:, 7, 0:n], AF.Sigmoid)
        for ci in range(2):
            r_, k_, v_ = MX[:, ci, 0:n], MX[:, 2 + ci, 0:n], MX[:, 4 + ci, 0:n]
            k.cp("pool", Vb[:, ci, 0:n], v_)
            pw = c.bank()
            k.mm(pw[:, 0:n], SMw[0:64, ci * 128:(ci + 1) * 128], TW[0:64, 0:n])
            pa = c.bank()
            k.mm(pa[:, 0:n], SMw[64:128, ci * 128:(ci + 1) * 128], TW[64:128, 0:n])
            pg = c.bank()
            k.mm(pg[:, 0:n], SMw[:, 256 + ci * 128:256 + (ci + 1) * 128], SG[:, 0:n])
            k.act(SGW[:, 0:n], pw[:, 0:n], AF.Sigmoid, bias=c.par("rw_w0", ci))
            k.act(AV[:, 0, 0:n], pa[:, 0:n], AF.Sigmoid, bias=c.par("rw_a0", ci))
            k.cp("act", GG[:, ci, 0:n], pg[:, 0:n])
            kkn = KKn[:, 0, 0:n]
            k.ts("dve", kkn, k_, c.par("rw_kk", ci))
            k.tt("pool", SQb[:, 0:n], kkn, kkn, ALU.mult)
            pss = c.bank()
            k.mm(pss[:, 0:n], c.blk64_b, SQb[:, 0:n])
            k.ts("dve", TT[:, 0:n], pss[:, 0:n], 1e-24, None, ALU.max)
            k.act(TT[:, 0:n], TT[:, 0:n], AF.Ln)
            k.act(TT[:, 0:n], TT[:, 0:n], AF.Exp, scale=-0.5)
            k.tt("dve", kkn, kkn, TT[:, 0:n], ALU.mult)
            kmd = KMD[:, 0, 0:n]
            k.ts("dve", TT[:, 0:n], AV[:, 0, 0:n], c.par("rw_ka", ci), c.par("d_omka", ci), ALU.mult, ALU.add)
            k.tt("dve", kmd, k_, TT[:, 0:n], ALU.mult)
            k.stt("dve", SQb[:, 0:n], r_, c.par("rw_rk", ci), kmd, ALU.mult, ALU.mult)
            pbn = c.bank()
            k.mm(pbn[:, 0:n], c.blk64_b, SQb[:, 0:n])
            k.tt("dve", BON[:, ci, 0:n], pbn[:, 0:n], v_, ALU.mult)
            k.scan("dve", CUM[:, 0:n], c.cst("reset", 0, n), SGW[:, 0:n], 0.0)
            k.act(PE_[:, 0:n], CUM[:, 0:n], AF.Exp, scale=C_W)
            k.act(PM[:, ci, 0:n], CUM[:, 0:n], AF.Exp, scale=-C_W)
            k.tt("pool", TT[:, 0:n], CUM[:, 0:n], SGW[:, 0:n], ALU.subtract)
            k.act(TT[:, 0:n], TT[:, 0:n], AF.Exp, scale=-C_W)
            v3 = lambda a: a.rearrange("p (a b) -> p a b", a=nch)
            k.stt("dve", AR[:, ci, 0:nch, 0, :], v3(kkn), -1.0, v3(TT[:, 0:n]), ALU.mult, ALU.mult)
            k.tt("dve", AR[:, ci, 0:nch, 1, :], v3(r_), v3(PM[:, ci, 0:n]), ALU.mult)
            k.tt("pool", TT[:, 0:n], kkn, AV[:, 0, 0:n], ALU.mult)
            k.tt("dve", Bt[:, ci, 0:n], TT[:, 0:n], PE_[:, 0:n], ALU.mult)
            k.tt("dve", Kt[:, ci, 0:n], kmd, PE_[:, 0:n], ALU.mult)
        for ci in range(nch):
            co = ci * 128
            par2 = ci % 2
            TL, Sx, STx, PTx = TL_[par2], Sx_[par2], STx_[par2], PTx_[par2]
            MRB, AAK, MRK, XZ, U = MRB_[par2], AAK_[par2], MRK_[par2], XZ_[par2], U_[par2]
            ptr = c.bank()
            ptb = ptr[:, 0:384].bitcast(BF16)
            for j, src in enumerate((Vb, Bt, Kt)):
                for cc in range(2):
                    k.tr(ptb[:, (2 * j + cc) * 128:(2 * j + cc + 1) * 128], src[:, cc, co:co + 128], c.ident_b)
            k.cp("act", TL.rearrange("p a b -> p (a b)"), ptb)
            Vh = lambda hh: TL[:, hh // 2, 64 * (hh % 2):64 * (hh % 2) + 64]
            Bh = lambda hh: TL[:, 2 + hh // 2, 64 * (hh % 2):64 * (hh % 2) + 64]
            Kh = lambda hh: TL[:, 4 + hh // 2, 64 * (hh % 2):64 * (hh % 2) + 64]
            pA = c.bank(2)
            pB = c.bank(2)
            pC = c.bank(2)
            for hh in range(4):
                pc, po = hh // 2, 64 * (hh % 2)
                q, j = hh % 2, hh // 2
                At = AR[po:po + 64, pc, ci, 0, :]
                ARf = AR[po:po + 64, pc, ci, :, :].rearrange("p a b -> p (a b)")
                Bth = Bt[po:po + 64, pc, co:co + 128]
                Kth = Kt[po:po + 64, pc, co:co + 128]
                k.mm(pA[:, q * 512 + j * 128:q * 512 + (j + 1) * 128], At, Bth)
                k.mm(pB[:, q * 512 + j * 256:q * 512 + (j + 1) * 256], Bth, ARf)
                k.mm(pC[:, q * 512 + j * 256:q * 512 + (j + 1) * 256], Kth, ARf)
            S, ST, PT = Sx[0], STx[0], PTx[0]
            pA4 = pA.rearrange("p (q j b) -> p q j b", q=2, j=4)[:, :, 0:2, :]
            pB4 = pB.rearrange("p (a b c) -> p a b c", a=4, b=2)
            pC4 = pC.rearrange("p (a b c) -> p a b c", a=4, b=2)
            m3 = lambda m: m.unsqueeze(1).to_broadcast([128, 4, 128])
            m22 = lambda m: m.unsqueeze(1).unsqueeze(1).to_broadcast([128, 2, 2, 128])
            k.tt("dve", S.rearrange("p (q j) b -> p q j b", q=2), pA4, m22(msl), ALU.mult)
            k.tt("dve", ST, pB4[:, :, 0, :], m3(msu), ALU.mult)
            k.tt("dve", MRB, pB4[:, :, 1, :], m3(miu), ALU.mult)
            k.tt("dve", AAK, pC4[:, :, 0, :], m3(msu), ALU.mult)
            k.tt("dve", MRK, pC4[:, :, 1, :], m3(miu), ALU.mult)
            k.tt("pool", PT, ST, identb4, ALU.add)
            hp = lambda hh: (hh % 2) * 2 + hh // 2
            cur = 0
            for lev in range(6):
                nxt = 1 - cur
                S, ST, PT = Sx[cur], STx[cur], PTx[cur]
                Sn, STn, PTn = Sx[nxt], STx[nxt], PTx[nxt]
                pS = c.bank()
                for hh in range(4):
                    k.mm(pS[:, hp(hh) * 128:(hp(hh) + 1) * 128], ST[:, hp(hh), :], S[:, hp(hh), :])
                k.cp("act", Sn.rearrange("p a b -> p (a b)"), pS[:, 0:512])
                if lev < 5:
                    pT = c.bank()
                    for hh in range(4):
                        k.mm(pT[:, hp(hh) * 128:(hp(hh) + 1) * 128], S[:, hp(hh), :], ST[:, hp(hh), :])
                    k.cp("dve", STn.rearrange("p a b -> p (a b)"), pT[:, 0:512])
                pP = c.bank()
                for hh in range(4):
                    k.mm(pP[:, hp(hh) * 128:(hp(hh) + 1) * 128], Sn[:, hp(hh), :], PT[:, hp(hh), :])
                k.tt("dve", PTn.rearrange("p a b -> p (a b)"), pP[:, 0:512], PT.rearrange("p a b -> p (a b)"), ALU.add)
                cur = nxt
            PT = PTx[cur]
            pX = c.bank()
            for hh in range(4):
                pc, po = hh // 2, 64 * (hh % 2)
                o = pX[:, hh * 64:(hh + 1) * 64]
                k.mm(o, AAK[:, hp(hh), :], Vh(hh), start=True, stop=False)
                k.mm(o, AR[po:po + 64, pc, ci, 0, :], Hb[po:po + 64, pc, :], start=False, stop=True)
            k.cp("act", XZ, pX[:, 0:256])
            pU = c.bank()
            for hh in range(4):
                k.mm(pU[:, hh * 64:(hh + 1) * 64], PT[:, hp(hh), :], XZ[:, hh * 64:(hh + 1) * 64])
            k.cp("act", U, pU[:, 0:256])
            pY = c.bank()
            for hh in range(4):
                pc, po = hh // 2, 64 * (hh % 2)
                o = pY[po:po + 64, pc * 128:(pc + 1) * 128]
                k.mm(o, Hb[po:po + 64, pc, :], AR[po:po + 64, pc, ci, 1, :], start=True, stop=False)
                k.mm(o, U[:, hh * 64:(hh + 1) * 64], MRB[:, hp(hh), :], start=False, stop=False)
                k.mm(o, Vh(hh), MRK[:, hp(hh), :], start=False, stop=True)
            k.cp("act", YF[:, :, co:co + 128], pY[:, 0:256].rearrange("p (a b) -> p a b", a=2))
            pH = c.bank()
            for hh in range(4):
                pc, po = hh // 2, 64 * (hh % 2)
                o = pH[po:po + 64, pc * 64:(pc + 1) * 64]
                k.mm(o, Bh(hh), U[:, hh * 64:(hh + 1) * 64], start=True, stop=False)
                k.mm(o, Kh(hh), Vh(hh), start=False, stop=True)
            for pc in range(2):
                pl = PM[:, pc, co + 127:co + 128]
                k.ts("pool", H0p[:, pc, :], H[:, pc, :], pl)
                k.stt("dve", H[:, pc, :], pH[:, pc * 64:(pc + 1) * 64], pl, H0p[:, pc, :], ALU.mult, ALU.add)
            k.cp("act", Hb, H)
        for ci2 in range(2):
            y = YF[:, ci2, 0:n]
            head_norm_F(c, y, n, c.EPS_RW, TT[:, 0:n], CUM[:, 0:n])
            k.ts("dve", y, y, c.par("rw_lnw", ci2), c.par("rw_lnb", ci2), ALU.mult, ALU.add)
            k.tt("dve", y, y, BON[:, ci2, 0:n], ALU.add)
            k.tt("dve", c.yT[:, 2 + ci2, t0:t1], y, GG[:, ci2, 0:n], ALU.mult)


def post_norm_residual(c, O, tiles_local, t0g, wname, SQ2, RS, after_tile=None):
    k = c.k
    for (a, b) in tiles_local:
        n = b - a
        pb = c.bank()
        for m in range(8):
            sq = SQ2[m % 2]
            k.tt("pool" if m % 2 else "dve", sq[:, 0:n], O[:, m, a:b], O[:, m, a:b], ALU.mult)
            k.mm(pb[:, 0:n], c.ones_b, sq[:, 0:n], start=(m == 0), stop=(m == 7))
        k.act(RS[:, 0:n], pb[:, 0:n], AF.Ln, bias=c.EPS_AP, scale=1.0 / 1024)
        k.act(RS[:, 0:n], RS[:, 0:n], AF.Exp, scale=-0.5)
        lo = 0
        if t0g + a < PAD:
            lo = PAD - (t0g + a)
        for m in range(8):
            eng = "pool" if m % 2 else "dve"
            k.stt(eng, O[:, m, a + lo:b], O[:, m, a + lo:b], c.par(wname, m), RS[:, lo:n], ALU.mult, ALU.mult)
            k.tt(eng, c.h[:, m, t0g + a + lo:t0g + b], c.h[:, m, t0g + a + lo:t0g + b], O[:, m, a + lo:b], ALU.add)
        if after_tile is not None:
            after_tile(t0g + a, t0g + b)


def wout_phase(c, l):
    k, ar = c.k, c.ar
    c.bank_mode[0] = "all"
    ar.reset()
    Wo = ar.bf16(8, D)
    O_ = [ar.f32(8, 512), ar.f32(8, 512)]
    SQ2 = [ar.bf16(512), ar.bf16(512)]
    RS_ = [ar.f32(512), ar.f32(512)]
    for kc in range(8):
        k.dma_in("pool", Wo[:, kc, :], c.wout_d[l, :, kc, :])
    for ti, (t0, t1) in enumerate(tiles_of(0, T, 512)):
        n = t1 - t0
        O, RS = O_[ti % 2], RS_[ti % 2]
        for m in range(8):
            pb = c.bank()
            for kc in range(8):
                k.mm(pb[:, 0:n], Wo[:, kc, m * 128:(m + 1) * 128], c.yT[:, kc, t0:t1], start=(kc == 0), stop=(kc == 7))
            k.cp("act", O[:, m, 0:n], pb[:, 0:n])
        post_norm_residual(c, O, [(0, n)], t0, "post_mix", SQ2, RS,
                           after_tile=lambda ta, tb: c.rms_stats(lambda kc: c.h[:, kc, ta:tb], tb - ta, c.rstd_all[:, ta:tb], sqbuf=SQ2))


def ffn_phase(c, l):
    k, ar = c.k, c.ar
    ar.reset()
    GT = 768
    hn2x = [ar.bf16(8, GT), ar.bf16(8, GT)]
    Fo = ar.f32(8, GT)
    Wgu = [ar.bf16(2, 8, 128), ar.bf16(2, 8, 128), ar.bf16(2, 8, 128)]
    Wd = [ar.bf16(NJ, 128), ar.bf16(NJ, 128), ar.bf16(NJ, 128)]
    SQ2 = [ar.bf16(512), ar.bf16(512)]
    RS = ar.f32(512)
    SIL = [ar.f32(512), ar.f32(512)]
    aT = c.yT[:, :, :].rearrange("p a b -> p (a b)")[:, 0:NJ * GT].rearrange("p (a b) -> p a b", a=NJ)
    groups = tiles_of(0, T, GT)

    def make_hn2(gi):
        g0, g1 = groups[gi]
        hn2 = hn2x[gi % 2]
        for (a, b) in tiles_of(0, g1 - g0, 512):
            for kc in range(8):
                k.stt("dve", hn2[:, kc, a:b], c.h[:, kc, g0 + a:g0 + b], c.par("pre_ffn", kc), c.rstd_all[:, g0 + a:g0 + b], ALU.mult, ALU.mult)

    make_hn2(0)
    for gi, (g0, g1) in enumerate(groups):
        ng = g1 - g0
        lt = tiles_of(0, ng, 512)
        hn2 = hn2x[gi % 2]
        for j in range(NJ):
            W = Wgu[j % 3]
            k.dma_in("pool", W.rearrange("p a b c -> p (a b c)"), c.wgu_d[l, j, :, :, :, :].rearrange("p a b c -> p (a b c)"))
            for (a, b) in lt:
                n = b - a
                pg = c.bank()
                pu = c.bank()
                for kc in range(8):
                    k.mm(pg[:, 0:n], W[:, 0, kc, :], hn2[:, kc, a:b], start=(kc == 0), stop=(kc == 7))
                for kc in range(8):
                    k.mm(pu[:, 0:n], W[:, 1, kc, :], hn2[:, kc, a:b], start=(kc == 0), stop=(kc == 7))
                sl = SIL[(j + (a > 0)) % 2]
                k.act(sl[:, 0:n], pg[:, 0:n], AF.Silu)
                k.tt("dve", aT[:, j, a:b], sl[:, 0:n], pu[:, 0:n], ALU.mult)
        if gi + 1 < len(groups):
            make_hn2(gi + 1)
        for m in range(8):
            W = Wd[m % 3]
            k.dma_in("pool", W.rearrange("p a b -> p (a b)"), c.wd_d[l, m, :, :, :].rearrange("p a b -> p (a b)"))
            for (a, b) in lt:
                n = b - a
                pb = c.bank()
                for j in range(NJ):
                    k.mm(pb[:, 0:n], W[:, j, :], aT[:, j, a:b], start=(j == 0), stop=(j == NJ - 1))
                k.cp("act", Fo[:, m, a:b], pb[:, 0:n])
        nxt = None
        if l + 1 < c.L and not os.environ.get("K_NOPRE"):
            nxt = lambda ta, tb: c.rms_stats(lambda kc: c.h[:, kc, ta:tb], tb - ta, c.rstd_all[:, ta:tb], sqbuf=SQ2)
        post_norm_residual(c, Fo, lt, g0, "post_ffn", SQ2, RS, after_tile=nxt)


def _cols(v):
    v = np.asarray(v, np.float32)
    return np.ascontiguousarray(v.reshape(-1, 128).T)


def make_consts():
    Cn = np.zeros((128, NCONST), np.float32)
    i = np.arange(128)

    def put(name, a):
        o, w = CO[name]
        Cn[:, o:o + w] = a
    put("ident", np.eye(128))
    put("ones", np.ones((128, 128)))
    put("blk64", np.kron(np.eye(2), np.ones((64, 64))))
    put("triu", (i[:, None] <= i[None, :]).astype(np.float32))
    put("maskneg", np.where(i[None, :] >= i[:, None], 0.0, -30000.0))
    put("msl", (i[:, None] > i[None, :]).astype(np.float32))
    put("msu", (i[:, None] < i[None, :]).astype(np.float32))
    put("miu", (i[:, None] <= i[None, :]).astype(np.float32))
    log_g = np.log1p(-np.exp2(-5.0 - np.arange(4, dtype=np.float64)))
    lt = np.zeros((128, 4, 128))
    for hh in range(4):
        rel = i[None, :] - i[:, None]
        lt[:, hh, :] = np.where(rel >= 0, np.exp(np.maximum(rel, 0) * log_g[hh]), 0.0)
    put("ltT", lt.reshape(128, 512))
    kd = np.zeros((128, 128))
    qd = np.zeros((128, 128))
    g128 = np.zeros((128, 1))
    hm = np.zeros((128, 4))
    for hh in range(4):
        kd[:, hh * 32:(hh + 1) * 32] = np.exp((127 - i) * log_g[hh])[:, None]
        qd[hh * 32:(hh + 1) * 32, :] = np.exp((i + 1) * log_g[hh])[None, :]
        g128[hh * 32:(hh + 1) * 32, 0] = np.exp(128 * log_g[hh])
        hm[hh * 32:(hh + 1) * 32, hh] = 1.0
    put("kdec", kd)
    put("qdec", qd)
    put("g128", g128)
    put("hm", hm)
    rs = np.ones((128, 256))
    rs[:, 0] = 0.0
    rs[:, 128] = 0.0
    put("reset", rs)
    half = 16
    freqs = 10000.0 ** (-np.arange(half, dtype=np.float64) / half)
    pos = np.arange(T, dtype=np.float64) - PAD
    ang = pos[None, :] * freqs[:, None]
    cos = np.concatenate([np.cos(ang), np.cos(ang)], 0)
    sin = np.concatenate([-np.sin(ang), np.sin(ang)], 0)
    cos = np.tile(cos, (4, 1))
    sin = np.tile(sin, (4, 1))
    sc = 32 ** -0.5
    rope = np.stack([cos, sin, cos * sc, sin * sc]).astype(np.float32)
    rope[:, :, :PAD] = 0.0
    return Cn, rope


def prep_weights(inp, L):
    g = lambda n: np.asarray(inp[n], np.float32)
    w_in = g("w_in")[:L]
    ret0 = 2564
    idx = np.arange(128)
    sw = (idx // 32) * 32 + ((idx % 32) + 16) % 32
    qsw = w_in[:, :, ret0 + sw]
    ksw = w_in[:, :, ret0 + 128 + sw]
    w_in_p = np.concatenate([w_in, qsw, ksw], axis=2)
    w_in_p = np.ascontiguousarray(w_in_p.reshape(L, 8, 128, WIN_COLS))
    w_out = np.ascontiguousarray(g("w_out")[:L].reshape(L, 8, 128, D).transpose(0, 2, 1, 3))
    wg = g("ffn_w_gate")[:L].reshape(L, 8, 128, NJ, 128)
    wu = g("ffn_w_up")[:L].reshape(L, 8, 128, NJ, 128)
    wgu = np.stack([wg, wu], axis=0)
    wgu = np.ascontiguousarray(wgu.transpose(1, 4, 3, 0, 2, 5))
    wd = g("ffn_w_down")[:L].reshape(L, NJ, 128, 8, 128)
    wd = np.ascontiguousarray(wd.transpose(0, 3, 2, 1, 4))
    params = np.zeros((L, 128, NPAR), np.float32)
    smat = np.zeros((L, 128, 1024), np.float32)
    for l in range(L):
        def put(name, a):
            o, w = PO[name]
            params[l, :, o:o + w] = a
        put("pre_mix", _cols(g("pre_mix_norm")[l]))
        put("post_mix", _cols(g("post_mix_norm")[l]))
        put("pre_ffn", _cols(g("pre_ffn_norm")[l]))
        put("post_ffn", _cols(g("post_ffn_norm")[l]))
        cw = g("ssd_conv_w")[l]
        put("ssd_cw", np.concatenate([np.stack([cw[j, ci * 128:(ci + 1) * 128] for j in range(4)], 1) for ci in range(6)], 1))
        put("ssd_cb", _cols(g("ssd_conv_b")[l]))
        put("ssd_dtb", np.tile(g("ssd_dt_bias")[l][None, :], (128, 1)))
        put("ssd_alog", np.tile(g("ssd_a_log")[l][None, :], (128, 1)))
        put("ssd_d", _cols(np.repeat(g("ssd_d")[l], 64)))
        put("ssd_nw", _cols(g("ssd_norm_w")[l]))
        put("rw_mu", _cols(g("rwkv_mu")[l]))
        put("rw_w0", _cols(g("rwkv_w0")[l]))
        put("rw_a0", _cols(g("rwkv_a0")[l]))
        put("rw_kk", _cols(g("rwkv_k_k")[l]))
        put("rw_ka", _cols(g("rwkv_k_a")[l]))
        put("rw_rk", _cols(g("rwkv_r_k")[l].reshape(-1)))
        put("rw_lnw", _cols(g("rwkv_ln_w")[l]))
        put("rw_lnb", _cols(g("rwkv_ln_b")[l]))
        lw = g("lru_conv_w")[l]
        put("lru_cw", np.concatenate([np.stack([lw[j, ci * 128:(ci + 1) * 128] for j in range(4)], 1) for ci in range(2)], 1))
        put("lru_cb", _cols(g("lru_conv_b")[l]))
        put("lru_ba", _cols(g("lru_ba")[l]))
        put("lru_bx", _cols(g("lru_bx")[l]))
        put("lru_lam", _cols(g("lru_lambda")[l]))
        put("ret_gnw", _cols(g("ret_gn_w")[l]))
        smat[l, 0:64, 0:256] = g("rwkv_w2")[l]
        smat[l, 64:128, 0:256] = g("rwkv_a2")[l]
        smat[l, :, 256:512] = g("rwkv_g2")[l]
        for nm, off in (("lru_wa", 512), ("lru_wx", 768)):
            w = g(nm)[l]
            for b in range(4):
                ci, po = b // 2, 64 * (b % 2)
                smat[l, po:po + 64, off + ci * 128 + po:off + ci * 128 + po + 64] = w[b]
    return dict(w_in=w_in_p, w_out=w_out, w_gu=wgu, w_d=wd, params=params, smat=smat)


_CACHE = {}


def kernel(**inputs):
    x = np.asarray(inputs["x"], np.float32)
    B = x.shape[0]
    Cn, rope = make_consts()
    wts = prep_weights(inputs, DEPTH)
    if "nc" not in _CACHE:
        st = os.environ.get("K_STAGES")
        _CACHE["nc"] = build(DEPTH) if st is None else build(DEPTH, stages=tuple(x for x in st.split(",") if x))
    nc = _CACHE["nc"]
    meta = np.ascontiguousarray(np.asarray(inputs["meta_tokens"], np.float32))
    in_maps = []
    for b in range(B):
        m = dict(x=np.ascontiguousarray(x[b]), meta=meta, consts=Cn, rope=rope)
        m.update(wts)
        in_maps.append({k_: v_ for k_, v_ in m.items() if k_ in nc._in_names})
    res = run_bass_kernel_spmd(nc, in_maps, core_ids=list(range(B)))
    return np.stack([np.asarray(r["out"], np.float32) for r in res.results], axis=0)
```

```python
import math
import os
import numpy as np
import ml_dtypes
import concourse.bass as bass
import concourse.mybir as mybir
from concourse.bass_utils import run_bass_kernel_spmd

F32 = mybir.dt.float32
BF16 = mybir.dt.bfloat16
AF = mybir.ActivationFunctionType
ALU = mybir.AluOpType

D = 1024
SEQ = 2048
DEPTH = 4
NMETA = 16
T = 2176
NCH = 17
PAD = 112
MT = 256
DFF = 2816
NJ = 22
EPS = 1e-6
C_W = 0.6065306597126334

CO = {}
_o = 0
for _n, _w in [("ident", 128), ("ones", 128), ("blk64", 128), ("triu", 128), ("maskneg", 128),
               ("msl", 128), ("msu", 128), ("miu", 128), ("ltT", 512), ("kdec", 128), ("qdec", 128),
               ("g128", 1), ("hm", 4), ("reset", 256)]:
    CO[_n] = (_o, _w)
    _o += _w
NCONST = ((_o + 63) // 64) * 64

PO = {}
_o = 0
for _n, _w in [("pre_mix", 8), ("post_mix", 8), ("pre_ffn", 8), ("post_ffn", 8),
               ("ssd_cw", 24), ("ssd_cb", 6), ("ssd_dtb", 4), ("ssd_alog", 4), ("ssd_d", 2), ("ssd_nw", 2),
               ("rw_mu", 8), ("rw_w0", 2), ("rw_a0", 2), ("rw_kk", 2), ("rw_ka", 2), ("rw_rk", 2),
               ("rw_lnw", 2), ("rw_lnb", 2),
               ("lru_cw", 8), ("lru_cb", 2), ("lru_ba", 2), ("lru_bx", 2), ("lru_lam", 2), ("ret_gnw", 2),
               ("d_aneg", 4), ("d_omka", 2), ("d_m8sp", 2), ("d_tmp", 4)]:
    PO[_n] = (_o, _w)
    _o += _w
NPAR = ((_o + 15) // 16) * 16

WIN_COLS = 3588


def _isz(dt):
    return 2 if dt == BF16 else 4


class Prog:
    ROT = 6000

    def __init__(self, nc):
        self.nc = nc
        self.ops = []
        self.track = {}

    @staticmethod
    def box(ap):
        isz = _isz(ap.dtype)
        pat = ap.ap
        pstride = pat[0][0] * isz
        offb = ap.offset * isz
        if pstride <= 0:
            p0, f0 = 0, offb
        else:
            p0, f0 = offb // pstride, offb % pstride
        ext = 1
        for st, cnt in pat[1:]:
            ext += (cnt - 1) * abs(st)
        nm = ap.tensor.name
        p1, b0, b1 = p0 + pat[0][1], f0, f0 + ext * isz
        if nm == "PS":
            p0, p1 = 0, 128
            b0 = (b0 // 2048) * 2048
            b1 = ((b1 + 2047) // 2048) * 2048
        return (nm, p0, p1, b0, b1)

    def add(self, eng, fn, reads, writes, kind="c", cost=300.0, tag=None):
        i = len(self.ops)
        if isinstance(eng, (list, tuple)):
            cands = list(eng)
            fns, costs = fn, cost
            eng = cands[0]
        else:
            cands, fns, costs = [eng], {eng: fn}, {eng: cost}
        deps = set()
        for ap in reads:
            nm, p0, p1, b0, b1 = self.box(ap)
            lst = self.track.setdefault(nm, [])
            for (j, q0, q1, c0, c1, w, e) in lst:
                if (w or nm == "PS") and q0 < p1 and p0 < q1 and c0 < b1 and b0 < c1:
                    deps.add(j)
        for ap in writes:
            nm, p0, p1, b0, b1 = self.box(ap)
            lst = self.track.setdefault(nm, [])
            nowar = os.environ.get("K_NOWAR") and nm == "arena"
            for (j, q0, q1, c0, c1, w, e) in lst:
                if q0 < p1 and p0 < q1 and c0 < b1 and b0 < c1 and not (nowar):
                    deps.add(j)
        for ap in reads:
            nm, p0, p1, b0, b1 = self.box(ap)
            if nm in ("Cc", "Cb", "eps_t"):
                continue
            self.track[nm].append((i, p0, p1, b0, b1, False, eng))
        for ap in writes:
            nm, p0, p1, b0, b1 = self.box(ap)
            lst = self.track[nm]
            lst[:] = [x for x in lst if not (p0 <= x[1] and x[2] <= p1 and b0 <= x[3] and x[4] <= b1)]
            lst.append((i, p0, p1, b0, b1, True, eng))
        deps.discard(i)
        self.ops.append(dict(eng=eng, fn=fns[eng], deps=deps, kind=kind, cost=costs[eng], cands=cands, fns=fns, costs=costs, tag=tag))
        return i

    def schedule(self):
        ops = self.ops
        n = len(ops)
        succ = [[] for _ in range(n)]
        indeg = [0] * n
        for i, o in enumerate(ops):
            indeg[i] = len(o["deps"])
            for j in o["deps"]:
                succ[j].append(i)
        fin = [0.0] * n
        engs = sorted(set(e for o in ops for e in o["cands"]))
        free = {e: 0.0 for e in engs}
        order = {e: [] for e in engs}
        ready = {}
        pend = []

        def release(i):
            o = ops[i]
            r = {}
            for e in o["cands"]:
                t = 0.0
                for j in o["deps"]:
                    tj = fin[j] + (60.0 if ops[j]["eng"] == e else 200.0)
                    if tj > t:
                        t = tj
                r[e] = t
            ready[i] = r
            pend.append(i)
        for i in range(n):
            if indeg[i] == 0:
                release(i)
        done = 0
        WIN = int(os.environ.get("K_WIN", "600"))
        lo = 0
        sched = [False] * n
        cur_tab = [None]
        TABLD = 1283.0
        while done < n:
            best = None
            for i in pend:
                if i > lo + WIN:
                    continue
                o = ops[i]
                r = ready[i]
                be = None
                for e in o["cands"]:
                    st = r[e] if r[e] > free[e] else free[e]
                    if e == "act" and o["tag"] is not None and o["tag"] != cur_tab[0]:
                        st += TABLD
                    f = st + o["costs"][e]
                    if be is None or f < be[0]:
                        be = (f, st, e)
                key = (be[1], i)
                if best is None or key < best[0]:
                    best = (key, i, be[2], be[1])
            if best is None:
                i = min(pend)
                o = ops[i]
                e = o["cands"][0]
                st = max(free[e], ready[i][e])
            else:
                _, i, e, st = best
                o = ops[i]
            pend.remove(i)
            del ready[i]
            o["eng"] = e
            o["fn"] = o["fns"][e]
            o["cost"] = o["costs"][e]
            if e == "act" and o["tag"] is not None:
                cur_tab[0] = o["tag"]
            f = st + o["cost"]
            free[e] = st + (o["cost"] if o["kind"] != "dma" else 60.0)
            fin[i] = f
            order[e].append(i)
            sched[i] = True
            done += 1
            while lo < n and sched[lo]:
                lo += 1
            for k2 in succ[i]:
                indeg[k2] -= 1
                if indeg[k2] == 0:
                    release(k2)
        self.est_ns = max(fin) if n else 0.0
        return order

    def plan(self, out_dma_ops):
        class _S:
            def __init__(self, i):
                self.idx = i
        cnt = [0]
        def gen():
            while True:
                cnt[0] += 1
                yield _S(cnt[0] - 1)
        self.order = self.schedule()
        self._plan = self._assign(gen(), out_dma_ops)
        return cnt[0]

    def emit(self, block, sems, out_dma_ops):
        red, sv, prevdma = self._plan
        ops = self.ops
        rs = lambda t: None if t is None else (sems[t[0].idx], t[1])
        sv = [rs(t) for t in sv]
        prevdma = [rs(t) for t in prevdma]
        self._emit(block, red, sv, prevdma, out_dma_ops)

    def _assign(self, si, out_dma_ops):
        ops = self.ops
        n = len(ops)
        need_inc = [False] * n
        pos = [0] * n
        for e, lst in self.order.items():
            for p_, i in enumerate(lst):
                pos[i] = p_
        red = []
        for i, o in enumerate(ops):
            best = {}
            dl = []
            for j in o["deps"]:
                oj = ops[j]
                if oj["kind"] == "dma":
                    dl.append(j)
                else:
                    if oj["eng"] == "pe" and o["eng"] == "pe" and o["kind"] == "c":
                        assert pos[j] < pos[i]
                        continue
                    b = best.get(oj["eng"])
                    if b is None or pos[b] < pos[j]:
                        best[oj["eng"]] = j
            dl += list(best.values())
            for j in dl:
                need_inc[j] = True
            red.append(dl)
        for j in out_dma_ops:
            need_inc[j] = True
        eng_sems = {}
        cnt = {}
        dq = {}
        dcnt = {}
        NDQ = 8
        sv = [None] * n
        prevdma = [None] * n
        seq = [i for e in sorted(self.order) for i in self.order[e]]
        for i in seq:
            o = ops[i]
            e = o["eng"]
            if o["kind"] == "dma":
                if e not in dq:
                    dq[e] = [next(si) for _ in range(NDQ)]
                    dcnt[e] = [0] * NDQ
                    cnt[("d", e)] = 0
                k = cnt[("d", e)] % NDQ
                cnt[("d", e)] += 1
                if dcnt[e][k] >= 16 * 1500:
                    dq[e][k] = next(si)
                    dcnt[e][k] = 0
                prevdma[i] = (dq[e][k], dcnt[e][k])
                dcnt[e][k] += 16
                sv[i] = (dq[e][k], dcnt[e][k])
            elif need_inc[i]:
                c = cnt.get(e, 0)
                if c % self.ROT == 0:
                    eng_sems[e] = next(si)
                cnt[e] = c + 1
                sv[i] = (eng_sems[e], c % self.ROT + 1)
        return red, sv, prevdma

    def _emit(self, block, red, sv, prevdma, out_dma_ops):
        ops = self.ops
        engmap = {"pe": block.tensor, "act": block.scalar, "dve": block.vector, "pool": block.gpsimd,
                  "sp": block.sync}
        for ename, deco in engmap.items():
            def body(eng, ename=ename):
                waited = {}
                for i in self.order.get(ename, []):
                    o = ops[i]
                    for j in red[i]:
                        s, v = sv[j]
                        if waited.get(s.num if hasattr(s, "num") else id(s), 0) >= v:
                            continue
                        eng.wait_ge(s, v)
                        waited[s.num if hasattr(s, "num") else id(s)] = v
                    if o["kind"] == "dma":
                        ps, pv = prevdma[i]
                        key = ps.num if hasattr(ps, "num") else id(ps)
                        if pv > 0 and waited.get(key, 0) < pv:
                            eng.wait_ge(ps, pv)
                            waited[key] = pv
                        o["fn"](eng).then_inc(sv[i][0], 16)
                    else:
                        ins = o["fn"](eng)
                        if sv[i] is not None:
                            ins.then_inc(sv[i][0], 1)
                if ename == "sp":
                    for j in out_dma_ops:
                        s, v = sv[j]
                        eng.wait_ge(s, v)
            deco(body)


def _fn(ap):
    n = 1
    for d in ap.shape[1:]:
        n *= d
    return n


def _ec(eng, out, ins, kind="tt"):
    n = _fn(out)
    if eng == "act":
        return 225.0 + n * 0.85
    if eng == "dve":
        return 150.0 + n * 1.15
    if kind == "ts":
        return 1100.0 + n * 1.2
    if kind == "cp":
        return 220.0 + n * 1.0
    return 330.0 + n * 1.95


class K:
    def __init__(self, nc, prog):
        self.nc = nc
        self.p = prog

    def mm(self, out, lhsT, rhs, start=True, stop=True):
        rd = [lhsT, rhs] + ([] if start else [out])
        nn = max(_fn(rhs), 64) * (4 if _isz(rhs.dtype) == 4 else 1)
        self.p.add("pe", lambda e: e.matmul(out, lhsT, rhs, start=start, stop=stop), rd, [out], cost=57.0 + nn / 2.4)

    def tr(self, out, in_, ident):
        self.p.add("pe", lambda e: e.transpose(out, in_, ident), [in_, ident], [out], cost=90.0 * (2 if _isz(in_.dtype) == 4 else 1))

    def act(self, out, in_, func, bias=None, scale=None, eng="act"):
        rd = [in_]
        kw = {}
        if bias is not None:
            kw["bias"] = bias
            if not isinstance(bias, float):
                rd.append(bias)
        if scale is not None:
            kw["scale"] = scale
            if not isinstance(scale, float):
                rd.append(scale)
        tag = None if func in (AF.Copy, AF.Identity) else ("explog" if func in (AF.Exp, AF.Ln) else str(func))
        self.p.add("act", lambda e: e.activation(out=out, in_=in_, func=func, **kw), rd, [out], cost=_ec("act", out, [in_]), tag=tag)

    @staticmethod
    def _cands(out, ins, act_ok=False):
        ps = any(a.tensor.name == "PS" for a in list(ins) + [out])
        c = ["dve"] if (ps or not os.environ.get("K_POOLCOMPUTE")) else ["dve", "pool"]
        if act_ok:
            c.append("act")
        return c

    def _flex(self, cands, mk, out, ins, reads, kind="tt"):
        fns = {e: mk(e) for e in cands}
        costs = {e: _ec(e, out, ins, kind) for e in cands}
        self.p.add(cands, fns, reads, [out], cost=costs)

    def tt(self, eng, out, in0, in1, op):
        mk = lambda en: (lambda e: e.tensor_tensor(out=out, in0=in0, in1=in1, op=op))
        self._flex(self._cands(out, [in0, in1]), mk, out, [in0, in1], [in0, in1])

    def ts(self, eng, out, in0, s1, s2=None, op0=ALU.mult, op1=None):
        rd = [in0] + [s for s in (s1, s2) if s is not None and not isinstance(s, float)]
        if op1 is None:
            mk = lambda en: (lambda e: e.tensor_scalar(out=out, in0=in0, scalar1=s1, scalar2=None, op0=op0))
        else:
            mk = lambda en: (lambda e: e.tensor_scalar(out=out, in0=in0, scalar1=s1, scalar2=s2, op0=op0, op1=op1))
        self._flex(self._cands(out, rd), mk, out, [in0], rd, kind="ts")

    def stt(self, eng, out, in0, scalar, in1, op0, op1):
        eng = "dve"
        rd = [in0, in1] + ([] if isinstance(scalar, float) else [scalar])
        self.p.add(eng, lambda e: e.scalar_tensor_tensor(out=out, in0=in0, scalar=scalar, in1=in1, op0=op0, op1=op1), rd, [out], cost=_ec(eng, out, [in0, in1]))

    def cp(self, eng, out, in_):
        def mk(en):
            if en == "act":
                return lambda e: e.activation(out=out, in_=in_, func=AF.Copy)
            return lambda e: e.tensor_copy(out=out, in_=in_)
        self._flex(self._cands(out, [in_], act_ok=True), mk, out, [in_], [in_], kind="cp")

    def ms(self, eng, ap, val):
        mk = lambda en: (lambda e: e.memset(ap, val))
        self._flex(self._cands(ap, []), mk, ap, [], [], kind="cp")

    def scan(self, eng, out, d0, d1, init):
        eng = "dve"
        rd = [d0, d1] + ([] if isinstance(init, float) else [init])
        self.p.add(eng, lambda e: e.tensor_tensor_scan(out=out, data0=d0, data1=d1, initial=init, op0=ALU.mult, op1=ALU.add), rd, [out], cost=100.0 + _fn(out) / 0.5)

    def dma_in(self, q, out, in_):
        return self.p.add(q, lambda e: e.dma_start(out=out, in_=in_), [], [out], kind="dma", cost=2500.0 + _fn(out) * 128 * 4 / 320.0)

    def dma_out(self, q, out, in_):
        return self.p.add(q, lambda e: e.dma_start(out=out, in_=in_), [in_], [], kind="dma", cost=2500.0 + _fn(in_) * 128 * 4 / 320.0)


class Arena:
    def __init__(self, ap_f32, nwords):
        self.ap = ap_f32
        self.n = nwords
        self.off = 0

    def reset(self, off=0):
        if os.environ.get("K_ARDBG") and getattr(self, "hi", 0):
            print("   arena hi", self.hi, "of", self.n)
        self.hi = 0
        self.off = off

    def f32(self, *shape):
        n = int(np.prod(shape))
        self.hi = max(getattr(self, "hi", 0), self.off + n)
        assert self.off + n <= self.n, ("arena overflow", self.off, n, self.n)
        v = self.ap[:, self.off:self.off + n]
        self.off += n
        return self._shape(v, shape)

    def bf16(self, *shape):
        n = int(np.prod(shape))
        nw = (n + 1) // 2
        self.hi = max(getattr(self, "hi", 0), self.off + nw)
        assert self.off + nw <= self.n, ("arena overflow", self.off, nw, self.n)
        v = self.ap[:, self.off:self.off + nw].bitcast(BF16)[:, 0:n]
        self.off += nw
        return self._shape(v, shape)

    @staticmethod
    def _shape(v, shape):
        if len(shape) == 1:
            return v
        if len(shape) == 2:
            return v.rearrange("p (a b) -> p a b", a=shape[0])
        if len(shape) == 3:
            return v.rearrange("p (a b c) -> p a b c", a=shape[0], b=shape[1])
        if len(shape) == 4:
            return v.rearrange("p (a b c d) -> p a b c d", a=shape[0], b=shape[1], c=shape[2])
        raise ValueError


def tiles_of(t0, t1, step):
    out = []
    t = t0
    while t < t1:
        out.append((t, min(t + step, t1)))
        t += step
    return out


def build(n_layers, debug=False, stages=("ssd", "rwkv", "lru", "ret", "wout", "ffn")):
    nc = bass.Bass("TRN2", target_bir_lowering=False)
    L = n_layers
    dr = {}

    anymix = any(st in stages for st in ("ssd", "rwkv", "lru", "ret"))
    need = {"x": True, "meta": not os.environ.get("K_NOMETA"), "consts": True, "rope": "ret" in stages,
            "params": not os.environ.get("K_NOSETUP"), "smat": not os.environ.get("K_NOSETUP"), "w_in": anymix,
            "w_out": "wout" in stages, "w_gu": "ffn" in stages, "w_d": "ffn" in stages}

    def din(name, shape, dt=F32):
        if not need[name]:
            return None
        dr[name] = nc.dram_tensor(name, shape, dt, kind="ExternalInput").ap()
        return dr[name]

    x_d = din("x", [SEQ, D])
    meta_d = din("meta", [NMETA, D])
    consts_d = din("consts", [128, NCONST])
    rope_d = din("rope", [4, 128, T])
    par_d = din("params", [L, 128, NPAR])
    smat_d = din("smat", [L, 128, 1024])
    win_d = din("w_in", [L, 8, 128, WIN_COLS])
    wout_d = din("w_out", [L, 128, 8, D])
    wgu_d = din("w_gu", [L, NJ, 128, 2, 8, 128])
    wd_d = din("w_d", [L, 8, 128, NJ, 128])
    out_d = nc.dram_tensor("out", [SEQ, D], F32, kind="ExternalOutput").ap()
    if debug:
        dbg_d = nc.dram_tensor("dbg", [128, 8, T], BF16, kind="ExternalOutput").ap()
        dbgh_d = nc.dram_tensor("dbgh", [128, 8, T], F32, kind="ExternalOutput").ap()

    ARW = int(os.environ.get("K_ARW", "21900"))
    from contextlib import ExitStack
    with ExitStack() as es:
        def sb(name, shape, dt):
            return es.enter_context(nc.sbuf_tensor(name, shape, dt))
        h = sb("h", [128, 8, T], F32)
        yT = sb("yT", [128, 8, T], BF16)
        rstd_all = sb("rstd_all", [128, T], F32)
        Cc = sb("Cc", [128, NCONST], F32)
        Cb = sb("Cb", [128, 384], BF16)
        PRM = sb("PRM", [128, NPAR], F32)
        SM = sb("SM", [128, 1024], BF16)
        arena_t = sb("arena", [128, ARW], F32)
        EPS_T = sb("eps_t", [128, 4], F32)
        PS = es.enter_context(nc.psum_tensor("PS", [128, 4096], F32))

        prog = Prog(nc)
        k = K(nc, prog)
        ar = Arena(arena_t[:, :], ARW)
        out_ops = []

        def cst(name, a=None, b=None):
            o, w = CO[name]
            if a is None:
                return Cc[:, o:o + w]
            return Cc[:, o + a:o + b]

        def par(name, c=0, w=1):
            o, _ = PO[name]
            return PRM[:, o + c:o + c + w]

        ident_f = cst("ident")
        ones_f = cst("ones")
        blk64_f = cst("blk64")
        ident_b = Cb[:, 0:128]
        ones_b = Cb[:, 128:256]
        blk64_b = Cb[:, 256:384]

        psn = {"all": 0, "proj": 0, "chain": 0}
        bank_mode = ["all"]

        def bank(nb=1, cls="chain"):
            if bank_mode[0] == "all":
                lo_, hi_ = 0, 8
                key = "all"
            elif cls == "proj":
                lo_, hi_ = 0, NPROJ_BANKS
                key = "proj"
            else:
                lo_, hi_ = NPROJ_BANKS, 8
                key = "chain"
            b = psn[key]
            if b < lo_ or b + nb > hi_:
                b = lo_
            psn[key] = b + nb
            return PS[:, b * 512:(b + nb) * 512]

        NPROJ_BANKS = int(os.environ.get("K_NPB", "2"))
        if debug:
            k.ms("pool", yT[:, :, :], 0.0)
        k.dma_in("sp", Cc[:, :], consts_d[:, :])
        k.dma_in("pool", Cb[:, :], consts_d[:, 0:384])

        ar.reset()
        stg = [ar.f32(D), ar.f32(D)]
        for ch in range(NCH):
            s = stg[ch % 2]
            if ch == 0:
                k.ms("pool", s, 0.0)
                if not os.environ.get("K_NOMETA"):
                    k.dma_in("sp", s[PAD:128, :], meta_d[:, :])
            else:
                k.dma_in("sp", s, x_d[(ch - 1) * 128:ch * 128, :])
            for half in range(2):
                pb = bank()
                for kk4 in range(4):
                    kc = half * 4 + kk4
                    k.tr(pb[:, kk4 * 128:(kk4 + 1) * 128], s[:, kc * 128:(kc + 1) * 128], ident_f)
                for kk4 in range(4):
                    kc = half * 4 + kk4
                    k.cp(os.environ.get("K_CPENG") or (("dve" if kk4 % 2 else "act") if os.environ.get("K_SWAP") else ("act" if kk4 % 2 else "dve")), h[:, kc, ch * 128:(ch + 1) * 128], pb[:, kk4 * 128:(kk4 + 1) * 128])

        def rms_stats(src_of_k, n, rs_out, nfeat_chunks=8, denom=1024.0, sqbuf=None):
            pb = bank()
            for kc in range(nfeat_chunks):
                sq = sqbuf[kc % 2]
                src = src_of_k(kc)
                eng = "pool" if kc % 2 else "dve"
                k.tt(eng, sq[:, 0:n], src, src, ALU.mult)
                k.mm(pb[:, 0:n], ones_b, sq[:, 0:n], start=(kc == 0), stop=(kc == nfeat_chunks - 1))
            k.act(rs_out, pb[:, 0:n], AF.Ln, bias=EPS_AP, scale=1.0 / denom)
            k.act(rs_out, rs_out, AF.Exp, scale=-0.5)

        def layer_setup(l):
            k.dma_in("sp", PRM[:, 0:PO["d_aneg"][0]], par_d[l, :, 0:PO["d_aneg"][0]])
            k.dma_in("pool", SM[:, :], smat_d[l, :, :])
            k.act(par("d_aneg", 0, 4), par("ssd_alog", 0, 4), AF.Exp)
            k.ts("dve", par("d_aneg", 0, 4), par("d_aneg", 0, 4), -1.0)
            k.ts("dve", par("d_omka", 0, 2), par("rw_ka", 0, 2), -1.0, 1.0, ALU.mult, ALU.add)
            k.act(par("d_m8sp", 0, 2), par("lru_lam", 0, 2), AF.Exp, scale=-1.0)
            k.act(par("d_m8sp", 0, 2), par("d_m8sp", 0, 2), AF.Ln, bias=ONE_AP)
            k.ts("dve", par("d_m8sp", 0, 2), par("d_m8sp", 0, 2), -8.0)

        if not os.environ.get("K_NOEPS"):
            k.ms("dve", EPS_T[:, 0:1], EPS)
            k.ms("dve", EPS_T[:, 1:2], 1.0)
            k.ms("dve", EPS_T[:, 2:3], 64e-5)
            k.ms("dve", EPS_T[:, 3:4], 1e-5)
        EPS_AP = EPS_T[:, 0:1]
        ONE_AP = EPS_T[:, 1:2]
        EPS_RW = EPS_T[:, 2:3]
        EPS_RET = EPS_T[:, 3:4]

        mtiles = tiles_of(0, T, MT)

        from types import SimpleNamespace
        ctx = SimpleNamespace(**locals())

        for l in range(L):
            if not os.environ.get("K_NOSETUP"):
                layer_setup(l)
            if "mix" in stages or any(s in stages for s in ("ssd", "rwkv", "lru", "ret")):
                mixer_phase(ctx, l, stages)
            if "wout" in stages:
                wout_phase(ctx, l)
            if "ffn" in stages:
                ffn_phase(ctx, l)
            if debug and l == 0:
                out_ops.append(k.dma_out("sp", dbg_d[:, :, :], yT[:, :, :]))
                out_ops.append(k.dma_out("sp", dbgh_d[:, :, :], h[:, :, :]))

        ar.reset()
        stg = [ar.f32(D), ar.f32(D)]
        for ch in range(1, NCH):
            s = stg[ch % 2]
            for half in range(2):
                pb = bank()
                for kk4 in range(4):
                    kc = half * 4 + kk4
                    k.tr(pb[:, kk4 * 128:(kk4 + 1) * 128], h[:, kc, ch * 128:(ch + 1) * 128], ident_f)
                k.cp("act" if half else "dve", s[:, half * 512:(half + 1) * 512], pb[:, 0:512])
            out_ops.append(k.dma_out("sp", out_d[(ch - 1) * 128:ch * 128, :], s))

        mx = int(os.environ.get("K_MAXOPS", "0"))
        if mx:
            prog.ops = prog.ops[:mx]
            out_ops = [j for j in out_ops if j < mx]
        nsem = prog.plan(out_ops)
        print("ops", len(prog.ops), "sems", nsem, "est_us", round(prog.est_ns / 1000.0, 1), flush=True)
        sems = [es.enter_context(nc.semaphore("s%d" % i)) for i in range(nsem)]
        block = es.enter_context(nc.Block())
        prog.emit(block, sems, out_ops)
    nc._in_names = list(dr.keys())
    return nc


def hn_tile(c, l, t0, n, hnT, first_pass, which="pre_mix"):
    k = c.k
    if first_pass:
        c.rms_stats(lambda kc: c.h[:, kc, t0:t0 + n], n, c.rstd_all[:, t0:t0 + n], sqbuf=c.sqbuf)
    for kc in range(8):
        eng = "pool" if kc % 2 else "dve"
        k.stt(eng, hnT[:, kc, 0:n], c.h[:, kc, t0:t0 + n], c.par(which, kc), c.rstd_all[:, t0:t0 + n], ALU.mult, ALU.mult)


def load_win(c, l, Wm, ranges):
    lo = 0
    for (a, b) in ranges:
        for kc in range(8):
            c.k.dma_in("pool", Wm[:, kc, lo:lo + (b - a)], c.win_d[l, kc, :, a:b])
        lo += b - a


def proj_F(c, Wm, hnT, n, col0, dst, evac="act"):
    pb = c.bank(cls="proj")
    for kc in range(8):
        c.k.mm(pb[:, 0:n], Wm[:, kc, col0:col0 + 128], hnT[:, kc, 0:n], start=(kc == 0), stop=(kc == 7))
    c.k.cp(evac, dst, pb[:, 0:n])


def head_norm_F(c, YF, n, eps_ap, tmp, rs):
    k = c.k
    pb = c.bank()
    k.mm(pb[:, 0:n], c.blk64_f, YF, start=True, stop=True)
    k.stt("dve", YF, pb[:, 0:n], -1.0 / 64, YF, ALU.mult, ALU.add)
    k.tt("pool", tmp, YF, YF, ALU.mult)
    pb2 = c.bank()
    k.mm(pb2[:, 0:n], c.blk64_f, tmp, start=True, stop=True)
    k.act(rs, pb2[:, 0:n], AF.Ln, bias=eps_ap, scale=1.0 / 64)
    k.act(rs, rs, AF.Exp, scale=-0.5)
    k.tt("dve", YF, YF, rs, ALU.mult)


def mixer_phase(c, l, stages):
    k, ar = c.k, c.ar
    c.bank_mode[0] = "split"
    ar.reset()
    hnT2 = [ar.bf16(8, MT), ar.bf16(8, MT)]
    hnT = hnT2[0]
    c.sqbuf = [ar.bf16(512), ar.bf16(512)]
    base0 = ar.off
    first = (l == 0) or ("ffn" not in stages) or bool(os.environ.get("K_NOPRE"))
    if "lru" in stages and "ret" in stages:
        ar.reset(base0)
        Wm = ar.bf16(8, 1536)
        load_win(c, l, Wm, [(2052, 2564), (2564, 3332), (3332, 3588)])
        lt = lru_pass(c, l, Wm, hnT2, None, wo=0)
        rt = ret_pass(c, l, Wm, hnT2, None, wo=512)
        for ti, (t0, t1) in enumerate(c.mtiles):
            hn_tile(c, l, t0, t1 - t0, hnT2[ti % 2], first)
            lt(ti, t0, t1)
            rt(ti, t0, t1)
        first = False
        todo = ("ssd", "rwkv")
    else:
        todo = ("lru", "ret", "ssd", "rwkv")
    for name in todo:
        if name not in stages:
            continue
        ar.reset(base0)
        Wm = ar.bf16(8, 1028)
        if name == "lru":
            load_win(c, l, Wm, [(2052, 2564)])
            lt = lru_pass(c, l, Wm, hnT2, None, wo=0)
            for ti, (t0, t1) in enumerate(c.mtiles):
                hn_tile(c, l, t0, t1 - t0, hnT2[ti % 2], first)
                lt(ti, t0, t1)
        elif name == "ret":
            load_win(c, l, Wm, [(2564, 3332), (3332, 3588)])
            rt = ret_pass(c, l, Wm, hnT2, None, wo=0)
            for ti, (t0, t1) in enumerate(c.mtiles):
                hn_tile(c, l, t0, t1 - t0, hnT2[ti % 2], first)
                rt(ti, t0, t1)
        elif name == "ssd":
            ssd_pass(c, l, Wm, hnT2, first)
        elif name == "rwkv":
            rwkv_pass(c, l, Wm, hnT, first)
        first = False


def conv4(c, eng, out, xbuf, n, wname, bname, ci):
    k = c.k
    k.ts(eng, out, xbuf[:, 3:3 + n], c.par(wname, ci * 4 + 3), c.par(bname, ci), ALU.mult, ALU.add)
    for j in (2, 1, 0):
        k.stt(eng, out, xbuf[:, j:j + n], c.par(wname, ci * 4 + j), out, ALU.mult, ALU.add)


def lru_pass(c, l, Wm, hnT, first, wo=0):
    k, ar = c.k, c.ar
    XB_ = [ar.f32(2, 3 + MT), ar.f32(2, 3 + MT)]
    GB_ = [ar.f32(2, MT), ar.f32(2, MT)]
    XC = ar.f32(2, MT)
    XCb = ar.bf16(2, MT)
    RG = ar.f32(MT)
    IG = ar.f32(MT)
    AA = ar.f32(MT)
    UU = ar.f32(MT)
    HT = ar.f32(MT)
    hlast = ar.f32(2)
    k.ms("dve", hlast, 0.0)

    hnT2 = hnT

    def tile(ti, t0, t1):
        hnT = hnT2[ti % 2]
        XB, GB, XBp = XB_[ti % 2], GB_[ti % 2], XB_[(ti + 1) % 2]
        n = t1 - t0
        if ti == 0:
            k.ms("pool", XB[:, :, 0:3], 0.0)
        else:
            k.cp("pool", XB[:, :, 0:3], XBp[:, :, MT:MT + 3])
        for ci in range(2):
            proj_F(c, Wm, hnT, n, wo + ci * 128, XB[:, ci, 3:3 + n], "act")
            proj_F(c, Wm, hnT, n, wo + 256 + ci * 128, GB[:, ci, 0:n], "act")
        for ci in range(2):
            conv4(c, "dve" if ci else "pool", XC[:, ci, 0:n], XB[:, ci, :], n, "lru_cw", "lru_cb", ci)
            k.cp("act", XCb[:, ci, 0:n], XC[:, ci, 0:n])
            pr = c.bank()
            k.mm(pr[:, 0:n], c.SM[:, 512 + ci * 128:512 + (ci + 1) * 128], XCb[:, ci, 0:n])
            pi = c.bank()
            k.mm(pi[:, 0:n], c.SM[:, 768 + ci * 128:768 + (ci + 1) * 128], XCb[:, ci, 0:n])
            k.act(RG[:, 0:n], pr[:, 0:n], AF.Sigmoid, bias=c.par("lru_ba", ci))
            k.act(IG[:, 0:n], pi[:, 0:n], AF.Sigmoid, bias=c.par("lru_bx", ci))
            k.act(AA[:, 0:n], RG[:, 0:n], AF.Exp, scale=c.par("d_m8sp", ci))
            k.tt("pool", UU[:, 0:n], AA[:, 0:n], AA[:, 0:n], ALU.mult)
            k.act(UU[:, 0:n], UU[:, 0:n], AF.Sqrt, bias=c.ONE_AP, scale=-1.0)
            k.tt("dve", UU[:, 0:n], UU[:, 0:n], IG[:, 0:n], ALU.mult)
            k.tt("dve", UU[:, 0:n], UU[:, 0:n], XC[:, ci, 0:n], ALU.mult)
            if ti == 0:
                k.ms("dve", UU[:, 0:PAD], 0.0)
            k.scan("dve", HT[:, 0:n], AA[:, 0:n], UU[:, 0:n], hlast[:, ci:ci + 1])
            k.cp("dve", hlast[:, ci:ci + 1], HT[:, n - 1:n])
            k.act(IG[:, 0:n], GB[:, ci, 0:n], AF.Gelu_apprx_tanh)
            k.tt("dve", c.yT[:, 4 + ci, t0:t1], HT[:, 0:n], IG[:, 0:n], ALU.mult)
    return tile


def ret_pass(c, l, Wm, hnT, first, wo=0):
    k, ar = c.k, c.ar
    PRj_ = [ar.f32(6, MT), ar.f32(6, MT)]
    ROP = ar.f32(4, MT)
    T1 = ar.f32(MT)
    T2 = ar.f32(MT)
    QR_ = [ar.bf16(MT), ar.bf16(MT)]
    KR_ = [ar.bf16(MT), ar.bf16(MT)]
    QD_ = [ar.bf16(MT), ar.bf16(MT)]
    KM_ = [ar.bf16(4, MT), ar.bf16(4, MT)]
    Vt_ = [ar.bf16(2, 256), ar.bf16(2, 256)]
    KD = ar.bf16(128)
    MS = ar.bf16(4, 128)
    YF = ar.f32(2, MT)
    Rm = ar.f32(4, 64)
    Rmb = ar.bf16(4, 64)
    TK = ar.f32(4, 64)
    k.ms("dve", Rm, 0.0)
    k.ms("dve", Rmb, 0.0)
    hm = c.cst("hm")
    srcs = [wo + 0, wo + 128, wo + 512, wo + 640, wo + 768, wo + 896]

    hnT2 = hnT

    def tile(ti, t0, t1):
        hnT = hnT2[ti % 2]
        PRj = PRj_[ti % 2]
        QR, KR, QD, KM, Vt = QR_[ti % 2], KR_[ti % 2], QD_[ti % 2], KM_[ti % 2], Vt_[ti % 2]
        n = t1 - t0
        nch = n // 128
        k.dma_in("sp", ROP[:, :, 0:n], c.rope_d[:, :, t0:t1].rearrange("a p t -> p a t"))
        for i, col in enumerate(srcs):
            proj_F(c, Wm, hnT, n, col, PRj[:, i, 0:n], "act" if i % 2 else "dve")
        for ci in range(nch):
            pv = c.bank()
            for kc in range(8):
                k.mm(pv[:, 0:256], hnT[:, kc, ci * 128:(ci + 1) * 128], Wm[:, kc, wo + 256:wo + 512], start=(kc == 0), stop=(kc == 7))
            k.cp("act", Vt[:, ci, :], pv[:, 0:256])
        k.tt("dve", T1[:, 0:n], PRj[:, 0, 0:n], ROP[:, 0, 0:n], ALU.mult)
        k.tt("pool", T2[:, 0:n], PRj[:, 4, 0:n], ROP[:, 1, 0:n], ALU.mult)
        k.tt("dve", T1[:, 0:n], T1[:, 0:n], T2[:, 0:n], ALU.add)
        k.cp("act", QR[:, 0:n], T1[:, 0:n])
        k.tt("dve", QD[:, 0:n].rearrange("p (a b) -> p a b", a=nch), T1[:, 0:n].rearrange("p (a b) -> p a b", a=nch),
             c.cst("qdec").unsqueeze(1).to_broadcast([128, nch, 128]), ALU.mult)
        k.tt("dve", T1[:, 0:n], PRj[:, 1, 0:n], ROP[:, 2, 0:n], ALU.mult)
        k.tt("pool", T2[:, 0:n], PRj[:, 5, 0:n], ROP[:, 3, 0:n], ALU.mult)
        k.tt("dve", T1[:, 0:n], T1[:, 0:n], T2[:, 0:n], ALU.add)
        k.cp("act", KR[:, 0:n], T1[:, 0:n])
        k.tt("dve", KM[:, :, 0:n], T1[:, 0:n].unsqueeze(1).to_broadcast([128, 4, n]),
             hm.unsqueeze(2).to_broadcast([128, 4, n]), ALU.mult)
        for ci in range(nch):
            co = ci * 128
            pt = c.bank()
            ptb = pt[:, 0:64].bitcast(BF16)
            k.tr(ptb, KR[:, co:co + 128], c.ident_b)
            k.tt("dve", KD, ptb, c.cst("kdec"), ALU.mult)
            psc = c.bank()
            for hh in range(4):
                k.mm(psc[:, hh * 128:(hh + 1) * 128], KM[:, hh, co:co + 128], QR[:, co:co + 128])
            k.tt("dve", MS.rearrange("p a b -> p (a b)"), psc[:, 0:512], c.cst("ltT"), ALU.mult)
            py = c.bank()
            for hh in range(4):
                pc, po = hh // 2, 64 * (hh % 2)
                o = py[po:po + 64, pc * 128:(pc + 1) * 128]
                k.mm(o, Vt[:, ci, hh * 64:(hh + 1) * 64], MS[:, hh, :], start=True, stop=False)
                k.mm(o, Rmb[:, hh, :], QD[:, co:co + 128], start=False, stop=True)
            k.cp("act", YF[:, :, co:co + 128], py[:, 0:256].rearrange("p (a b) -> p a b", a=2))
            pk = c.bank()
            for hh in range(4):
                k.mm(pk[:, hh * 64:(hh + 1) * 64], KD, Vt[:, ci, hh * 64:(hh + 1) * 64])
            k.tt("dve", TK, pk[:, 0:256].rearrange("p (a b) -> p a b", a=4), hm.unsqueeze(2).to_broadcast([128, 4, 64]), ALU.mult)
            k.stt("dve", Rm, Rm, c.cst("g128"), TK, ALU.mult, ALU.add)
            k.cp("act", Rmb, Rm)
        for ci2 in range(2):
            head_norm_F(c, YF[:, ci2, 0:n], n, c.EPS_RET, T1[:, 0:n], T2[:, 0:n])
            k.act(T1[:, 0:n], PRj[:, 2 + ci2, 0:n], AF.Silu)
            k.stt("dve", c.yT[:, 6 + ci2, t0:t1], YF[:, ci2, 0:n], c.par("ret_gnw", ci2), T1[:, 0:n], ALU.mult, ALU.mult)
    return tile


def ssd_pass(c, l, Wm, hnT, first):
    k, ar = c.k, c.ar
    load_win(c, l, Wm, [(0, 1028)])
    hnT2 = hnT
    ZB_ = [ar.f32(2, MT), ar.f32(2, MT)]
    XBC_ = [ar.f32(6, 3 + MT), ar.f32(6, 3 + MT)]
    ACC = ar.f32(MT)
    XCb_ = [ar.bf16(6, MT), ar.bf16(6, MT)]
    D4_ = [ar.f32(8, 4), ar.f32(8, 4)]
    TA_ = [ar.f32(4, 128), ar.f32(4, 128)]
    T1_ = [ar.f32(4, 128), ar.f32(4, 128)]
    LT_ = [ar.f32(4, 128), ar.f32(4, 128)]
    EC_ = [ar.f32(4, 128), ar.f32(4, 128)]
    XDT_ = [ar.bf16(4, 64), ar.bf16(4, 64)]
    XDC_ = [ar.bf16(4, 64), ar.bf16(4, 64)]
    BTt_ = [ar.bf16(2, 128), ar.bf16(2, 128)]
    MSK_ = [ar.bf16(4, 128), ar.bf16(4, 128)]
    CDC_ = [ar.bf16(4, 128), ar.bf16(4, 128)]
    S = ar.f32(4, 64)
    Sb = ar.bf16(4, 64)
    TS_ = ar.f32(4, 64)
    YF = ar.f32(2, MT)
    ZS = ar.f32(MT)
    RS = ar.f32(MT)
    SQ = ar.bf16(MT)
    k.ms("dve", S, 0.0)
    k.ms("dve", Sb, 0.0)
    triu = c.cst("triu")
    for ti, (t0, t1) in enumerate(c.mtiles):
        n = t1 - t0
        nch = n // 128
        hnT = hnT2[ti % 2]
        ZB, XBC, XBCp = ZB_[ti % 2], XBC_[ti % 2], XBC_[(ti + 1) % 2]
        XCb = XCb_[ti % 2]
        hn_tile(c, l, t0, n, hnT, first)
        if ti == 0:
            k.ms("pool", XBC[:, :, 0:3], 0.0)
        else:
            k.cp("pool", XBC[:, :, 0:3], XBCp[:, :, MT:MT + 3])
        for ci in range(2):
            proj_F(c, Wm, hnT, n, ci * 128, ZB[:, ci, 0:n], "act")
        for ci in range(6):
            proj_F(c, Wm, hnT, n, 256 + ci * 128, XBC[:, ci, 3:3 + n], "act" if ci % 2 else "dve")
        for ci in range(6):
            conv4(c, "pool" if ci % 2 else "dve", ACC[:, 0:n], XBC[:, ci, :], n, "ssd_cw", "ssd_cb", ci)
            k.act(XCb[:, ci, 0:n], ACC[:, 0:n], AF.Silu)
        if ti == 0:
            k.ms("pool", XCb[:, 0:2, 0:PAD], 0.0)
        for ci in range(nch):
            co = ci * 128
            q2 = ci % 2
            D4, TA, T1, LT, EC = D4_[q2], TA_[q2], T1_[q2], LT_[q2], EC_[q2]
            XDT, XDC, BTt, MSK, CDC = XDT_[q2], XDC_[q2], BTt_[q2], MSK_[q2], CDC_[q2]
            pdt = c.bank()
            for kc in range(8):
                k.mm(pdt[:, 0:4], hnT[:, kc, co:co + 128], Wm[:, kc, 1024:1028], start=(kc == 0), stop=(kc == 7))
            k.tt("dve", D4[:, 0, :], pdt[:, 0:4], c.par("ssd_dtb", 0, 4), ALU.add)
            k.act(D4[:, 0, :], D4[:, 0, :], AF.Exp)
            k.act(D4[:, 1, :], D4[:, 0, :], AF.Ln, bias=c.ONE_AP)
            k.tt("dve", D4[:, 2, :], D4[:, 1, :], c.par("d_aneg", 0, 4), ALU.mult)
            pcs = c.bank()
            k.mm(pcs[:, 0:4], triu, D4[:, 2, :])
            k.ts("dve", D4[:, 3, :], pcs[:, 0:4], -1.0)
            k.tt("dve", TA, triu.unsqueeze(1).to_broadcast([128, 4, 128]), D4[:, 2, :].unsqueeze(2).to_broadcast([128, 4, 128]), ALU.mult)
            pcb = c.bank()
            k.mm(pcb[:, 0:512], c.ones_f, TA.rearrange("p a b -> p (a b)"))
            pcb3 = pcb[:, 0:512].rearrange("p (a b) -> p a b", a=4)
            k.tt("dve", T1, pcb3, c.cst("maskneg").unsqueeze(1).to_broadcast([128, 4, 128]), ALU.add)
            k.tt("dve", T1, T1, D4[:, 3, :].unsqueeze(2).to_broadcast([128, 4, 128]), ALU.add)
            k.act(LT, T1, AF.Exp)
            k.act(EC, pcb3, AF.Exp)
            k.tt("dve", D4[:, 7, :], D4[:, 3, :], pcb3[:, :, 127], ALU.add)
            k.act(D4[:, 4, :], D4[:, 7, :], AF.Exp)
            k.act(D4[:, 6, :], pcb3[:, :, 127], AF.Exp)
            k.tt("dve", D4[:, 5, :], D4[:, 1, :], D4[:, 4, :], ALU.mult)
            ptr = c.bank()
            ptb = ptr[:, 0:256].bitcast(BF16)
            for j in range(4):
                k.tr(ptb[:, j * 128:(j + 1) * 128], XCb[:, j, co:co + 128], c.ident_b)
            xT = ptb[:, 0:256].rearrange("p (a b) -> p a b", a=4)
            k.tt("dve", XDT, xT, D4[:, 1, :].unsqueeze(2).to_broadcast([128, 4, 64]), ALU.mult)
            k.tt("dve", XDC, xT, D4[:, 5, :].unsqueeze(2).to_broadcast([128, 4, 64]), ALU.mult)
            k.cp("act", BTt.rearrange("p a b -> p (a b)"), ptb[:, 256:512])
            psc = c.bank()
            for g in range(2):
                k.mm(psc[:, g * 128:(g + 1) * 128], XCb[:, 2 + g, co:co + 128], XCb[:, 4 + g, co:co + 128])
            sc4 = psc[:, 0:256].rearrange("p (a b) -> p a b", a=2).unsqueeze(2).to_broadcast([128, 2, 2, 128])
            k.tt("dve", MSK.rearrange("p (a b) c -> p a b c", a=2), sc4, LT.rearrange("p (a b) c -> p a b c", a=2), ALU.mult)
            c4 = XCb[:, 4:6, co:co + 128].unsqueeze(2).to_broadcast([128, 2, 2, 128])
            k.tt("pool", CDC.rearrange("p (a b) c -> p a b c", a=2), c4, EC.rearrange("p (a b) c -> p a b c", a=2), ALU.mult)
            py = c.bank()
            for hh in range(4):
                pc, po = hh // 2, 64 * (hh % 2)
                o = py[po:po + 64, pc * 128:(pc + 1) * 128]
                k.mm(o, XDT[:, hh, :], MSK[:, hh, :], start=True, stop=False)
                k.mm(o, Sb[:, hh, :], CDC[:, hh, :], start=False, stop=True)
            k.cp("act", YF[:, :, co:co + 128], py[:, 0:256].rearrange("p (a b) -> p a b", a=2))
            pst = c.bank()
            for hh in range(4):
                k.mm(pst[:, hh * 64:(hh + 1) * 64], BTt[:, hh // 2, :], XDC[:, hh, :])
            k.tt("dve", TS_, S, D4[:, 6, :].unsqueeze(2).to_broadcast([128, 4, 64]), ALU.mult)
            k.tt("dve", S, TS_, pst[:, 0:256].rearrange("p (a b) -> p a b", a=4), ALU.add)
            k.cp("act", Sb, S)
        for ci2 in range(2):
            y = YF[:, ci2, 0:n]
            k.stt("dve", y, XCb[:, ci2, 0:n], c.par("ssd_d", ci2), y, ALU.mult, ALU.add)
            k.act(ZS[:, 0:n], ZB[:, ci2, 0:n], AF.Silu)
            k.tt("dve", y, y, ZS[:, 0:n], ALU.mult)
            k.tt("pool", SQ[:, 0:n], y, y, ALU.mult)
            pb = c.bank()
            k.mm(pb[:, 0:n], c.ones_b, SQ[:, 0:n])
            k.act(RS[:, 0:n], pb[:, 0:n], AF.Ln, bias=c.EPS_AP, scale=1.0 / 128)
            k.act(RS[:, 0:n], RS[:, 0:n], AF.Exp, scale=-0.5)
            k.stt("dve", c.yT[:, ci2, t0:t1], y, c.par("ssd_nw", ci2), RS[:, 0:n], ALU.mult, ALU.mult)


def rwkv_pass(c, l, Wm, hnT, first):
    k, ar = c.k, c.ar
    if os.environ.get("K_RWSPLIT", "1") == "0":
        c.bank_mode[0] = "all"
    load_win(c, l, Wm, [(1028, 2052)])
    NC2 = MT // 128
    P8 = ar.f32(8, 1 + MT)
    HIST = ar.f32(8, 1)
    TMPS = ar.f32(2, MT)
    TW = ar.bf16(MT)
    SG = ar.bf16(MT)
    Vb = ar.bf16(2, MT)
    AV = ar.f32(1, MT)
    GG = ar.bf16(2, MT)
    KKn = ar.f32(1, MT)
    KMD = ar.f32(1, MT)
    BON = ar.f32(2, MT)
    SGW = ar.f32(MT)
    CUM = ar.f32(MT)
    PE_ = ar.f32(MT)
    TT = ar.f32(MT)
    PM = ar.f32(2, MT)
    SQb = c.sqbuf[1]
    AR = ar.bf16(2, NC2, 2, 128)
    Bt = ar.bf16(2, MT)
    Kt = ar.bf16(2, MT)
    TL_ = [ar.bf16(6, 128), ar.bf16(6, 128)]
    Sx_ = [[ar.bf16(4, 128), ar.bf16(4, 128)] for _ in range(2)]
    STx_ = [[ar.bf16(4, 128), ar.bf16(4, 128)] for _ in range(2)]
    PTx_ = [[ar.bf16(4, 128), ar.bf16(4, 128)] for _ in range(2)]
    MRB_ = [ar.bf16(4, 128), ar.bf16(4, 128)]
    AAK_ = [ar.bf16(4, 128), ar.bf16(4, 128)]
    MRK_ = [ar.bf16(4, 128), ar.bf16(4, 128)]
    XZ_ = [ar.bf16(256), ar.bf16(256)]
    U_ = [ar.bf16(256), ar.bf16(256)]
    H = ar.f32(2, 64)
    H0p = ar.f32(2, 64)
    Hb = ar.bf16(2, 64)
    YF = ar.f32(2, MT)
    k.ms("dve", H, 0.0)
    k.ms("dve", Hb, 0.0)
    msl, msu, miu = c.cst("msl"), c.cst("msu"), c.cst("miu")
    identb4 = c.ident_b.unsqueeze(1).to_broadcast([128, 4, 128])
    SMw = c.SM
    for ti, (t0, t1) in enumerate(c.mtiles):
        n = t1 - t0
        nch = n // 128
        hn_tile(c, l, t0, n, hnT, first)
        if ti == 0:
            k.ms("pool", P8[:, :, 0:1], 0.0)
        else:
            k.cp("pool", P8[:, :, 0:1], HIST)
        for ci in range(8):
            proj_F(c, Wm, hnT, n, ci * 128, P8[:, ci, 1:1 + n], "act" if ci % 2 else "dve")
        k.cp("pool", HIST, P8[:, :, n:n + 1])
        MX = P8[:, :, 1:1 + MT]
        for g4 in range(4):
            sl = slice(g4 * 2, g4 * 2 + 2)
            k.tt("dve", TMPS[:, :, 0:n], P8[:, sl, 0:n], P8[:, sl, 1:1 + n], ALU.subtract)
            k.tt("pool", TMPS[:, :, 0:n], TMPS[:, :, 0:n], c.par("rw_mu", g4 * 2, 2).unsqueeze(2).to_broadcast([128, 2, n]), ALU.mult)
            k.tt("dve", P8[:, sl, 1:1 + n], TMPS[:, :, 0:n], P8[:, sl, 1:1 + n], ALU.add)
        k.act(TW[0:64, 0:n], MX[0:64, 6, 0:n], AF.Tanh)
        k.cp("act", TW[64:128, 0:n], MX[64:128, 6, 0:n])
        k.act(SG[:, 0:n], MX[:, 7, 0:n], AF.Sigmoid)
        for ci in range(2):
            r_, k_, v_ = MX[:, ci, 0:n], MX[:, 2 + ci, 0:n], MX[:, 4 + ci, 0:n]
            k.cp("pool", Vb[:, ci, 0:n], v_)
            pw = c.bank()
            k.mm(pw[:, 0:n], SMw[0:64, ci * 128:(ci + 1) * 128], TW[0:64, 0:n])
            pa = c.bank()
            k.mm(pa[:, 0:n], SMw[64:128, ci * 128:(ci + 1) * 128], TW[64:128, 0:n])
            pg = c.bank()
            k.mm(pg[:, 0:n], SMw[:, 256 + ci * 128:256 + (ci + 1) * 128], SG[:, 0:n])
            k.act(SGW[:, 0:n], pw[:, 0:n], AF.Sigmoid, bias=c.par("rw_w0", ci))
            k.act(AV[:, 0, 0:n], pa[:, 0:n], AF.Sigmoid, bias=c.par("rw_a0", ci))
            k.cp("act", GG[:, ci, 0:n], pg[:, 0:n])
            kkn = KKn[:, 0, 0:n]
            k.ts("dve", kkn, k_, c.par("rw_kk", ci))
            k.tt("pool", SQb[:, 0:n], kkn, kkn, ALU.mult)
            pss = c.bank()
            k.mm(pss[:, 0:n], c.blk64_b, SQb[:, 0:n])
            k.ts("dve", TT[:, 0:n], pss[:, 0:n], 1e-24, None, ALU.max)
            k.act(TT[:, 0:n], TT[:, 0:n], AF.Ln)
            k.act(TT[:, 0:n], TT[:, 0:n], AF.Exp, scale=-0.5)
            k.tt("dve", kkn, kkn, TT[:, 0:n], ALU.mult)
            kmd = KMD[:, 0, 0:n]
            k.ts("dve", TT[:, 0:n], AV[:, 0, 0:n], c.par("rw_ka", ci), c.par("d_omka", ci), ALU.mult, ALU.add)
            k.tt("dve", kmd, k_, TT[:, 0:n], ALU.mult)
            k.stt("dve", SQb[:, 0:n], r_, c.par("rw_rk", ci), kmd, ALU.mult, ALU.mult)
            pbn = c.bank()
            k.mm(pbn[:, 0:n], c.blk64_b, SQb[:, 0:n])
            k.tt("dve", BON[:, ci, 0:n], pbn[:, 0:n], v_, ALU.mult)
            k.scan("dve", CUM[:, 0:n], c.cst("reset", 0, n), SGW[:, 0:n], 0.0)
            k.act(PE_[:, 0:n], CUM[:, 0:n], AF.Exp, scale=C_W)
            k.act(PM[:, ci, 0:n], CUM[:, 0:n], AF.Exp, scale=-C_W)
            k.tt("pool", TT[:, 0:n], CUM[:, 0:n], SGW[:, 0:n], ALU.subtract)
            k.act(TT[:, 0:n], TT[:, 0:n], AF.Exp, scale=-C_W)
            v3 = lambda a: a.rearrange("p (a b) -> p a b", a=nch)
            k.stt("dve", AR[:, ci, 0:nch, 0, :], v3(kkn), -1.0, v3(TT[:, 0:n]), ALU.mult, ALU.mult)
            k.tt("dve", AR[:, ci, 0:nch, 1, :], v3(r_), v3(PM[:, ci, 0:n]), ALU.mult)
            k.tt("pool", TT[:, 0:n], kkn, AV[:, 0, 0:n], ALU.mult)
            k.tt("dve", Bt[:, ci, 0:n], TT[:, 0:n], PE_[:, 0:n], ALU.mult)
            k.tt("dve", Kt[:, ci, 0:n], kmd, PE_[:, 0:n], ALU.mult)
        for ci in range(nch):
            co = ci * 128
            par2 = ci % 2
            TL, Sx, STx, PTx = TL_[par2], Sx_[par2], STx_[par2], PTx_[par2]
            MRB, AAK, MRK, XZ, U = MRB_[par2], AAK_[par2], MRK_[par2], XZ_[par2], U_[par2]
            ptr = c.bank()
            ptb = ptr[:, 0:384].bitcast(BF16)
            for j, src in enumerate((Vb, Bt, Kt)):
                for cc in range(2):
                    k.tr(ptb[:, (2 * j + cc) * 128:(2 * j + cc + 1) * 128], src[:, cc, co:co + 128], c.ident_b)
            k.cp("act", TL.rearrange("p a b -> p (a b)"), ptb)
            Vh = lambda hh: TL[:, hh // 2, 64 * (hh % 2):64 * (hh % 2) + 64]
            Bh = lambda hh: TL[:, 2 + hh // 2, 64 * (hh % 2):64 * (hh % 2) + 64]
            Kh = lambda hh: TL[:, 4 + hh // 2, 64 * (hh % 2):64 * (hh % 2) + 64]
            pA = c.bank(2)
            pB = c.bank(2)
            pC = c.bank(2)
            for hh in range(4):
                pc, po = hh // 2, 64 * (hh % 2)
                q, j = hh % 2, hh // 2
                At = AR[po:po + 64, pc, ci, 0, :]
                ARf = AR[po:po + 64, pc, ci, :, :].rearrange("p a b -> p (a b)")
                Bth = Bt[po:po + 64, pc, co:co + 128]
                Kth = Kt[po:po + 64, pc, co:co + 128]
                k.mm(pA[:, q * 512 + j * 128:q * 512 + (j + 1) * 128], At, Bth)
                k.mm(pB[:, q * 512 + j * 256:q * 512 + (j + 1) * 256], Bth, ARf)
                k.mm(pC[:, q * 512 + j * 256:q * 512 + (j + 1) * 256], Kth, ARf)
            S, ST, PT = Sx[0], STx[0], PTx[0]
            pA4 = pA.rearrange("p (q j b) -> p q j b", q=2, j=4)[:, :, 0:2, :]
            pB4 = pB.rearrange("p (a b c) -> p a b c", a=4, b=2)
            pC4 = pC.rearrange("p (a b c) -> p a b c", a=4, b=2)
            m3 = lambda m: m.unsqueeze(1).to_broadcast([128, 4, 128])
            m22 = lambda m: m.unsqueeze(1).unsqueeze(1).to_broadcast([128, 2, 2, 128])
            k.tt("dve", S.rearrange("p (q j) b -> p q j b", q=2), pA4, m22(msl), ALU.mult)
            k.tt("dve", ST, pB4[:, :, 0, :], m3(msu), ALU.mult)
            k.tt("dve", MRB, pB4[:, :, 1, :], m3(miu), ALU.mult)
            k.tt("dve", AAK, pC4[:, :, 0, :], m3(msu), ALU.mult)
            k.tt("dve", MRK, pC4[:, :, 1, :], m3(miu), ALU.mult)
            k.tt("pool", PT, ST, identb4, ALU.add)
            hp = lambda hh: (hh % 2) * 2 + hh // 2
            cur = 0
            for lev in range(6):
                nxt = 1 - cur
                S, ST, PT = Sx[cur], STx[cur], PTx[cur]
                Sn, STn, PTn = Sx[nxt], STx[nxt], PTx[nxt]
                pS = c.bank()
                for hh in range(4):
                    k.mm(pS[:, hp(hh) * 128:(hp(hh) + 1) * 128], ST[:, hp(hh), :], S[:, hp(hh), :])
                k.cp("act", Sn.rearrange("p a b -> p (a b)"), pS[:, 0:512])
                if lev < 5:
                    pT = c.bank()
                    for hh in range(4):
                        k.mm(pT[:, hp(hh) * 128:(hp(hh) + 1) * 128], S[:, hp(hh), :], ST[:, hp(hh), :])
                    k.cp("dve", STn.rearrange("p a b -> p (a b)"), pT[:, 0:512])
                pP = c.bank()
                for hh in range(4):
                    k.mm(pP[:, hp(hh) * 128:(hp(hh) + 1) * 128], Sn[:, hp(hh), :], PT[:, hp(hh), :])
                k.tt("dve", PTn.rearrange("p a b -> p (a b)"), pP[:, 0:512], PT.rearrange("p a b -> p (a b)"), ALU.add)
                cur = nxt
            PT = PTx[cur]
            pX = c.bank()
            for hh in range(4):
                pc, po = hh // 2, 64 * (hh % 2)
                o = pX[:, hh * 64:(hh + 1) * 64]
                k.mm(o, AAK[:, hp(hh), :], Vh(hh), start=True, stop=False)
                k.mm(o, AR[po:po + 64, pc, ci, 0, :], Hb[po:po + 64, pc, :], start=False, stop=True)
            k.cp("act", XZ, pX[:, 0:256])
            pU = c.bank()
            for hh in range(4):
                k.mm(pU[:, hh * 64:(hh + 1) * 64], PT[:, hp(hh), :], XZ[:, hh * 64:(hh + 1) * 64])
            k.cp("act", U, pU[:, 0:256])
            pY = c.bank()
            for hh in range(4):
                pc, po = hh // 2, 64 * (hh % 2)
                o = pY[po:po + 64, pc * 128:(pc + 1) * 128]
                k.mm(o, Hb[po:po + 64, pc, :], AR[po:po + 64, pc, ci, 1, :], start=True, stop=False)
                k.mm(o, U[:, hh * 64:(hh + 1) * 64], MRB[:, hp(hh), :], start=False, stop=False)
                k.mm(o, Vh(hh), MRK[:, hp(hh), :], start=False, stop=True)
            k.cp("act", YF[:, :, co:co + 128], pY[:, 0:256].rearrange("p (a b) -> p a b", a=2))
            pH = c.bank()
            for hh in range(4):
                pc, po = hh // 2, 64 * (hh % 2)
                o = pH[po:po + 64, pc * 64:(pc + 1) * 64]
                k.mm(o, Bh(hh), U[:, hh * 64:(hh + 1) * 64], start=True, stop=False)
                k.mm(o, Kh(hh), Vh(hh), start=False, stop=True)
            for pc in range(2):
                pl = PM[:, pc, co + 127:co + 128]
                k.ts("pool", H0p[:, pc, :], H[:, pc, :], pl)
                k.stt("dve", H[:, pc, :], pH[:, pc * 64:(pc + 1) * 64], pl, H0p[:, pc, :], ALU.mult, ALU.add)
            k.cp("act", Hb, H)
        for ci2 in range(2):
            y = YF[:, ci2, 0:n]
            head_norm_F(c, y, n, c.EPS_RW, TT[:, 0:n], CUM[:, 0:n])
            k.ts("dve", y, y, c.par("rw_lnw", ci2), c.par("rw_lnb", ci2), ALU.mult, ALU.add)
            k.tt("dve", y, y, BON[:, ci2, 0:n], ALU.add)
            k.tt("dve", c.yT[:, 2 + ci2, t0:t1], y, GG[:, ci2, 0:n], ALU.mult)


def post_norm_residual(c, O, tiles_local, t0g, wname, SQ2, RS, after_tile=None):
    k = c.k
    for (a, b) in tiles_local:
        n = b - a
        pb = c.bank()
        for m in range(8):
            sq = SQ2[m % 2]
            k.tt("pool" if m % 2 else "dve", sq[:, 0:n], O[:, m, a:b], O[:, m, a:b], ALU.mult)
            k.mm(pb[:, 0:n], c.ones_b, sq[:, 0:n], start=(m == 0), stop=(m == 7))
        k.act(RS[:, 0:n], pb[:, 0:n], AF.Ln, bias=c.EPS_AP, scale=1.0 / 1024)
        k.act(RS[:, 0:n], RS[:, 0:n], AF.Exp, scale=-0.5)
        lo = 0
        if t0g + a < PAD:
            lo = PAD - (t0g + a)
        for m in range(8):
            eng = "pool" if m % 2 else "dve"
            k.stt(eng, O[:, m, a + lo:b], O[:, m, a + lo:b], c.par(wname, m), RS[:, lo:n], ALU.mult, ALU.mult)
            k.tt(eng, c.h[:, m, t0g + a + lo:t0g + b], c.h[:, m, t0g + a + lo:t0g + b], O[:, m, a + lo:b], ALU.add)
        if after_tile is not None:
            after_tile(t0g + a, t0g + b)


def wout_phase(c, l):
    k, ar = c.k, c.ar
    c.bank_mode[0] = "all"
    ar.reset()
    Wo = ar.bf16(8, D)
    O_ = [ar.f32(8, 512), ar.f32(8, 512)]
    SQ2 = [ar.bf16(512), ar.bf16(512)]
    RS_ = [ar.f32(512), ar.f32(512)]
    for kc in range(8):
        k.dma_in("pool", Wo[:, kc, :], c.wout_d[l, :, kc, :])
    for ti, (t0, t1) in enumerate(tiles_of(0, T, 512)):
        n = t1 - t0
        O, RS = O_[ti % 2], RS_[ti % 2]
        for m in range(8):
            pb = c.bank()
            for kc in range(8):
                k.mm(pb[:, 0:n], Wo[:, kc, m * 128:(m + 1) * 128], c.yT[:, kc, t0:t1], start=(kc == 0), stop=(kc == 7))
            k.cp("act", O[:, m, 0:n], pb[:, 0:n])
        post_norm_residual(c, O, [(0, n)], t0, "post_mix", SQ2, RS,
                           after_tile=lambda ta, tb: c.rms_stats(lambda kc: c.h[:, kc, ta:tb], tb - ta, c.rstd_all[:, ta:tb], sqbuf=SQ2))


def ffn_phase(c, l):
    k, ar = c.k, c.ar
    ar.reset()
    GT = 768
    hn2x = [ar.bf16(8, GT), ar.bf16(8, GT)]
    Fo = ar.f32(8, GT)
    Wgu = [ar.bf16(2, 8, 128), ar.bf16(2, 8, 128), ar.bf16(2, 8, 128)]
    Wd = [ar.bf16(NJ, 128), ar.bf16(NJ, 128), ar.bf16(NJ, 128)]
    SQ2 = [ar.bf16(512), ar.bf16(512)]
    RS = ar.f32(512)
    SIL = [ar.f32(512), ar.f32(512)]
    aT = c.yT[:, :, :].rearrange("p a b -> p (a b)")[:, 0:NJ * GT].rearrange("p (a b) -> p a b", a=NJ)
    groups = tiles_of(0, T, GT)

    def make_hn2(gi):
        g0, g1 = groups[gi]
        hn2 = hn2x[gi % 2]
        for (a, b) in tiles_of(0, g1 - g0, 512):
            for kc in range(8):
                k.stt("dve", hn2[:, kc, a:b], c.h[:, kc, g0 + a:g0 + b], c.par("pre_ffn", kc), c.rstd_all[:, g0 + a:g0 + b], ALU.mult, ALU.mult)

    make_hn2(0)
    for gi, (g0, g1) in enumerate(groups):
        ng = g1 - g0
        lt = tiles_of(0, ng, 512)
        hn2 = hn2x[gi % 2]
        for j in range(NJ):
            W = Wgu[j % 3]
            for gu in range(2):
                k.dma_in("pool", W[:, gu, :, :].rearrange("p b c -> p (b c)"), c.wgu_d[l, j, :, gu, :, :].rearrange("p b c -> p (b c)"))
            for (a, b) in lt:
                n = b - a
                pg = c.bank()
                pu = c.bank()
                for kc in range(8):
                    k.mm(pg[:, 0:n], W[:, 0, kc, :], hn2[:, kc, a:b], start=(kc == 0), stop=(kc == 7))
                for kc in range(8):
                    k.mm(pu[:, 0:n], W[:, 1, kc, :], hn2[:, kc, a:b], start=(kc == 0), stop=(kc == 7))
                sl = SIL[(j + (a > 0)) % 2]
                k.act(sl[:, 0:n], pg[:, 0:n], AF.Silu)
                k.tt("dve", aT[:, j, a:b], sl[:, 0:n], pu[:, 0:n], ALU.mult)
        if gi + 1 < len(groups):
            make_hn2(gi + 1)
        for m in range(8):
            W = Wd[m % 3]
            k.dma_in("pool", W.rearrange("p a b -> p (a b)"), c.wd_d[l, m, :, :, :].rearrange("p a b -> p (a b)"))
            for (a, b) in lt:
                n = b - a
                pb = c.bank()
                for j in range(NJ):
                    k.mm(pb[:, 0:n], W[:, j, :], aT[:, j, a:b], start=(j == 0), stop=(j == NJ - 1))
                k.cp("act", Fo[:, m, a:b], pb[:, 0:n])
        nxt = None
        if l + 1 < c.L and not os.environ.get("K_NOPRE"):
            nxt = lambda ta, tb: c.rms_stats(lambda kc: c.h[:, kc, ta:tb], tb - ta, c.rstd_all[:, ta:tb], sqbuf=SQ2)
        post_norm_residual(c, Fo, lt, g0, "post_ffn", SQ2, RS, after_tile=nxt)


def _cols(v):
    v = np.asarray(v, np.float32)
    return np.ascontiguousarray(v.reshape(-1, 128).T)


def make_consts():
    Cn = np.zeros((128, NCONST), np.float32)
    i = np.arange(128)

    def put(name, a):
        o, w = CO[name]
        Cn[:, o:o + w] = a
    put("ident", np.eye(128))
    put("ones", np.ones((128, 128)))
    put("blk64", np.kron(np.eye(2), np.ones((64, 64))))
    put("triu", (i[:, None] <= i[None, :]).astype(np.float32))
    put("maskneg", np.where(i[None, :] >= i[:, None], 0.0, -30000.0))
    put("msl", (i[:, None] > i[None, :]).astype(np.float32))
    put("msu", (i[:, None] < i[None, :]).astype(np.float32))
    put("miu", (i[:, None] <= i[None, :]).astype(np.float32))
    log_g = np.log1p(-np.exp2(-5.0 - np.arange(4, dtype=np.float64)))
    lt = np.zeros((128, 4, 128))
    for hh in range(4):
        rel = i[None, :] - i[:, None]
        lt[:, hh, :] = np.where(rel >= 0, np.exp(np.maximum(rel, 0) * log_g[hh]), 0.0)
    put("ltT", lt.reshape(128, 512))
    kd = np.zeros((128, 128))
    qd = np.zeros((128, 128))
    g128 = np.zeros((128, 1))
    hm = np.zeros((128, 4))
    for hh in range(4):
        kd[:, hh * 32:(hh + 1) * 32] = np.exp((127 - i) * log_g[hh])[:, None]
        qd[hh * 32:(hh + 1) * 32, :] = np.exp((i + 1) * log_g[hh])[None, :]
        g128[hh * 32:(hh + 1) * 32, 0] = np.exp(128 * log_g[hh])
        hm[hh * 32:(hh + 1) * 32, hh] = 1.0
    put("kdec", kd)
    put("qdec", qd)
    put("g128", g128)
    put("hm", hm)
    rs = np.ones((128, 256))
    rs[:, 0] = 0.0
    rs[:, 128] = 0.0
    put("reset", rs)
    half = 16
    freqs = 10000.0 ** (-np.arange(half, dtype=np.float64) / half)
    pos = np.arange(T, dtype=np.float64) - PAD
    ang = pos[None, :] * freqs[:, None]
    cos = np.concatenate([np.cos(ang), np.cos(ang)], 0)
    sin = np.concatenate([-np.sin(ang), np.sin(ang)], 0)
    cos = np.tile(cos, (4, 1))
    sin = np.tile(sin, (4, 1))
    sc = 32 ** -0.5
    rope = np.stack([cos, sin, cos * sc, sin * sc]).astype(np.float32)
    rope[:, :, :PAD] = 0.0
    return Cn, rope


def prep_weights(inp, L):
    g = lambda n: np.asarray(inp[n], np.float32)
    w_in = g("w_in")[:L]
    ret0 = 2564
    idx = np.arange(128)
    sw = (idx // 32) * 32 + ((idx % 32) + 16) % 32
    qsw = w_in[:, :, ret0 + sw]
    ksw = w_in[:, :, ret0 + 128 + sw]
    w_in_p = np.concatenate([w_in, qsw, ksw], axis=2)
    w_in_p = np.ascontiguousarray(w_in_p.reshape(L, 8, 128, WIN_COLS))
    w_out = np.ascontiguousarray(g("w_out")[:L].reshape(L, 8, 128, D).transpose(0, 2, 1, 3))
    wg = g("ffn_w_gate")[:L].reshape(L, 8, 128, NJ, 128)
    wu = g("ffn_w_up")[:L].reshape(L, 8, 128, NJ, 128)
    wgu = np.stack([wg, wu], axis=0)
    wgu = np.ascontiguousarray(wgu.transpose(1, 4, 3, 0, 2, 5))
    wd = g("ffn_w_down")[:L].reshape(L, NJ, 128, 8, 128)
    wd = np.ascontiguousarray(wd.transpose(0, 3, 2, 1, 4))
    params = np.zeros((L, 128, NPAR), np.float32)
    smat = np.zeros((L, 128, 1024), np.float32)
    for l in range(L):
        def put(name, a):
            o, w = PO[name]
            params[l, :, o:o + w] = a
        put("pre_mix", _cols(g("pre_mix_norm")[l]))
        put("post_mix", _cols(g("post_mix_norm")[l]))
        put("pre_ffn", _cols(g("pre_ffn_norm")[l]))
        put("post_ffn", _cols(g("post_ffn_norm")[l]))
        cw = g("ssd_conv_w")[l]
        put("ssd_cw", np.concatenate([np.stack([cw[j, ci * 128:(ci + 1) * 128] for j in range(4)], 1) for ci in range(6)], 1))
        put("ssd_cb", _cols(g("ssd_conv_b")[l]))
        put("ssd_dtb", np.tile(g("ssd_dt_bias")[l][None, :], (128, 1)))
        put("ssd_alog", np.tile(g("ssd_a_log")[l][None, :], (128, 1)))
        put("ssd_d", _cols(np.repeat(g("ssd_d")[l], 64)))
        put("ssd_nw", _cols(g("ssd_norm_w")[l]))
        put("rw_mu", _cols(g("rwkv_mu")[l]))
        put("rw_w0", _cols(g("rwkv_w0")[l]))
        put("rw_a0", _cols(g("rwkv_a0")[l]))
        put("rw_kk", _cols(g("rwkv_k_k")[l]))
        put("rw_ka", _cols(g("rwkv_k_a")[l]))
        put("rw_rk", _cols(g("rwkv_r_k")[l].reshape(-1)))
        put("rw_lnw", _cols(g("rwkv_ln_w")[l]))
        put("rw_lnb", _cols(g("rwkv_ln_b")[l]))
        lw = g("lru_conv_w")[l]
        put("lru_cw", np.concatenate([np.stack([lw[j, ci * 128:(ci + 1) * 128] for j in range(4)], 1) for ci in range(2)], 1))
        put("lru_cb", _cols(g("lru_conv_b")[l]))
        put("lru_ba", _cols(g("lru_ba")[l]))
        put("lru_bx", _cols(g("lru_bx")[l]))
        put("lru_lam", _cols(g("lru_lambda")[l]))
        put("ret_gnw", _cols(g("ret_gn_w")[l]))
        smat[l, 0:64, 0:256] = g("rwkv_w2")[l]
        smat[l, 64:128, 0:256] = g("rwkv_a2")[l]
        smat[l, :, 256:512] = g("rwkv_g2")[l]
        for nm, off in (("lru_wa", 512), ("lru_wx", 768)):
            w = g(nm)[l]
            for b in range(4):
                ci, po = b // 2, 64 * (b % 2)
                smat[l, po:po + 64, off + ci * 128 + po:off + ci * 128 + po + 64] = w[b]
    return dict(w_in=w_in_p, w_out=w_out, w_gu=wgu, w_d=wd, params=params, smat=smat)


_CACHE = {}


def kernel(**inputs):
    x = np.asarray(inputs["x"], np.float32)
    B = x.shape[0]
    Cn, rope = make_consts()
    wts = prep_weights(inputs, DEPTH)
    if "nc" not in _CACHE:
        st = os.environ.get("K_STAGES")
        _CACHE["nc"] = build(DEPTH) if st is None else build(DEPTH, stages=tuple(x for x in st.split(",") if x))
    nc = _CACHE["nc"]
    meta = np.ascontiguousarray(np.asarray(inputs["meta_tokens"], np.float32))
    in_maps = []
    for b in range(B):
        m = dict(x=np.ascontiguousarray(x[b]), meta=meta, consts=Cn, rope=rope)
        m.update(wts)
        in_maps.append({k_: v_ for k_, v_ in m.items() if k_ in nc._in_names})
    res = run_bass_kernel_spmd(nc, in_maps, core_ids=list(range(B)))
    return np.stack([np.asarray(r["out"], np.float32) for r in res.results], axis=0)
```
